# Optimizing a Trainium2 kernel written in Bass

```python
import jax
import jax.numpy as jnp
from jax import lax

D_MODEL = 1024
BATCH = 4
SEQ = 4096
DEPTH = 4
DEC_BATCH = 32
DEC_SEQ = 1
PAST_LEN = 8192
PAGE_SIZE = 128

N_MIXERS = 4
DN_ALPHA = (2.0 * DEPTH) ** 0.25
DN_BETA = (8.0 * DEPTH) ** -0.25
LN_EPS = 1e-5
CONV_W = 3
Q_BLOCK = 128

SC_WIDTH = D_MODEL

SB_HEADS = 16
SB_HEAD_DIM = D_MODEL // SB_HEADS
SB_BIAS_LO = -9.0
SB_BIAS_HI = -3.0

GLA_HEADS = 4
GLA_DK = D_MODEL // 2 // GLA_HEADS
GLA_DV = D_MODEL // GLA_HEADS
GLA_RANK = 16
GLA_TAU = 16.0
GLA_CHUNK = 64
GLA_QK = GLA_HEADS * GLA_DK
GLA_V = GLA_HEADS * GLA_DV
GLA_SPLITS = (GLA_QK, 2 * GLA_QK, 2 * GLA_QK + GLA_V, 2 * GLA_QK + 2 * GLA_V)
GLA_IN = 2 * GLA_QK + 2 * GLA_V + GLA_RANK

DSW_GROUPS = ((128, 1), (512, 4), (2048, 16))
DSW_N_GROUPS = len(DSW_GROUPS)
DSW_HEADS = 8
DSW_HEAD_DIM = 64
DSW_INNER = DSW_HEADS * DSW_HEAD_DIM

D_FF = 2816

kernel_name = 'hybrid_shortconv_stickbreak_gla_dilated_decode_step'


def layer_norm(x, g, b):
    xf = x.astype(jnp.float32)
    mu = jnp.mean(xf, axis=-1, keepdims=True)
    var = jnp.mean(jnp.square(xf - mu), axis=-1, keepdims=True)
    return ((xf - mu) * lax.rsqrt(var + LN_EPS) * g + b).astype(x.dtype)


def causal_dwconv(z, prev, w):
    n = z.shape[1]
    zp = jnp.concatenate([prev.astype(z.dtype), z], axis=1)
    y = w[0] * zp[:, :n] + w[1] * zp[:, 1:n + 1] + w[2] * zp[:, 2:n + 2]
    return y, zp[:, n:]


def short_conv_mixer(x, prev, w_in, w_conv, w_out):
    b_gate, c_gate, v = jnp.split(x @ w_in, 3, axis=-1)
    z, new_prev = causal_dwconv(c_gate * v, prev, w_conv)
    return (b_gate * z) @ w_out, new_prev


def sb_qkv(x, w_qkv):
    bsz, n = x.shape[:2]
    qkv = (x @ w_qkv).reshape(bsz, n, 3, SB_HEADS, SB_HEAD_DIM)
    return qkv[:, :, 0], qkv[:, :, 1], qkv[:, :, 2]


def stick_breaking_attend(q, k, v, q_pos, k_pos, bias):
    z = jnp.einsum('bqhd,bkhd->bhqk', q, k).astype(jnp.float32) * (SB_HEAD_DIM ** -0.5)
    z = z + bias.astype(jnp.float32)[:, None, None]
    causal = k_pos[None, :] < q_pos[:, None]
    log_rest = jnp.where(causal, jax.nn.log_sigmoid(-z), 0.0)
    between = lax.cumsum(log_rest, axis=3, reverse=True) - log_rest
    weight = jnp.where(causal, jnp.exp(jax.nn.log_sigmoid(z) + between), 0.0)
    return jnp.einsum('bhqk,bkhd->bqhd', weight.astype(v.dtype), v)


def sb_mixer_prompt(x, w_qkv, w_out, bias):
    bsz, n = x.shape[:2]
    q, k, v = sb_qkv(x, w_qkv)
    k_pos = jnp.arange(n)

    def block(i):
        start = i * Q_BLOCK
        q_blk = lax.dynamic_slice_in_dim(q, start, Q_BLOCK, axis=1)
        return stick_breaking_attend(q_blk, k, v, start + jnp.arange(Q_BLOCK), k_pos, bias)

    o = lax.map(block, jnp.arange(n // Q_BLOCK))
    o = jnp.moveaxis(o, 0, 1).reshape(bsz, n, SB_HEADS * SB_HEAD_DIM)
    return o @ w_out, k, v


def sb_mixer_sample(x, cache_k, cache_v, page_table, w_qkv, w_out, bias):
    bsz, n = x.shape[:2]
    q, k, v = sb_qkv(x, w_qkv)
    past = page_table.shape[1] * PAGE_SIZE
    k_past = cache_k[page_table].reshape(bsz, past, SB_HEADS, SB_HEAD_DIM).astype(k.dtype)
    v_past = cache_v[page_table].reshape(bsz, past, SB_HEADS, SB_HEAD_DIM).astype(v.dtype)
    k_all = jnp.concatenate([k_past, k], axis=1)
    v_all = jnp.concatenate([v_past, v], axis=1)
    o = stick_breaking_attend(q, k_all, v_all, past + jnp.arange(n), jnp.arange(past + n), bias)
    return o.reshape(bsz, n, SB_HEADS * SB_HEAD_DIM) @ w_out, k, v


def gla_chunk(s0, q, k, v, log_a):
    n = q.shape[1]
    b = jnp.cumsum(log_a, axis=1)
    causal = jnp.tril(jnp.ones((n, n), dtype=bool))[None, :, :, None, None]
    rel = jnp.where(causal, b[:, :, None] - b[:, None, :], -jnp.inf)
    scores = jnp.einsum('bthd,bshd,btshd->bhts', q, k, jnp.exp(rel))
    o = jnp.einsum('bhts,bshv->bthv', scores, v) + jnp.einsum('bthd,bhdv->bthv', q * jnp.exp(b), s0)
    b_last = b[:, -1]
    s_new = jnp.exp(b_last)[..., None] * s0 + jnp.einsum('bshd,bshv->bhdv', k * jnp.exp(b_last[:, None] - b), v)
    return o, s_new


def gla_mixer(x, s0, w_in, w_gate_up, b_gate, g_norm, w_out):
    bsz, n = x.shape[:2]
    f32 = jnp.float32
    q, k, v, r, g_low = jnp.split(x @ w_in, GLA_SPLITS, axis=-1)
    log_a = jax.nn.log_sigmoid((g_low @ w_gate_up + b_gate).astype(f32)) / GLA_TAU

    def heads(t, d):
        return t.reshape(bsz, n, GLA_HEADS, d).astype(f32)

    q = heads(q, GLA_DK) * (GLA_DK ** -0.5)
    k = heads(k, GLA_DK)
    v = heads(v, GLA_DV)
    log_a = heads(log_a, GLA_DK)
    chunk = GLA_CHUNK if n % GLA_CHUNK == 0 else n
    n_chunks = n // chunk

    def to_chunks(t):
        return jnp.moveaxis(t.reshape(bsz, n_chunks, chunk, GLA_HEADS, t.shape[-1]), 1, 0)

    def step(s, xs):
        o_c, s_next = gla_chunk(s, *xs)
        return s_next, o_c

    s_final, o = lax.scan(step, s0.astype(f32), (to_chunks(q), to_chunks(k), to_chunks(v), to_chunks(log_a)))
    o = jnp.moveaxis(o, 0, 1).reshape(bsz, n, GLA_HEADS, GLA_DV)
    o = o * lax.rsqrt(jnp.mean(o * o, axis=-1, keepdims=True) + LN_EPS) * g_norm
    o = o.reshape(bsz, n, GLA_V).astype(x.dtype) * jax.nn.silu(r)
    return o @ w_out, s_final.astype(s0.dtype)


def alibi_slopes():
    n_heads = DSW_N_GROUPS * DSW_HEADS
    m = 2.0 ** (-8.0 * jnp.arange(1, n_heads + 1, dtype=jnp.float32) / n_heads)
    return m.reshape(DSW_N_GROUPS, DSW_HEADS)


def dilated_attend(q, k_ctx, v_ctx, q_idx, slopes, window, dilation):
    dist = dilation * jnp.arange(window // dilation + 1)
    idx = q_idx[:, None] - dist[None, :]
    valid = idx >= 0
    idx = jnp.maximum(idx, 0)
    k_g = k_ctx[:, idx]
    v_g = v_ctx[:, idx]
    s = jnp.einsum('bqhd,bqnhd->bhqn', q, k_g).astype(jnp.float32) * (DSW_HEAD_DIM ** -0.5)
    s = s - slopes[:, None, None] * dist.astype(jnp.float32)
    s = jnp.where(valid, s, -jnp.inf)
    m = jnp.max(s, axis=-1, keepdims=True)
    p = jnp.exp(s - m)
    l = jnp.sum(p, axis=-1, keepdims=True)
    o = jnp.einsum('bhqn,bqnhd->bqhd', (p / l).astype(v_ctx.dtype), v_g)
    return o, (m + jnp.log(l))[..., 0]


def dilated_mix(qs, ks, vs, q_idxs):
    slopes = alibi_slopes()
    outs, lses = [], []
    for g, (window, dilation) in enumerate(DSW_GROUPS):
        o, lse = dilated_attend(qs[g], ks[g], vs[g], q_idxs[g], slopes[g], window, dilation)
        outs.append(o.astype(jnp.float32))
        lses.append(lse)
    w = jax.nn.softmax(jnp.stack(lses), axis=0)
    w = jnp.transpose(w, (0, 1, 3, 2))[..., None]
    return jnp.sum(jnp.stack(outs) * w, axis=0)


def dsw_qkv(x, w_qkv):
    bsz, n = x.shape[:2]
    return (x @ w_qkv).reshape(bsz, n, DSW_N_GROUPS, 3, DSW_HEADS, DSW_HEAD_DIM)


def dsw_mixer_prompt(x, w_qkv, w_out):
    bsz, n = x.shape[:2]
    qkv = dsw_qkv(x, w_qkv)
    ks = [qkv[:, :, g, 1] for g in range(DSW_N_GROUPS)]
    vs = [qkv[:, :, g, 2] for g in range(DSW_N_GROUPS)]

    def block(i):
        start = i * Q_BLOCK
        qs = [lax.dynamic_slice_in_dim(qkv[:, :, g, 0], start, Q_BLOCK, axis=1) for g in range(DSW_N_GROUPS)]
        q_idx = start + jnp.arange(Q_BLOCK)
        return dilated_mix(qs, ks, vs, [q_idx] * DSW_N_GROUPS)

    o = lax.map(block, jnp.arange(n // Q_BLOCK))
    o = jnp.moveaxis(o, 0, 1).reshape(bsz, n, DSW_INNER).astype(x.dtype)
    new_kv = [qkv[:, n - min(w, n):, g, 1:] for g, (w, _) in enumerate(DSW_GROUPS)]
    return o @ w_out, new_kv


def dsw_mixer_sample(x, bufs, w_qkv, w_out):
    bsz, n = x.shape[:2]
    qkv = dsw_qkv(x, w_qkv)
    ctx = [jnp.concatenate([bufs[g].astype(qkv.dtype), qkv[:, :, g, 1:]], axis=1) for g in range(DSW_N_GROUPS)]
    qs = [qkv[:, :, g, 0] for g in range(DSW_N_GROUPS)]
    q_idxs = [bufs[g].shape[1] + jnp.arange(n) for g in range(DSW_N_GROUPS)]
    o = dilated_mix(qs, [c[:, :, 0] for c in ctx], [c[:, :, 1] for c in ctx], q_idxs)
    new_bufs = [c[:, n:] for c in ctx]
    return o.reshape(bsz, n, DSW_INNER).astype(x.dtype) @ w_out, new_bufs


def conv_ffn(x, prev, w_up, w_conv, w_down):
    gate, up = jnp.split(x @ w_up, 2, axis=-1)
    gate, new_prev = causal_dwconv(gate, prev, w_conv)
    return (jax.nn.silu(gate) * up) @ w_down, new_prev


def setup_inputs(seed: int = 0) -> dict:
    key = jax.random.key(seed)
    ks = jax.random.split(key, 32)
    f32 = jnp.float32

    def nrm(k, shape, scale=1.0):
        return jax.random.normal(k, shape, f32) * scale

    n_pages = PAST_LEN // PAGE_SIZE
    n_used = DEC_BATCH * n_pages
    n_pool = n_used + max(1, n_used // 4)
    page_table = jax.random.permutation(ks[0], n_pool)[:n_used].reshape(DEC_BATCH, n_pages).astype(jnp.int32)
    dsw_bufs = [nrm(ks[1 + g], (DEC_BATCH, min(w, PAST_LEN), 2, DSW_HEADS, DSW_HEAD_DIM))
                for g, (w, _) in enumerate(DSW_GROUPS)]
    return {
        'x_prompt': nrm(ks[4], (BATCH, SEQ, D_MODEL)),
        'x_sample': nrm(ks[5], (DEC_BATCH, DEC_SEQ, D_MODEL)),
        'cache_sc_conv': nrm(ks[6], (DEC_BATCH, CONV_W - 1, SC_WIDTH)),
        'cache_sb_k': nrm(ks[7], (n_pool, PAGE_SIZE, SB_HEADS, SB_HEAD_DIM)),
        'cache_sb_v': nrm(ks[8], (n_pool, PAGE_SIZE, SB_HEADS, SB_HEAD_DIM)),
        'state_gla': nrm(ks[9], (DEC_BATCH, GLA_HEADS, GLA_DK, GLA_DV)),
        'cache_dsw_kv0': dsw_bufs[0],
        'cache_dsw_kv1': dsw_bufs[1],
        'cache_dsw_kv2': dsw_bufs[2],
        'state_ffn_conv': nrm(ks[10], (DEPTH, DEC_BATCH, CONV_W - 1, D_FF)),
        'page_table': page_table,
        'ln_g': 1.0 + nrm(ks[11], (DEPTH, 2, D_MODEL), 0.02),
        'ln_b': nrm(ks[12], (DEPTH, 2, D_MODEL), 0.02),
        'w_sc_in': nrm(ks[13], (D_MODEL, 3 * SC_WIDTH), D_MODEL ** -0.5),
        'w_sc_conv': nrm(ks[14], (CONV_W, SC_WIDTH), CONV_W ** -0.5),
        'w_sc_out': nrm(ks[15], (SC_WIDTH, D_MODEL), DN_BETA * SC_WIDTH ** -0.5),
        'w_sb_qkv': nrm(ks[16], (D_MODEL, 3 * SB_HEADS * SB_HEAD_DIM), D_MODEL ** -0.5),
        'w_sb_out': nrm(ks[17], (SB_HEADS * SB_HEAD_DIM, D_MODEL), DN_BETA * (SB_HEADS * SB_HEAD_DIM) ** -0.5),
        'b_sb': jnp.linspace(SB_BIAS_LO, SB_BIAS_HI, SB_HEADS, dtype=f32) + nrm(ks[28], (SB_HEADS,), 0.1),
        'w_gla_in': nrm(ks[18], (D_MODEL, GLA_IN), D_MODEL ** -0.5),
        'w_gla_gate_up': nrm(ks[19], (GLA_RANK, GLA_QK), GLA_RANK ** -0.5),
        'b_gla_gate': nrm(ks[20], (GLA_QK,), 0.1),
        'g_gla_norm': 1.0 + nrm(ks[21], (GLA_DV,), 0.02),
        'w_gla_out': nrm(ks[22], (GLA_V, D_MODEL), DN_BETA * GLA_V ** -0.5),
        'w_dsw_qkv': nrm(ks[23], (D_MODEL, DSW_N_GROUPS * 3 * DSW_INNER), D_MODEL ** -0.5),
        'w_dsw_out': nrm(ks[24], (DSW_INNER, D_MODEL), DN_BETA * DSW_INNER ** -0.5),
        'w_ffn_up': nrm(ks[25], (DEPTH, D_MODEL, 2 * D_FF), D_MODEL ** -0.5),
        'w_ffn_conv': nrm(ks[26], (DEPTH, CONV_W, D_FF), CONV_W ** -0.5),
        'w_ffn_down': nrm(ks[27], (DEPTH, D_FF, D_MODEL), DN_BETA * D_FF ** -0.5),
    }


def reference(x_prompt, x_sample, cache_sc_conv, cache_sb_k, cache_sb_v, state_gla,
              cache_dsw_kv0, cache_dsw_kv1, cache_dsw_kv2, state_ffn_conv, page_table,
              ln_g, ln_b, w_sc_in, w_sc_conv, w_sc_out, w_sb_qkv, w_sb_out, b_sb,
              w_gla_in, w_gla_gate_up, b_gla_gate, g_gla_norm, w_gla_out,
              w_dsw_qkv, w_dsw_out, w_ffn_up, w_ffn_conv, w_ffn_down):
    xp, xs = x_prompt, x_sample
    bp = xp.shape[0]
    ffn_p, ffn_s = [], []
    for layer in range(DEPTH):
        kind = layer % N_MIXERS
        if kind == 0:
            mp, sc_p = short_conv_mixer(xp, jnp.zeros((bp, CONV_W - 1, SC_WIDTH), xp.dtype), w_sc_in, w_sc_conv, w_sc_out)
            ms, sc_s = short_conv_mixer(xs, cache_sc_conv, w_sc_in, w_sc_conv, w_sc_out)
        elif kind == 1:
            mp, sb_k_p, sb_v_p = sb_mixer_prompt(xp, w_sb_qkv, w_sb_out, b_sb)
            ms, sb_k_s, sb_v_s = sb_mixer_sample(xs, cache_sb_k, cache_sb_v, page_table, w_sb_qkv, w_sb_out, b_sb)
        elif kind == 2:
            s0 = jnp.zeros((bp, GLA_HEADS, GLA_DK, GLA_DV), state_gla.dtype)
            mp, gla_p = gla_mixer(xp, s0, w_gla_in, w_gla_gate_up, b_gla_gate, g_gla_norm, w_gla_out)
            ms, gla_s = gla_mixer(xs, state_gla, w_gla_in, w_gla_gate_up, b_gla_gate, g_gla_norm, w_gla_out)
        else:
            mp, dsw_p = dsw_mixer_prompt(xp, w_dsw_qkv, w_dsw_out)
            ms, dsw_s = dsw_mixer_sample(xs, [cache_dsw_kv0, cache_dsw_kv1, cache_dsw_kv2], w_dsw_qkv, w_dsw_out)
        xp = layer_norm(DN_ALPHA * xp + mp, ln_g[layer, 0], ln_b[layer, 0])
        xs = layer_norm(DN_ALPHA * xs + ms, ln_g[layer, 0], ln_b[layer, 0])
        fp, fprev_p = conv_ffn(xp, jnp.zeros((bp, CONV_W - 1, D_FF), xp.dtype), w_ffn_up[layer], w_ffn_conv[layer], w_ffn_down[layer])
        fs, fprev_s = conv_ffn(xs, state_ffn_conv[layer], w_ffn_up[layer], w_ffn_conv[layer], w_ffn_down[layer])
        ffn_p.append(fprev_p)
        ffn_s.append(fprev_s)
        xp = layer_norm(DN_ALPHA * xp + fp, ln_g[layer, 1], ln_b[layer, 1])
        xs = layer_norm(DN_ALPHA * xs + fs, ln_g[layer, 1], ln_b[layer, 1])
    return (xp, xs, sc_p, sc_s, sb_k_p, sb_v_p, sb_k_s, sb_v_s, gla_p, gla_s,
            dsw_p[0], dsw_p[1], dsw_p[2], dsw_s[0], dsw_s[1], dsw_s[2],
            jnp.stack(ffn_p), jnp.stack(ffn_s))
```

```python
import contextlib
import os as _os
import numpy as np
import concourse.bass as bass
import concourse.mybir as mybir
from concourse.bass_utils import run_bass_kernel_spmd

F32 = mybir.dt.float32
BF16 = mybir.dt.bfloat16
I32 = mybir.dt.int32
AF = mybir.ActivationFunctionType
ALU = mybir.AluOpType

D = 1024
NP = 4096
NS = 4
NTOK = NP + NS
DFF = 2816
NJ = DFF // 128
DN_ALPHA = (2.0 * 4) ** 0.25
LN_EPS = 1e-5
TILES = [(i * 512, 512) for i in range(8)] + [(NP, NS)]
N_CORES = 8
PROMPT_CORES = [0, 1, 4, 5]

SAME_ENGINE_SYNC = _os.environ.get("KSES", "1") == "1"
SKIP = _os.environ.get("KSKIP", "").split(",")
KSTOP = int(_os.environ.get("KSTOP", "1000000000"))


class Buf:
    __slots__ = ("name", "w", "r", "excl")

    def __init__(self, name="", excl=False):
        self.name = name
        self.excl = excl
        self.w = None
        self.r = {}


class _Eng:
    def __init__(self, name, handle, sem):
        self.name = name
        self.h = handle
        self.sem = sem
        self.count = 0
        self.ops = []
        self.known = {}
        self.slots = []
        self.slot_i = 0
        self.pending = []


class Prog:
    def __init__(self, nc, stack, n_dma_slots=12):
        self.nc = nc
        self.stack = stack
        self.E = {}
        for name, h in (("pe", nc.tensor), ("act", nc.scalar), ("dve", nc.vector), ("pool", nc.gpsimd), ("sp", nc.sync)):
            sem = stack.enter_context(nc.semaphore("s_" + name))
            self.E[name] = _Eng(name, h, sem)
        for name in ("sp", "pool", "act"):
            e = self.E[name]
            for i in range(n_dma_slots):
                e.slots.append([stack.enter_context(nc.semaphore(f"d_{name}{i}")), 0])
        self.nops = 0

    def _deps(self, reads, writes):
        waits = {}

        def add(ev):
            if ev is None:
                return
            s, v = ev
            if waits.get(s, (None, 0))[1] < v:
                waits[s] = (s, v)

        for b in reads:
            add(b.w)
            if b.excl:
                for s, v in b.r.items():
                    add((s, v))
        for b in writes:
            add(b.w)
            for s, v in b.r.items():
                add((s, v))
        return waits

    def _filter(self, e, waits, skip_own):
        out = []
        if e.pending:
            for (s, v) in e.pending:
                if s is e.sem and e.name == "pe":
                    continue
                if waits.get(s, (None, 0))[1] < v:
                    waits[s] = (s, v)
            e.pending = []
        for key, (s, v) in waits.items():
            if skip_own and s is e.sem:
                continue
            if e.known.get(id(s), 0) >= v:
                continue
            e.known[id(s)] = v
            out.append((s, v))
        return out

    def _mark(self, ev, reads, writes):
        s, v = ev
        for b in reads:
            if b.excl:
                b.w = ev
                b.r = {}
            elif b.r.get(s, 0) < v:
                b.r[s] = v
        for b in writes:
            b.w = ev
            b.r = {}

    def op(self, eng, fn, reads=(), writes=()):
        if self.nops >= KSTOP:
            return None
        e = self.E[eng]
        waits = self._deps(reads, writes)
        wl = self._filter(e, waits, skip_own=(eng == "pe") or not SAME_ENGINE_SYNC)
        e.count += 1
        ev = (e.sem, e.count)
        e.ops.append((wl, fn, (e.sem, 1)))
        self._mark(ev, reads, writes)
        self.nops += 1
        return ev

    def dma(self, eng, fn, reads=(), writes=()):
        if self.nops >= KSTOP:
            return None
        e = self.E[eng]
        slot = e.slots[e.slot_i % len(e.slots)]
        e.slot_i += 1
        waits = self._deps(reads, writes)
        if slot[1] > 0:
            s = slot[0]
            if waits.get(s, (None, 0))[1] < slot[1]:
                waits[s] = (s, slot[1])
        wl = self._filter(e, waits, skip_own=False)
        slot[1] += 16
        ev = (slot[0], slot[1])
        e.ops.append((wl, fn, (slot[0], 16)))
        self._mark(ev, reads, writes)
        self.nops += 1
        return ev

    def barrier(self):
        if _os.environ.get("KVERB"):
            print("barrier at op", self.nops)
        evs = []
        for en in self.E.values():
            for sm, cum in en.slots:
                if cum > 0:
                    evs.append((sm, cum))
            if en.count > 0:
                evs.append((en.sem, en.count))
        for en in self.E.values():
            en.pending = list(evs)

    def finish(self):
        e = self.E["sp"]
        wl = []
        for en in self.E.values():
            for s, cum in en.slots:
                if cum > 0:
                    wl.append((s, cum))
            if en.count > 0 and en is not e:
                wl.append((en.sem, en.count))
        e.ops.append((wl, None, None))

    def emit(self, block):
        def mk(e):
            def body(h):
                for wl, fn, inc in e.ops:
                    for s, v in wl:
                        h.wait_ge(s, v)
                    if fn is not None:
                        ins = fn(h)
                        ins.then_inc(inc[0], inc[1])
            return body

        block.tensor(mk(self.E["pe"]))
        block.scalar(mk(self.E["act"]))
        block.vector(mk(self.E["dve"]))
        block.gpsimd(mk(self.E["pool"]))
        block.sync(mk(self.E["sp"]))


class K:
    def __init__(self, layers=(0, 1, 2, 3), dbg=False, n_pool=2560):
        self.layers = layers
        self.dbg = dbg
        self.n_pool = n_pool
        self.nc = bass.Bass("TRN2", target_bir_lowering=False)
        self.stack = contextlib.ExitStack()
        self.P = Prog(self.nc, self.stack)
        self.din = {}
        self.dout = {}

    def inp(self, name, shape, dt=F32):
        t = self.nc.dram_tensor(name, list(shape), dt, kind="ExternalInput").ap()
        self.din[name] = t
        return t

    def outp(self, name, shape, dt=F32):
        t = self.nc.dram_tensor(name, list(shape), dt, kind="ExternalOutput").ap()
        self.dout[name] = t
        return t

    def scratch(self, name, shape, dt=F32):
        return self.nc.dram_tensor(name, list(shape), dt, kind="Internal").ap()

    def sb(self, name, shape, dt=F32, stack=None):
        self._uid = getattr(self, "_uid", 0) + 1
        return (stack or self.stack).enter_context(self.nc.sbuf_tensor(f"{name}_{self._uid}", list(shape), dt))

    def pst(self, name, shape, dt=F32, stack=None):
        return (stack or self.stack).enter_context(self.nc.psum_tensor(name, list(shape), dt))

    def mm(self, out, lhsT, rhs, start, stop, reads, writes, **kw):
        return self.P.op("pe", lambda h: h.matmul(out, lhsT, rhs, start=start, stop=stop, **kw), reads, writes)

    def act(self, out, in_, func, reads, writes, bias=None, scale=None, eng="act"):
        kw = {}
        if bias is not None:
            kw["bias"] = bias
        if scale is not None:
            kw["scale"] = scale
        return self.P.op("act", lambda h: h.activation(out, in_, func, **kw), reads, writes)

    def tt(self, eng, out, in0, in1, op, reads, writes):
        return self.P.op(eng, lambda h: h.tensor_tensor(out, in0, in1, op), reads, writes)

    def ts(self, eng, out, in0, s1, s2, op0, op1, reads, writes):
        if op1 is None:
            return self.P.op(eng, lambda h: h.tensor_scalar(out, in0, s1, None, op0), reads, writes)
        return self.P.op(eng, lambda h: h.tensor_scalar(out, in0, s1, s2, op0, op1), reads, writes)

    def stt(self, out, in0, scalar, in1, op0, op1, reads, writes):
        return self.P.op("dve", lambda h: h.scalar_tensor_tensor(out, in0, scalar, in1, op0, op1), reads, writes)

    def cp(self, eng, out, in_, reads, writes):
        if eng == "act":
            return self.P.op("act", lambda h: h.copy(out, in_), reads, writes)
        return self.P.op(eng, lambda h: h.tensor_copy(out, in_), reads, writes)

    def memset(self, eng, ap, val, writes):
        return self.P.op(eng, lambda h: h.memset(ap, val), (), writes)

    def dma(self, eng, out, in_, reads, writes, **kw):
        return self.P.dma(eng, lambda h: h.dma_start(out=out, in_=in_, **kw), reads, writes)

    def build(self):
        nc, P = self.nc, self.P
        L = self.layers
        xT0 = self.inp("xT0", [D, NTOK])
        pv = self.inp("pv", [128, NPV])
        cst = self.inp("cstf", [128, NCF])
        cstb_d = self.inp("cstb", [128, NCB])
        w = {}
        w["sc_in"] = self.inp("w_sc_in", [D, 3 * D])
        w["sc_out"] = self.inp("w_sc_out", [D, D])
        w["ffn_up"] = self.inp("w_ffn_up", [4, D, 2 * DFF])
        w["ffn_down"] = self.inp("w_ffn_down", [4, DFF, D])
        io = {}
        if 1 in L:
            w["sb_qkv"] = self.inp("w_sb_qkv", [D, 3 * D])
            w["sb_out"] = self.inp("w_sb_out", [D, D])
            io["cache_k"] = self.inp("cache_sb_k", [self.n_pool, 128, 16, 64])
            io["cache_v"] = self.inp("cache_sb_v", [self.n_pool, 128, 16, 64])
            io["page_table"] = self.inp("page_table", [NS, 64], I32)
            io["bd16"] = self.inp("bd16", [16, D])
            io["bsbr"] = self.inp("bsbr", [128, D])
            io["sb_k_p"] = self.outp("sb_k_p", [NP, D])
            io["sb_v_p"] = self.outp("sb_v_p", [NP, D])
            io["sb_k_s"] = self.outp("sb_k_s", [NS, D])
            io["sb_v_s"] = self.outp("sb_v_s", [NS, D])
        if 2 in L:
            w["gla_in"] = self.inp("w_gla_in", [D, 3088])
            w["gla_gate_up"] = self.inp("w_gla_gate_up", [16, 512])
            w["gla_out"] = self.inp("w_gla_out", [D, D])
            io["state_gla"] = self.inp("state_gla", [NS, 4, 128, 256])
            io["gla_p"] = self.outp("gla_p", [4, 128, 256])
            io["gla_s"] = self.outp("gla_s", [NS, 4, 128, 256])
        if 3 in L:
            w["dsw_qkv"] = self.inp("w_dsw_qkv", [D, 4608])
            w["dsw_out"] = self.inp("w_dsw_out", [512, D])
            io["dswB"] = self.inp("dswB", [8, 128, 24 * 128])
            io["dswAB"] = self.inp("dswAB", [128, 24])
            io["bd8"] = self.inp("bd8", [8, 512])
            for g, wg in enumerate((128, 512, 2048)):
                io[f"cache_dsw{g}"] = self.inp(f"cache_dsw{g}", [NS, wg, D])
                io[f"dsw_p{g}"] = self.outp(f"dsw_p{g}", [wg, D])
                io[f"dsw_s{g}"] = self.outp(f"dsw_s{g}", [NS, wg, D])
        sc_cache = self.inp("sc_cache", [NS * 2, D])
        ffn_state = self.inp("ffn_state", [4, NS * 2, DFF])
        yT = self.outp("yT", [D, NTOK])
        sc_out = self.outp("sc_out", [D, 2 + 2 * NS])
        ffn_out = self.outp("ffn_out", [4, DFF, 2 + 2 * NS])
        xs = [self.scratch("xs0", [D, NTOK]), self.scratch("xs1", [D, NTOK])]
        self.xs = xs
        self.pv = self.sb("pv_sb", [128, NPV])
        self.cst = self.sb("cst_sb", [128, NCF])
        self.cstb = self.sb("cstb_sb", [128, NCB], BF16)
        self.b_pv = Buf("pv")
        self.b_cst = Buf("cst")
        self.dma("sp", self.pv[:], pv[:, :], (), (self.b_pv,))
        self.dma("sp", self.cst[:], cst[:, :], (), (self.b_cst,))
        self.dma("pool", self.cstb[:], cstb_d[:, :], (), (self.b_cst,))
        self.banks = [self.pst(f"bank{i}", [128, 512]) for i in range(8)]
        self.bbuf = [Buf(f"bank{i}", excl=True) for i in range(8)]
        self.bank_rr = {}
        self.b_xs = [Buf("xs0"), Buf("xs1")]

        src = xT0
        b_src = Buf("xT0")
        for layer in L:
            if layer == 0:
                self.mixer_sconv(src, b_src, xs[0], self.b_xs[0], w, sc_cache, sc_out)
            elif layer == 1:
                self.mixer_sb(src, b_src, xs[0], self.b_xs[0], w, io)
            elif layer == 2:
                self.mixer_gla(src, b_src, xs[0], self.b_xs[0], w, io)
            elif layer == 3:
                self.mixer_dsw(src, b_src, xs[0], self.b_xs[0], w, io)
            else:
                raise NotImplementedError
            P.barrier()
            last = layer == L[-1]
            self.ffn(layer, xs[0], self.b_xs[0], yT if last else xs[1], Buf("yT") if last else self.b_xs[1], w, ffn_state, ffn_out)
            P.barrier()
            src = xs[1]
            b_src = self.b_xs[1]
        P.finish()
        with nc.Block() as block:
            P.emit(block)
        self.stack.close()
        return nc

    def ps(self, pool, idxs):
        i = self.bank_rr.get(pool, 0)
        self.bank_rr[pool] = i + 1
        k = idxs[i % len(idxs)]
        return self.banks[k], self.bbuf[k]

    def load_w(self, dst, b_dst, src2d, K, M, mchunk=1024):
        sv = src2d.rearrange("(k p) m -> p k m", p=128)
        for m0 in range(0, M, mchunk):
            m1 = min(M, m0 + mchunk)
            self.dma("pool", dst[:, :, m0:m1], sv[:, :, m0:m1], (), (b_dst,))

    def post_ln(self, st, xt, b_xt, N, ln_idx, dst, b_dst, c0, tmp):
        cst, cstb = self.cst, self.cstb
        onesb = cstb[:, C_ONES_S:C_ONES_S + 128]
        A, bA = self.ps("ln", [6, 7])
        B, bB = self.ps("ln", [6, 7])
        rb, b_rb, sq, b_sq = tmp["rb"], tmp["b_rb"], tmp["sq"], tmp["b_sq"]
        for k in range(8):
            i = k % 2
            self.cp("dve", rb[i][:, :N], xt[:, k, :N], (b_xt,), (b_rb[i],))
            self.act(sq[i][:, :N], xt[:, k, :N], AF.Square, (b_xt,), (b_sq[i],))
            self.mm(A[:, :N], onesb, rb[i][:, :N], k == 0, k == 7, (b_rb[i], self.b_cst), (bA,))
            self.mm(B[:, :N], onesb, sq[i][:, :N], k == 0, k == 7, (b_sq[i], self.b_cst), (bB,))
        mean, b_mean = tmp["mean"], tmp["b_mean"]
        t1, b_t1 = tmp["t1"], tmp["b_t1"]
        rstd, b_rstd = tmp["rstd"], tmp["b_rstd"]
        self.cp("act", mean[:, :N], A[:, :N], (bA,), (b_mean,))
        self.tt("dve", t1[:, :N], mean[:, :N], mean[:, :N], ALU.mult, (b_mean,), (b_t1,))
        self.tt("dve", t1[:, :N], B[:, :N], t1[:, :N], ALU.subtract, (bB, b_t1), (b_t1,))
        self.act(t1[:, :N], t1[:, :N], AF.Sqrt, (b_t1,), (b_t1,), bias=self.pv[:, PV_EPS:PV_EPS + 1])
        self.P.op("dve", lambda h: h.reciprocal(rstd[:, :N], t1[:, :N]), (b_t1,), (b_rstd,))
        self.stt(t1[:, :N], mean[:, :N], -1.0, rstd[:, :N], ALU.mult, ALU.mult, (b_mean, b_rstd), (b_t1,))
        g0 = PV_LNG + ln_idx * 8
        b0 = PV_LNB + ln_idx * 8
        for k in range(8):
            self.tt("dve", xt[:, k, :N], xt[:, k, :N], rstd[:, :N], ALU.mult, (b_xt, b_rstd), (b_xt,))
            self.tt("dve", xt[:, k, :N], xt[:, k, :N], t1[:, :N], ALU.add, (b_xt, b_t1), (b_xt,))
            self.act(xt[:, k, :N], xt[:, k, :N], AF.Identity, (b_xt, self.b_pv), (b_xt,),
                     bias=self.pv[:, b0 + k:b0 + k + 1], scale=self.pv[:, g0 + k:g0 + k + 1])
        self.dma("sp", dst[:, c0:c0 + N].rearrange("(k p) n -> p k n", p=128), xt[:, :, :N], (b_xt,), (b_dst,))

    def ln_tmp(self, st):
        t = {}
        t["rb"] = [self.sb(f"ln_rb{i}", [128, 512], BF16, st) for i in range(2)]
        t["sq"] = [self.sb(f"ln_sq{i}", [128, 512], BF16, st) for i in range(2)]
        t["b_rb"] = [Buf(), Buf()]
        t["b_sq"] = [Buf(), Buf()]
        for nm in ("mean", "t1", "rstd"):
            t[nm] = self.sb("ln_" + nm, [128, 512], F32, st)
            t["b_" + nm] = Buf()
        return t

    def load_halo(self, st, src2d, nj, dst, b_dst):
        ident = self.cst[:, C_IDENT:C_IDENT + 128]
        rows = 2 * NS * nj
        sv = src2d.rearrange("q (j c) -> (q j) c", c=128)
        pt, bpt = self.ps("tr", [4, 5])
        r0 = 0
        while r0 < rows:
            n = min(128, rows - r0)
            t = self.sb("halo_stage", [128, 128], F32, st)
            bt = Buf()
            self.dma("sp", t[0:n, :], sv[r0:r0 + n, :], (), (bt,))
            self.P.op("pe", lambda h, t=t, n=n, r0=r0: h.transpose(pt[:, r0:r0 + n], t[0:n, :], ident[0:n, 0:n]),
                      (bt, self.b_cst), (bpt,))
            r0 += n
        self.cp("dve", dst[:, :, 0:2, :], pt[:, 0:rows].rearrange("p (s r j) -> p j r s", s=NS, r=2), (bpt,), (b_dst,))

    def conv3(self, out, taps, wcol, reads, writes, eng="dve"):
        self.ts("dve", out, taps[0], wcol(0), None, ALU.mult, None, reads, writes)
        self.stt(out, taps[1], wcol(1), out, ALU.mult, ALU.add, list(reads) + list(writes), writes)
        self.stt(out, taps[2], wcol(2), out, ALU.mult, ALU.add, list(reads) + list(writes), writes)

    def mixer_sconv(self, src, b_src, dst, b_dst, w, sc_cache, sc_out):
        with contextlib.ExitStack() as st:
            win = self.sb("sc_win", [128, 8, 3 * D], BF16, st)
            wout = self.sb("sc_wout", [128, 8, D], BF16, st)
            b_win, b_wout = Buf("win"), Buf("wout")
            self.load_w(win, b_win, w["sc_in"], D, 3 * D)
            self.load_w(wout, b_wout, w["sc_out"], D, D)
            xts = [self.sb(f"sc_xt{i}", [128, 8, 512], F32, st) for i in range(2)]
            b_xts = [Buf("xt0"), Buf("xt1")]
            xb = self.sb("sc_xb", [128, 8, 512], BF16, st)
            b_xb = Buf("xb")
            u = self.sb("sc_u", [128, 8, 516], F32, st)
            b_u = Buf("u")
            yb = self.sb("sc_yb", [128, 8, 512], BF16, st)
            b_yb = Buf("yb")
            cs = [self.sb(f"sc_c{i}", [128, 512], F32, st) for i in range(2)]
            b_cs = [Buf(), Buf()]
            zs = [self.sb(f"sc_z{i}", [128, 512], F32, st) for i in range(2)]
            b_zs = [Buf(), Buf()]
            us = self.sb("sc_us", [128, 8, 3, NS], F32, st)
            b_us = Buf("us")
            sco = self.sb("sc_o", [128, 8, 2 + 2 * NS], F32, st)
            b_sco = Buf()
            tmps = [self.ln_tmp(st), self.ln_tmp(st)]
            ident = self.cst[:, C_IDENT:C_IDENT + 128]
            self.load_halo(st, sc_cache, 8, us, b_us)
            self.memset("dve", u[:, :, 0:2], 0.0, (b_u,))
            sv0 = src[:, 0:512].rearrange("(k p) n -> p k n", p=128)
            self.dma("pool", xb[:, :, :512], sv0, (b_src,), (b_xb,))
            for ti, (c0, N) in enumerate(TILES):
                sample = N == NS
                xt, b_xt, tmp = xts[ti % 2], b_xts[ti % 2], tmps[ti % 2]
                sv = src[:, c0:c0 + N].rearrange("(k p) n -> p k n", p=128)
                self.dma("sp", xt[:, :, :N], sv, (b_src,), (b_xt,))
                if not sample and ti > 0:
                    self.cp("dve", u[:, :, 0:2], u[:, :, 512:514], (b_u,), (b_u,))
                for j in range(8):
                    pb, bpb = self.ps("a6", [0, 1, 2, 3, 4, 5])
                    pc, bpc = self.ps("a6", [0, 1, 2, 3, 4, 5])
                    pvv, bpv = self.ps("a6", [0, 1, 2, 3, 4, 5])
                    for (pp, bpp, off) in ((pb, bpb, 0), (pc, bpc, D), (pvv, bpv, 2 * D)):
                        for k in range(8):
                            self.mm(pp[:, :N], win[:, k, off + j * 128: off + (j + 1) * 128], xb[:, k, :N], k == 0, k == 7,
                                    (b_win, b_xb), (bpp,))
                    i = j % 2
                    self.cp("act", cs[i][:, :N], pc[:, :N], (bpc,), (b_cs[i],))
                    if sample:
                        ud = us[:, j, 2, :]
                        taps = [us[:, j, 0, :], us[:, j, 1, :], us[:, j, 2, :]]
                        bu = b_us
                    else:
                        ud = u[:, j, 2:2 + N]
                        taps = [u[:, j, 0:N], u[:, j, 1:1 + N], u[:, j, 2:2 + N]]
                        bu = b_u
                    self.tt("dve", ud, cs[i][:, :N], pvv[:, :N], ALU.mult, (b_cs[i], bpv), (bu,))
                    self.conv3(zs[i][:, :N], taps, lambda t, j=j: self.pv[:, PV_SCW + j * 3 + t: PV_SCW + j * 3 + t + 1],
                               (bu, self.b_pv), (b_zs[i],))
                    self.tt("dve", yb[:, j, :N], zs[i][:, :N], pb[:, :N], ALU.mult, (b_zs[i], bpb), (b_yb,))
                if ti + 1 < len(TILES):
                    c0n, Nn = TILES[ti + 1]
                    self.dma("pool", xb[:, :, :Nn], src[:, c0n:c0n + Nn].rearrange("(k p) n -> p k n", p=128), (b_src,), (b_xb,))
                for m in range(8):
                    po, bpo = self.ps("o", [6, 7])
                    for k in range(8):
                        self.mm(po[:, :N], wout[:, k, m * 128:(m + 1) * 128], yb[:, k, :N], k == 0, k == 7, (b_wout, b_yb), (bpo,))
                    self.stt(xt[:, m, :N], xt[:, m, :N], DN_ALPHA, po[:, :N], ALU.mult, ALU.add, (b_xt, bpo), (b_xt,))
                self.post_ln(st, xt, b_xt, N, 0, dst, b_dst, c0, tmp)
                if ti == 7:
                    self.cp("pool", sco[:, :, 0:2], u[:, :, 512:514], (b_u,), (b_sco,))
            self.cp("pool", sco[:, :, 2:2 + 2 * NS].rearrange("p k (r s) -> p k r s", r=2), us[:, :, 1:3, :], (b_us,), (b_sco,))
            self.dma("sp", sc_out.rearrange("(k p) c -> p k c", p=128), sco[:], (b_sco,), (Buf(),))

    def out_proj_ln(self, st, oT, b_oT, nk, wsrc, Kdim, res_src, b_res, dst, b_dst, ln_idx):
        wo = self.sb("wo", [128, nk, D], BF16, st)
        b_wo = Buf("wo")
        self.load_w(wo, b_wo, wsrc, Kdim, D)
        xts = [self.sb(f"op_xt{i}", [128, 8, 512], F32, st) for i in range(2)]
        b_xts = [Buf("xt0"), Buf("xt1")]
        tmps = [self.ln_tmp(st), self.ln_tmp(st)]
        for ti, (c0, N) in enumerate(TILES):
            xt, b_xt, tmp = xts[ti % 2], b_xts[ti % 2], tmps[ti % 2]
            sv = res_src[:, c0:c0 + N].rearrange("(k p) n -> p k n", p=128)
            self.dma("sp", xt[:, :, :N], sv, (b_res,), (b_xt,))
            for m in range(8):
                po, bpo = self.ps("o", [4, 5])
                for k in range(nk):
                    self.mm(po[:, :N], wo[:, k, m * 128:(m + 1) * 128], oT[:, k, c0:c0 + N], k == 0, k == nk - 1,
                            (b_wo,) + tuple(b_oT), (bpo,))
                self.stt(xt[:, m, :N], xt[:, m, :N], DN_ALPHA, po[:, :N], ALU.mult, ALU.add, (b_xt, bpo), (b_xt,))
            self.post_ln(st, xt, b_xt, N, ln_idx, dst, b_dst, c0, tmp)

    def mixer_sb(self, src, b_src, dst, b_dst, w, io):
        qT = self.scratch("sb_qT", [D, NP], BF16)
        kT = self.scratch("sb_kT", [D, NP], BF16)
        vtok = self.scratch("sb_vtok", [NP, D], BF16)
        b_qT, b_kT, b_vtok = Buf("qT"), Buf("kT"), Buf("vtok")
        cst, cstb, pv = self.cst, self.cstb, self.pv
        with contextlib.ExitStack() as sl:
            oT = self.sb("sb_oT", [128, 8, NTOK], BF16, sl)
            b_oT = [Buf(f"oT{j}") for j in range(8)]
            qs = self.sb("sb_qs", [NS, D], F32, sl)
            b_qs = Buf("qs")
            with contextlib.ExitStack() as st:
                wq = self.sb("sb_wq", [128, 8, 3 * D], BF16, st)
                b_wq = Buf("wq")
                self.load_w(wq, b_wq, w["sb_qkv"], D, 3 * D)
                xb = self.sb("sb_xb", [128, 8, 512], BF16, st)
                b_xb = Buf("xb")
                stg = [self.sb(f"sb_stg{i}", [128, 8, 512], BF16, st) for i in range(2)]
                b_stg = [Buf(), Buf()]
                stf = [self.sb(f"sb_stf{i}", [128, D], F32, st) for i in range(2)]
                b_stf = [Buf(), Buf()]
                stv = self.sb("sb_stv", [128, D], BF16, st)
                b_stv = Buf()
                for ti, (c0, N) in enumerate(TILES):
                    sample = N == NS
                    sv = src[:, c0:c0 + N].rearrange("(k p) n -> p k n", p=128)
                    self.dma("pool", xb[:, :, :N], sv, (b_src,), (b_xb,))
                    if not sample:
                        for qk in range(2):
                            for m in range(8):
                                pp, bpp = self.ps("a", [0, 1, 2, 3])
                                for k in range(8):
                                    self.mm(pp[:, :N], wq[:, k, qk * D + m * 128: qk * D + (m + 1) * 128], xb[:, k, :N], k == 0, k == 7,
                                            (b_wq, b_xb), (bpp,))
                                self.cp("act" if m % 2 else "dve", stg[qk][:, m, :N], pp[:, :N], (bpp,), (b_stg[qk],))
                            dd, bd = (qT, b_qT) if qk == 0 else (kT, b_kT)
                            self.dma("sp", dd[:, c0:c0 + N].rearrange("(k p) n -> p k n", p=128), stg[qk][:, :, :N], (b_stg[qk],), (bd,))
                        for blk in range(4):
                            r0 = c0 + blk * 128
                            for kv in range(2):
                                for half in range(2):
                                    pp, bpp = self.ps("o", [4, 5])
                                    off = (1 + kv) * D + half * 512
                                    for k in range(8):
                                        self.mm(pp[:, :], xb[:, k, blk * 128:(blk + 1) * 128], wq[:, k, off:off + 512], k == 0, k == 7,
                                                (b_wq, b_xb), (bpp,))
                                    self.cp("act", stf[kv][:, half * 512:(half + 1) * 512], pp[:, :], (bpp,), (b_stf[kv],))
                                    if kv == 1:
                                        self.cp("dve", stv[:, half * 512:(half + 1) * 512], pp[:, :], (bpp,), (b_stv,))
                                od = io["sb_k_p"] if kv == 0 else io["sb_v_p"]
                                self.dma("sp", od[r0:r0 + 128, :], stf[kv][:], (b_stf[kv],), (Buf(),))
                            self.dma("sp", vtok[r0:r0 + 128, :], stv[:], (b_stv,), (b_vtok,))
                    else:
                        for qkv in range(3):
                            for half in range(2):
                                pp, bpp = self.ps("o", [4, 5])
                                off = qkv * D + half * 512
                                for k in range(8):
                                    self.mm(pp[0:NS, :], xb[:, k, 0:NS], wq[:, k, off:off + 512], k == 0, k == 7, (b_wq, b_xb), (bpp,))
                                if qkv == 0:
                                    self.cp("act", qs[:, half * 512:(half + 1) * 512], pp[0:NS, :], (bpp,), (b_qs,))
                                else:
                                    self.cp("act", stf[qkv - 1][0:NS, half * 512:(half + 1) * 512], pp[0:NS, :], (bpp,), (b_stf[qkv - 1],))
                            if qkv > 0:
                                od = io["sb_k_s"] if qkv == 1 else io["sb_v_s"]
                                self.dma("sp", od[:, :], stf[qkv - 1][0:NS, :], (b_stf[qkv - 1],), (Buf(),))
            self.P.barrier()
            with contextlib.ExitStack() as st:
              if "sbp" not in SKIP:
                  kj = [[self.sb(f"sb_kj{i}{x}", [128, NP], BF16, st) for x in range(2)] for i in range(2)]
                  qj = [self.sb(f"sb_qj{i}", [128, NP], BF16, st) for i in range(2)]
                  vj = [[self.sb(f"sb_vj{i}{x}", [128, 32, 128], BF16, st) for x in range(2)] for i in range(2)]
                  b_kj, b_qj, b_vj = [Buf(), Buf()], [Buf(), Buf()], [Buf(), Buf()]
                  for i in range(2):
                      self.memset("pool", kj[i][0][64:128, :], 0.0, (b_kj[i],))
                      self.memset("pool", kj[i][1][0:64, :], 0.0, (b_kj[i],))
                      self.memset("pool", vj[i][0][:, :, 64:128], 0.0, (b_vj[i],))
                      self.memset("pool", vj[i][1][:, :, 0:64], 0.0, (b_vj[i],))
                  NR = 3
                  e_t = [self.sb(f"sb_e{i}", [128, 512], F32, st) for i in range(NR)]
                  sp_t = [self.sb(f"sb_sp{i}", [128, 512], BF16, st) for i in range(NR)]
                  eb_t = [self.sb(f"sb_eb{i}", [128, 512], F32, st) for i in range(NR)]
                  w_t = [self.sb(f"sb_w{i}", [128, 512], BF16, st) for i in range(NR)]
                  b_e, b_sp, b_eb, b_w = ([Buf() for _ in range(NR)] for _ in range(4))
                  R_t = [self.sb(f"sb_R{i}", [128, 512], F32, st) for i in range(2)]
                  Rb_t = [self.sb(f"sb_Rb{i}", [128, 512], BF16, st) for i in range(2)]
                  b_R, b_Rb = [Buf(), Buf()], [Buf(), Buf()]
                  mask = cst[:, C_MASK_S:C_MASK_S + 128]
                  ntri = cstb[:, C_NTRI:C_NTRI + 128]
                  nones = cstb[:, C_NONES:C_NONES + 128]
                  zer = cstb[:, C_ZERO:C_ZERO + 128]
                  vt3 = vtok.rearrange("(b p) c -> p b c", p=128)
                  for j in range(8):
                      jb = j % 2
                      self.dma("sp", kj[jb][0][0:64, :], kT[j * 128:j * 128 + 64, :], (b_kT,), (b_kj[jb],))
                      self.dma("sp", kj[jb][1][64:128, :], kT[j * 128 + 64:(j + 1) * 128, :], (b_kT,), (b_kj[jb],))
                      self.dma("sp", qj[jb][:], qT[j * 128:(j + 1) * 128, :], (b_qT,), (b_qj[jb],))
                      self.dma("sp", vj[jb][0][:, :, 0:64], vt3[:, :, j * 128:j * 128 + 64], (b_vtok,), (b_vj[jb],))
                      self.dma("sp", vj[jb][1][:, :, 64:128], vt3[:, :, j * 128 + 64:(j + 1) * 128], (b_vtok,), (b_vj[jb],))
                      blocks = []
                      seqn = 0
                      for t in range(8):
                          for hh in range(2):
                              for kb in range(4 * t + 3, -1, -1):
                                  blocks.append((t, hh, kb, kb == 4 * t + 3, kb == 0, seqn))
                              seqn += 1

                      def geom(b):
                          t, hh, kb, first, last, sq = b
                          r = kb - 4 * t
                          c_lo = 128 * max(r, 0)
                          return t, hh, kb, first, last, sq, r, c_lo

                      def stA(i):
                          t, hh, kb, first, last, sq, r, c_lo = geom(blocks[i])
                          S, bS = self.banks[i % 2], self.bbuf[i % 2]
                          self.mm(S[:, c_lo:512], kj[jb][hh][:, kb * 128:(kb + 1) * 128], qj[jb][:, t * 512 + c_lo:(t + 1) * 512],
                                  True, True, (b_kj[jb], b_qj[jb]), (bS,))

                      def stB(i):
                          t, hh, kb, first, last, sq, r, c_lo = geom(blocks[i])
                          S, bS = self.banks[i % 2], self.bbuf[i % 2]
                          x = i % NR
                          h = 2 * j + hh
                          self.act(e_t[x][:, c_lo:512], S[:, c_lo:512], AF.Exp, (bS, self.b_pv), (b_e[x],),
                                   bias=pv[:, PV_BSB + h:PV_BSB + h + 1], scale=0.125)
                          if r >= 0:
                              self.tt("dve", e_t[x][:, c_lo:c_lo + 128], e_t[x][:, c_lo:c_lo + 128], mask, ALU.mult, (b_e[x], self.b_cst), (b_e[x],))
                          self.act(sp_t[x][:, c_lo:512], e_t[x][:, c_lo:512], AF.Ln, (b_e[x],), (b_sp[x],), bias=1.0)

                      def stC(i):
                          t, hh, kb, first, last, sq, r, c_lo = geom(blocks[i])
                          BT, bBT = self.banks[2 + i % 2], self.bbuf[2 + i % 2]
                          x = i % NR
                          self.mm(BT[:, c_lo:512], ntri, sp_t[x][:, c_lo:512], True, first, (b_sp[x], self.b_cst), (bBT,))
                          if not first:
                              self.mm(BT[:, c_lo:512], nones, Rb_t[sq % 2][:, c_lo:512], False, True, (b_Rb[sq % 2], self.b_cst), (bBT,))

                      def stD(i):
                          t, hh, kb, first, last, sq, r, c_lo = geom(blocks[i])
                          BT, bBT = self.banks[2 + i % 2], self.bbuf[2 + i % 2]
                          x = i % NR
                          self.act(eb_t[x][:, c_lo:512], BT[:, c_lo:512], AF.Exp, (bBT,), (b_eb[x],))
                          self.tt("dve", w_t[x][:, c_lo:512], e_t[x][:, c_lo:512], eb_t[x][:, c_lo:512], ALU.mult, (b_e[x], b_eb[x]), (b_w[x],))
                          if not last:
                              y = sq % 2
                              if first:
                                  self.memset("pool", R_t[y][:], 0.0, (b_R[y],))
                                  self.memset("pool", Rb_t[y][:], 0.0, (b_Rb[y],))
                              self.tt("pool", R_t[y][:, c_lo:512], R_t[y][:, c_lo:512], sp_t[x][:, c_lo:512], ALU.add, (b_R[y], b_sp[x]), (b_R[y],))
                              self.cp("act" if i % 2 else "dve", Rb_t[y][:, c_lo:512], R_t[y][:, c_lo:512], (b_R[y],), (b_Rb[y],))

                      def stE(i):
                          t, hh, kb, first, last, sq, r, c_lo = geom(blocks[i])
                          O, bO = self.banks[6 + t % 2], self.bbuf[6 + t % 2]
                          x = i % NR
                          if first and hh == 0:
                              self.mm(O[:, :], zer, qj[jb][:, 0:512], True, False, (b_qj[jb], self.b_cst), (bO,))
                          self.mm(O[:, c_lo:512], vj[jb][hh][:, kb, :], w_t[x][:, c_lo:512], False, last and hh == 1, (b_vj[jb], b_w[x]), (bO,))
                          if last and hh == 1:
                              self.cp("dve", oT[:, j, t * 512:(t + 1) * 512], O[:, :], (bO,), (b_oT[j],))

                      nb = len(blocks)
                      for i in range(nb + 2):
                          if i < nb:
                              stA(i)
                              stB(i)
                          if 0 <= i - 1 < nb:
                              stC(i - 1)
                              stD(i - 1)
                          if 0 <= i - 2 < nb:
                              stE(i - 2)
            self.P.barrier()
            with contextlib.ExitStack() as st:
                if "sbs" not in SKIP:
                    self.sb_sample(st, io, oT, b_oT, qs, b_qs)
            self.P.barrier()
            with contextlib.ExitStack() as st:
                self.out_proj_ln(st, oT, b_oT, 8, w["sb_out"], D, src, b_src, dst, b_dst, 2)
        self.P.barrier()

    def sb_sample(self, st, io, oT, b_oT, qs, b_qs):
        cst, cstb, pv = self.cst, self.cstb, self.pv
        NPG = 64
        ck = io["cache_k"].rearrange("n p h d -> (n p) (h d)")
        cv = io["cache_v"].rearrange("n p h d -> (n p) (h d)")
        pti = self.sb("ss_pti", [128, NPG], I32, st)
        ptf = self.sb("ss_ptf", [128, NPG], F32, st)
        idx = self.sb("ss_idx", [128, NPG], I32, st)
        b_pti, b_ptf, b_idx = Buf(), Buf(), Buf()
        qb = self.sb("ss_qb", [128, D], BF16, st)
        b_qb = Buf()
        NKB = 4
        kpg = [self.sb(f"ss_k{i}", [128, D], BF16, st) for i in range(NKB)]
        b_kpg = [Buf() for _ in range(NKB)]
        vpg = [self.sb(f"ss_v{i}", [128, D], BF16, st) for i in range(NKB)]
        b_vpg = [Buf() for _ in range(NKB)]
        prod = [self.sb(f"ss_prod{i}", [128, D], F32, st) for i in range(2)]
        b_prod = [Buf(), Buf()]
        z = self.sb("ss_z", [128, NPG * 16], F32, st)
        b_z = Buf()
        e = self.sb("ss_e", [128, NPG * 16], F32, st)
        b_e = Buf()
        spb = self.sb("ss_spb", [128, NPG * 16], BF16, st)
        b_spb = Buf()
        ca = self.sb("ss_ca", [128, NPG * 16], F32, st)
        cb = self.sb("ss_cb", [128, NPG * 16], F32, st)
        b_ca, b_cb = Buf(), Buf()
        wt = self.sb("ss_w", [128, NPG * 16], BF16, st)
        b_wt = Buf()
        osb = self.sb("ss_o", [16, D], F32, st)
        b_osb = Buf()
        ocol = self.sb("ss_oc", [128, 8], F32, st)
        b_ocol = Buf()
        ntri = cstb[:, C_NTRI:C_NTRI + 128]
        nones = cstb[:, C_NONES:C_NONES + 128]
        qsb = self.sb("ss_qsb", [NS, D], BF16, st)
        b_qsb = Buf()
        bd = self.sb("ss_bd", [16, D], F32, st)
        bsbr = self.sb("ss_bsbr", [128, D], F32, st)
        b_bd = Buf()
        self.dma("sp", bd[:], io["bd16"][:, :], (), (b_bd,))
        self.dma("sp", bsbr[:], io["bsbr"][:, :], (), (b_bd,))
        self.cp("dve", qsb[:], qs[:], (b_qs,), (b_qsb,))
        for s_ in range(NS):
            self.dma("sp", pti[:], io["page_table"][s_:s_ + 1, :].broadcast_to([128, NPG]), (), (b_pti,))
            self.cp("dve", ptf[:], pti[:], (b_pti,), (b_ptf,))
            self.ts("dve", ptf[:], ptf[:], 128.0, cst[:, C_PCOL:C_PCOL + 1], ALU.mult, ALU.add, (b_ptf, self.b_cst), (b_ptf,))
            self.cp("dve", idx[:], ptf[:], (b_ptf,), (b_idx,))
            for half in range(2):
                pp, bpp = self.ps("o", [4, 5])
                self.mm(pp[:, :], cstb[0:NS, C_SEL + s_ * 128:C_SEL + (s_ + 1) * 128], qsb[:, half * 512:(half + 1) * 512], True, True,
                        (b_qsb, self.b_cst), (bpp,))
                self.cp("act", qb[:, half * 512:(half + 1) * 512], pp[:, :], (bpp,), (b_qb,))
            for pg in range(NPG):
                x = pg % NKB
                self.P.dma("pool", lambda h, x=x, pg=pg: h.indirect_dma_start(
                    kpg[x][:], None, ck, bass.IndirectOffsetOnAxis(ap=idx[:, pg:pg + 1], axis=0)), (b_idx,), (b_kpg[x],))
                y = pg % 2
                self.tt("dve", prod[y][:], kpg[x][:], qb[:], ALU.mult, (b_kpg[x], b_qb), (b_prod[y],))
                self.P.op("dve", lambda h, y=y, pg=pg: h.tensor_reduce(
                    z[:, pg * 16:(pg + 1) * 16], prod[y][:].rearrange("p (h d) -> p h d", d=64), mybir.AxisListType.X, ALU.add),
                    (b_prod[y],), (b_z,))
            self.stt(z[:], z[:], 0.125, bsbr[:], ALU.mult, ALU.add, (b_z, b_bd), (b_z,))
            self.act(e[:], z[:], AF.Exp, (b_z,), (b_e,))
            self.act(spb[:], e[:], AF.Ln, (b_e,), (b_spb,), bias=1.0)
            bts = []
            for half in range(2):
                BT, bBT = self.banks[half], self.bbuf[half]
                self.mm(BT[:, :], ntri, spb[:, half * 512:(half + 1) * 512], True, True, (b_spb, self.b_cst), (bBT,))
                bts.append((BT, bBT))
            for half in range(2):
                T, bT = self.banks[2 + half], self.bbuf[2 + half]
                self.mm(T[:, :], nones, spb[:, half * 512:(half + 1) * 512], True, True, (b_spb, self.b_cst), (bT,))
                lo = half * 512
                if half == 0:
                    self.cp("dve", ca[:, 0:496], T[:, 16:512], (bT,), (b_ca,))
                else:
                    self.cp("dve", ca[:, 496:512], T[:, 0:16], (bT,), (b_ca,))
                    self.cp("dve", ca[:, 512:1008], T[:, 16:512], (bT,), (b_ca,))
            self.memset("dve", ca[:, 1008:1024], 0.0, (b_ca,))
            a, ba, b2, bb2 = ca, b_ca, cb, b_cb
            for step in (1, 2, 4, 8, 16, 32):
                n = (NPG - step) * 16
                self.tt("dve", b2[:, 0:n], a[:, 0:n], a[:, step * 16:step * 16 + n], ALU.add, (ba,), (bb2,))
                self.cp("pool", b2[:, n:1024], a[:, n:1024], (ba,), (bb2,))
                a, ba, b2, bb2 = b2, bb2, a, ba
            for half in range(2):
                BT, bBT = bts[half]
                self.tt("dve", b2[:, half * 512:(half + 1) * 512], BT[:, :], a[:, half * 512:(half + 1) * 512], ALU.add, (bBT, ba), (bb2,))
            self.act(b2[:], b2[:], AF.Exp, (bb2,), (bb2,))
            self.tt("dve", wt[:], e[:], b2[:], ALU.mult, (b_e, bb2), (b_wt,))
            O0, bO0 = self.banks[6], self.bbuf[6]
            O1, bO1 = self.banks[7], self.bbuf[7]
            for pg in range(NPG):
                x = pg % NKB
                self.P.dma("pool", lambda h, x=x, pg=pg: h.indirect_dma_start(
                    vpg[x][:], None, cv, bass.IndirectOffsetOnAxis(ap=idx[:, pg:pg + 1], axis=0)), (b_idx,), (b_vpg[x],))
                self.mm(O0[0:16, :], wt[:, pg * 16:(pg + 1) * 16], vpg[x][:, 0:512], pg == 0, pg == NPG - 1, (b_wt, b_vpg[x]), (bO0,))
                self.mm(O1[0:16, :], wt[:, pg * 16:(pg + 1) * 16], vpg[x][:, 512:1024], pg == 0, pg == NPG - 1, (b_wt, b_vpg[x]), (bO1,))
            self.tt("dve", osb[:, 0:512], O0[0:16, :], bd[:, 0:512], ALU.mult, (bO0, b_bd), (b_osb,))
            self.tt("dve", osb[:, 512:1024], O1[0:16, :], bd[:, 512:1024], ALU.mult, (bO1, b_bd), (b_osb,))
            pc, bpc = self.ps("o", [4, 5])
            for jj in range(8):
                self.mm(pc[:, jj:jj + 1], osb[0:16, jj * 128:(jj + 1) * 128], cst[0:16, C_ONES:C_ONES + 1], True, True, (b_osb, self.b_cst), (bpc,))
            self.cp("dve", oT[:, :, NP + s_], pc[:, 0:8], (bpc,), tuple(b_oT))

    def mixer_gla(self, src, b_src, dst, b_dst, w, io):
        cst, cstb, pv = self.cst, self.cstb, self.pv
        QS = 128 ** -0.5
        with contextlib.ExitStack() as st:
            GIN = 3088
            win = self.sb("g_win", [128, 8, GIN], BF16, st)
            b_win = Buf("win")
            self.load_w(win, b_win, w["gla_in"], D, GIN, mchunk=1024)
            wo = self.sb("g_wo", [128, 8, D], BF16, st)
            b_wo = Buf("wo")
            self.load_w(wo, b_wo, w["gla_out"], D, D)
            wgu = self.sb("g_wgu", [16, 512], BF16, st)
            b_wgu = Buf()
            self.dma("pool", wgu[:], w["gla_gate_up"][:, :], (), (b_wgu,))
            negb = self.sb("g_negb", [128, 4], F32, st)
            b_loc = Buf("gloc")
            self.ts("dve", negb[:], pv[:, PV_BG:PV_BG + 4], -1.0, None, ALU.mult, None, (self.b_pv,), (b_loc,))
            minc = self.sb("g_minc", [128, 128], F32, st)
            self.tt("dve", minc[:], cst[:, C_MASK_S:C_MASK_S + 128], cst[:, C_IDENT:C_IDENT + 128], ALU.add, (self.b_cst,), (b_loc,))
            identb = self.sb("g_identb", [128, 128], BF16, st)
            self.cp("dve", identb[:], cst[:, C_IDENT:C_IDENT + 128], (self.b_cst,), (b_loc,))
            o256 = self.sb("g_o256", [128, 128], BF16, st)
            self.memset("dve", o256[:], 1.0 / 256, (b_loc,))
            rmask = self.sb("g_rmask", [128, 512], F32, st)
            self.memset("dve", rmask[:], 1.0, (b_loc,))
            self.memset("dve", rmask[:].rearrange("p (c i) -> p c i", i=128)[:, :, 0:1], 0.0, (b_loc,))
            self_f = self.sb("g_self", [NS, NS * 128], F32, st)
            self.cp("dve", self_f[:], cstb[0:NS, C_SEL:C_SEL + NS * 128], (self.b_cst,), (b_loc,))
            S = [self.sb(f"g_S{h}", [128, 256], F32, st) for h in range(4)]
            Sb = [self.sb(f"g_Sb{h}", [128, 256], BF16, st) for h in range(4)]
            b_S = [Buf() for _ in range(4)]
            b_Sb = [Buf() for _ in range(4)]
            for h in range(4):
                self.memset("pool", S[h][:], 0.0, (b_S[h],))
                self.memset("pool", Sb[h][:], 0.0, (b_Sb[h],))
            xb = self.sb("g_xb", [128, 8, 512], BF16, st)
            xts = [self.sb(f"g_xt{i}", [128, 8, 512], F32, st) for i in range(2)]
            b_xb = Buf("xb")
            b_xts = [Buf("xt0"), Buf("xt1")]
            vtk = self.sb("g_vtk", [128, 4, D], BF16, st)
            b_vtk = Buf("vtk")
            glT = self.sb("g_glT", [16, 512], BF16, st)
            b_glT = Buf()
            f32t = {}
            for nm in ("e1", "sp", "Bp", "eb", "enb", "ebl"):
                f32t[nm] = (self.sb("g_" + nm, [128, 512], F32, st), Buf(nm))
            nbl = self.sb("g_nbl", [128, 4], F32, st)
            elast = self.sb("g_elast", [128, 4], F32, st)
            b_nbl, b_elast = Buf(), Buf()
            qg = self.sb("g_qg", [128, 512], BF16, st)
            kg = self.sb("g_kg", [128, 512], BF16, st)
            kl = self.sb("g_kl", [128, 512], BF16, st)
            b_qg, b_kg, b_kl = Buf(), Buf(), Buf()
            klt = [self.sb(f"g_klt{i}", [128, 128], BF16, st) for i in range(2)]
            b_klt = [Buf(), Buf()]
            sc = [self.sb(f"g_sc{i}", [128, 128], BF16, st) for i in range(2)]
            b_sc = [Buf(), Buf()]
            oraw = self.sb("g_oraw", [128, 8, 512], F32, st)
            b_oraw = [Buf() for _ in range(8)]
            sq = [self.sb(f"g_sq{i}", [128, 512], BF16, st) for i in range(2)]
            b_sq = [Buf(), Buf()]
            rstd = self.sb("g_rstd", [128, 512], F32, st)
            b_rstd = Buf()
            sr = [self.sb(f"g_sr{i}", [128, 512], F32, st) for i in range(2)]
            b_sr = [Buf(), Buf()]
            oTt = self.sb("g_oT", [128, 8, 512], BF16, st)
            b_oTt = Buf("oTt")
            qs_ = self.sb("g_qs", [128, 4, NS], F32, st)
            ks_ = self.sb("g_ks", [128, 4, NS], F32, st)
            as_ = self.sb("g_as", [128, 4, NS], F32, st)
            qa_ = self.sb("g_qa", [128, 4, NS], BF16, st)
            qk_ = self.sb("g_qk", [128, 4, NS], BF16, st)
            vT_ = self.sb("g_vT", [128, 8, NS], F32, st)
            vrow = self.sb("g_vrow", [NS, D], F32, st)
            b_smp = Buf("smp")
            S0 = [self.sb(f"g_S0{i}", [128, 256], F32, st) for i in range(2)]
            S0b = [self.sb(f"g_S0b{i}", [128, 256], BF16, st) for i in range(2)]
            b_S0 = [Buf(), Buf()]
            b_S0b = [Buf(), Buf()]
            Sn = [self.sb(f"g_Sn{i}", [128, 256], F32, st) for i in range(2)]
            b_Sn = [Buf(), Buf()]
            tmps = [self.ln_tmp(st), self.ln_tmp(st)]
            self.dma("pool", xb[:, :, :512], src[:, 0:512].rearrange("(k p) n -> p k n", p=128), (b_src,), (b_xb,))

            def proj(mcol, N, M=128):
                pp, bpp = self.ps("p", [0, 1])
                for k in range(8):
                    self.mm(pp[0:M, :N], win[:, k, mcol:mcol + M], xb[:, k, :N], k == 0, k == 7, (b_win, b_xb), (bpp,))
                return pp, bpp

            for ti, (c0, N) in enumerate(TILES):
                sample = N == NS
                sv = src[:, c0:c0 + N].rearrange("(k p) n -> p k n", p=128)
                xt, b_xt, tmp = xts[ti % 2], b_xts[ti % 2], tmps[ti % 2]
                self.dma("sp", xt[:, :, :N], sv, (b_src,), (b_xt,))
                nblk = 1 if sample else 4
                for blk in range(nblk):
                    M = NS if sample else 128
                    for half in range(2):
                        pp, bpp = self.ps("p", [0, 1])
                        for k in range(8):
                            self.mm(pp[0:M, :], xb[:, k, blk * 128:blk * 128 + M], win[:, k, 1024 + half * 512:1024 + (half + 1) * 512],
                                    k == 0, k == 7, (b_win, b_xb), (bpp,))
                        if sample:
                            self.cp("act", vrow[:, half * 512:(half + 1) * 512], pp[0:NS, :], (bpp,), (b_smp,))
                        else:
                            self.cp("act" if half else "dve", vtk[:, blk, half * 512:(half + 1) * 512], pp[:, :], (bpp,), (b_vtk,))
                pp, bpp = proj(3072, N, 16)
                self.cp("act", glT[:, :N], pp[0:16, :N], (bpp,), (b_glT,))
                for h in range(4):
                    gt, bgt = self.ps("p", [0, 1])
                    self.mm(gt[:, :N], wgu[0:16, h * 128:(h + 1) * 128], glT[0:16, :N], True, True, (b_wgu, b_glT), (bgt,))
                    e1, b_e1 = f32t["e1"]
                    sp, b_sp = f32t["sp"]
                    self.act(e1[:, :N], gt[:, :N], AF.Exp, (bgt, b_loc), (b_e1,), bias=negb[:, h:h + 1], scale=-1.0)
                    self.act(sp[:, :N], e1[:, :N], AF.Ln, (b_e1,), (b_sp,), bias=1.0)
                    if sample:
                        self.act(as_[:, h, :], sp[:, :N], AF.Exp, (b_sp,), (b_smp,), scale=-1.0 / 16)
                        pq, bpq = proj(h * 128, N)
                        self.ts("dve", qs_[:, h, :], pq[:, :N], QS, None, ALU.mult, None, (bpq,), (b_smp,))
                        pk, bpk = proj(512 + h * 128, N)
                        self.cp("act", ks_[:, h, :], pk[:, :N], (bpk,), (b_smp,))
                        self.tt("dve", qa_[:, h, :], qs_[:, h, :], as_[:, h, :], ALU.mult, (b_smp,), (b_smp,))
                        self.tt("dve", qk_[:, h, :], qs_[:, h, :], ks_[:, h, :], ALU.mult, (b_smp,), (b_smp,))
                        for m in range(2):
                            pv_, bpv_ = proj(1024 + h * 256 + m * 128, N)
                            self.cp("act", vT_[:, h * 2 + m, :], pv_[:, :N], (bpv_,), (b_smp,))
                        pqk, bpqk = self.banks[7], self.bbuf[7]
                        self.mm(pqk[:, 0:NS], cstb[:, C_NONES:C_NONES + 128], qk_[:, h, :], True, True, (b_smp, self.b_cst), (bpqk,))
                        for m in range(2):
                            oacc, boacc = self.banks[4 + m], self.bbuf[4 + m]
                            for s_ in range(NS):
                                y = (h * NS + s_) % 2
                                if m == 0:
                                    self.dma("sp", S0[y][:], io["state_gla"][s_, h], (), (b_S0[y],))
                                    self.dma("pool", S0b[y][:], io["state_gla"][s_, h], (), (b_S0b[y],))
                                else:
                                    self.dma("pool", S0b[y][:], io["state_gla"][s_, h], (), (b_S0b[y],))
                                self.mm(oacc[:, s_:s_ + 1], S0b[y][:, m * 128:(m + 1) * 128], qa_[:, h, s_:s_ + 1], True, True,
                                        (b_S0b[y], b_smp), (boacc,))
                                if m == 0:
                                    pvb, bpvb = self.ps("s", [2, 3])
                                    self.mm(pvb[:, 0:256], self_f[0:NS, s_ * 128:(s_ + 1) * 128], vrow[0:NS, h * 256:(h + 1) * 256], True, True,
                                            (b_loc, b_smp), (bpvb,))
                                    self.ts("dve", Sn[y][:], pvb[:, 0:256], ks_[:, h, s_:s_ + 1], None, ALU.mult, None, (bpvb, b_smp), (b_Sn[y],))
                                    self.stt(Sn[y][:], S0[y][:], as_[:, h, s_:s_ + 1], Sn[y][:], ALU.mult, ALU.add, (b_S0[y], b_smp, b_Sn[y]), (b_Sn[y],))
                                    self.dma("sp", io["gla_s"][s_, h], Sn[y][:], (b_Sn[y],), (Buf(),))
                            self.tt("dve", oraw[:, h * 2 + m, :N], pqk[:, 0:NS], vT_[:, h * 2 + m, :], ALU.mult, (bpqk, b_smp), (b_oraw[h * 2 + m],))
                            self.tt("dve", oraw[:, h * 2 + m, :N], oacc[:, 0:NS], oraw[:, h * 2 + m, :N], ALU.subtract, (boacc, b_oraw[h * 2 + m]), (b_oraw[h * 2 + m],))
                            i2 = m % 2
                            self.act(sq[i2][:, :N], oraw[:, h * 2 + m, :N], AF.Square, (b_oraw[h * 2 + m],), (b_sq[i2],))
                            pms, bpms = self.banks[6], self.bbuf[6]
                            self.mm(pms[:, :N], o256[:], sq[i2][:, :N], m == 0, m == 1, (b_sq[i2], b_loc), (bpms,))
                    else:
                        Bp, b_Bp = f32t["Bp"]
                        eb, b_eb = f32t["eb"]
                        enb, b_enb = f32t["enb"]
                        ebl, b_ebl = f32t["ebl"]
                        self.P.op("dve", lambda hh, Bp=Bp, sp=sp: hh.tensor_tensor_scan(Bp[:, :], rmask[:, :], sp[:, :], 0.0, ALU.mult, ALU.add),
                                  (b_sp, b_loc), (b_Bp,))
                        self.act(eb[:, :], Bp[:, :], AF.Exp, (b_Bp,), (b_eb,), scale=-1.0 / 16)
                        self.act(enb[:, :], Bp[:, :], AF.Exp, (b_Bp,), (b_enb,), scale=1.0 / 16)
                        self.ts("dve", nbl[:, :], Bp[:, :].rearrange("p (c i) -> p c i", i=128)[:, :, 127], -1.0 / 16, None, ALU.mult, None, (b_Bp,), (b_nbl,))
                        self.act(elast[:, :], nbl[:, :], AF.Exp, (b_nbl,), (b_elast,))
                        for c in range(4):
                            self.act(ebl[:, c * 128:(c + 1) * 128], Bp[:, c * 128:(c + 1) * 128], AF.Exp, (b_Bp, b_nbl), (b_ebl,),
                                     bias=nbl[:, c:c + 1], scale=1.0 / 16)
                        pq, bpq = proj(h * 128, N)
                        self.stt(qg[:, :], pq[:, :], QS, eb[:, :], ALU.mult, ALU.mult, (bpq, b_eb), (b_qg,))
                        pk, bpk = proj(512 + h * 128, N)
                        self.tt("dve", kg[:, :], pk[:, :], enb[:, :], ALU.mult, (bpk, b_enb), (b_kg,))
                        self.tt("dve", kl[:, :], pk[:, :], ebl[:, :], ALU.mult, (bpk, b_ebl), (b_kl,))
                        oacc = [(self.banks[4], self.bbuf[4]), (self.banks[5], self.bbuf[5])]
                        for c in range(4):
                            cs_ = slice(c * 128, (c + 1) * 128)
                            i2 = c % 2
                            ptb, bptb = self.ps("s", [2, 3])
                            ptv = ptb[:, 0:64].bitcast(BF16)
                            self.P.op("pe", lambda hh, ptv=ptv, cs_=cs_: hh.transpose(ptv, kl[:, cs_], identb[:, :]), (b_kl, b_loc), (bptb,))
                            self.cp("act", klt[i2][:], ptv, (bptb,), (b_klt[i2],))
                            psc, bpsc = self.ps("s", [2, 3])
                            self.mm(psc[:, 0:128], kg[:, cs_], qg[:, cs_], True, True, (b_kg, b_qg), (bpsc,))
                            self.tt("dve", sc[i2][:], psc[:, 0:128], minc[:], ALU.mult, (bpsc, b_loc), (b_sc[i2],))
                            for m in range(2):
                                ob, bob = oacc[m]
                                self.mm(ob[:, cs_], vtk[:, c, h * 256 + m * 128:h * 256 + (m + 1) * 128], sc[i2][:], True, False, (b_vtk, b_sc[i2]), (bob,))
                                self.mm(ob[:, cs_], Sb[h][:, m * 128:(m + 1) * 128], qg[:, cs_], False, True, (b_Sb[h], b_qg), (bob,))
                            pu, bpu = self.ps("s", [2, 3])
                            self.mm(pu[:, 0:256], klt[i2][:], vtk[:, c, h * 256:(h + 1) * 256], True, True, (b_klt[i2], b_vtk), (bpu,))
                            self.stt(S[h][:], S[h][:], elast[:, c:c + 1], pu[:, 0:256], ALU.mult, ALU.add, (b_S[h], b_elast, bpu), (b_S[h],))
                            self.cp("pool", Sb[h][:], S[h][:], (b_S[h],), (b_Sb[h],))
                        for m in range(2):
                            ob, bob = oacc[m]
                            self.cp("act", oraw[:, h * 2 + m, :], ob[:, :], (bob,), (b_oraw[h * 2 + m],))
                            i2 = m % 2
                            self.act(sq[i2][:, :], oraw[:, h * 2 + m, :], AF.Square, (b_oraw[h * 2 + m],), (b_sq[i2],))
                            pms, bpms = self.banks[6], self.bbuf[6]
                            self.mm(pms[:, :N], o256[:], sq[i2][:, :N], m == 0, m == 1, (b_sq[i2], b_loc), (bpms,))
                    pms, bpms = self.banks[6], self.bbuf[6]
                    self.act(rstd[:, :N], pms[:, :N], AF.Sqrt, (bpms, self.b_pv), (b_rstd,), bias=pv[:, PV_EPS:PV_EPS + 1])
                    self.P.op("dve", lambda hh, N=N: hh.reciprocal(rstd[:, :N], rstd[:, :N]), (b_rstd,), (b_rstd,))
                    for m in range(2):
                        jx = h * 2 + m
                        pr, bpr = proj(2048 + jx * 128, N)
                        i2 = m % 2
                        self.act(sr[i2][:, :N], pr[:, :N], AF.Silu, (bpr,), (b_sr[i2],))
                        self.tt("dve", oraw[:, jx, :N], oraw[:, jx, :N], rstd[:, :N], ALU.mult, (b_oraw[jx], b_rstd), (b_oraw[jx],))
                        self.stt(oTt[:, jx, :N], oraw[:, jx, :N], pv[:, PV_GN + m:PV_GN + m + 1], sr[i2][:, :N], ALU.mult, ALU.mult,
                                 (b_oraw[jx], b_sr[i2], self.b_pv), (b_oTt,))
                if ti + 1 < len(TILES):
                    c0n, Nn = TILES[ti + 1]
                    self.dma("pool", xb[:, :, :Nn], src[:, c0n:c0n + Nn].rearrange("(k p) n -> p k n", p=128), (b_src,), (b_xb,))
                for m in range(8):
                    po, bpo = self.ps("o", [4, 5])
                    for k in range(8):
                        self.mm(po[:, :N], wo[:, k, m * 128:(m + 1) * 128], oTt[:, k, :N], k == 0, k == 7, (b_wo, b_oTt), (bpo,))
                    self.stt(xt[:, m, :N], xt[:, m, :N], DN_ALPHA, po[:, :N], ALU.mult, ALU.add, (b_xt, bpo), (b_xt,))
                self.post_ln(st, xt, b_xt, N, 4, dst, b_dst, c0, tmp)
                if ti == 7:
                    for h in range(4):
                        self.dma("sp", io["gla_p"][h], S[h][:], (b_S[h],), (Buf(),))
        self.P.barrier()

    def mixer_dsw(self, src, b_src, dst, b_dst, w, io):
        cst, cstb, pv = self.cst, self.cstb, self.pv
        qT3 = self.scratch("dsw_qT", [1536, NP], BF16)
        kT3 = self.scratch("dsw_kT", [1536, NP], BF16)
        v3 = self.scratch("dsw_v", [3, NP, 512], BF16)
        qkvs = self.scratch("dsw_qkvs", [NS, 4608], F32)
        b_qT3, b_kT3, b_v3, b_qkvs = Buf(), Buf(), Buf(), Buf()
        WG = (128, 512, 2048)
        DG = (1, 4, 16)
        ENT = [(0, 0), (0, 1)] + [(1, j) for j in range(5)] + [(2, j) for j in range(17)]
        for g in range(3):
            for s_ in range(NS):
                self.dma("sp", io[f"dsw_s{g}"][s_, 0:WG[g] - 1, :], io[f"cache_dsw{g}"][s_, 1:WG[g], :], (), (Buf(),))
        with contextlib.ExitStack() as sl:
            oT = self.sb("d_oT", [128, 4, NTOK], BF16, sl)
            b_oT = [Buf(f"oT{j}") for j in range(4)]
            with contextlib.ExitStack() as st:
                wq = self.sb("d_wq", [128, 8, 4608], BF16, st)
                b_wq = Buf("wq")
                self.load_w(wq, b_wq, w["dsw_qkv"], D, 4608, mchunk=1536)
                xb = self.sb("d_xb", [128, 8, 512], BF16, st)
                b_xb = Buf("xb")
                stg = [self.sb(f"d_stg{i}", [128, 12, 512], BF16, st) for i in range(2)]
                b_stg = [Buf(), Buf()]
                stf = [self.sb(f"d_stf{i}", [128, D], F32, st) for i in range(2)]
                b_stf = [Buf(), Buf()]
                stv = [self.sb(f"d_stv{i}", [128, 512], BF16, st) for i in range(2)]
                b_stv = [Buf(), Buf()]
                sts = self.sb("d_sts", [NS, 4608], F32, st)
                b_sts = Buf()
                cnt = 0
                for ti, (c0, N) in enumerate(TILES):
                    sample = N == NS
                    sv = src[:, c0:c0 + N].rearrange("(k p) n -> p k n", p=128)
                    self.dma("pool", xb[:, :, :N], sv, (b_src,), (b_xb,))
                    if sample:
                        for c9 in range(9):
                            pp, bpp = self.ps("o", [4, 5])
                            for k in range(8):
                                self.mm(pp[0:NS, :], xb[:, k, 0:NS], wq[:, k, c9 * 512:(c9 + 1) * 512], k == 0, k == 7, (b_wq, b_xb), (bpp,))
                            self.cp("act", sts[:, c9 * 512:(c9 + 1) * 512], pp[0:NS, :], (bpp,), (b_sts,))
                        self.dma("sp", qkvs[:, :], sts[:], (b_sts,), (b_qkvs,))
                        continue
                    for qk in range(2):
                        for g in range(3):
                            for jj in range(4):
                                pp, bpp = self.ps("a", [0, 1, 2, 3])
                                col = g * 1536 + qk * 512 + jj * 128
                                for k in range(8):
                                    self.mm(pp[:, :N], wq[:, k, col:col + 128], xb[:, k, :N], k == 0, k == 7, (b_wq, b_xb), (bpp,))
                                self.cp("act" if jj % 2 else "dve", stg[qk][:, g * 4 + jj, :N], pp[:, :N], (bpp,), (b_stg[qk],))
                        dd, bd = (qT3, b_qT3) if qk == 0 else (kT3, b_kT3)
                        self.dma("sp", dd[:, c0:c0 + N].rearrange("(m p) n -> p m n", p=128), stg[qk][:, :, :N], (b_stg[qk],), (bd,))
                    for blk in range(4):
                        bi = ti * 4 + blk
                        for g in range(3):
                            tail0 = 32 - WG[g] // 128
                            intail = bi >= tail0
                            i2 = cnt % 2
                            cnt += 1
                            for half in ((0, 1) if intail else (1,)):
                                pp, bpp = self.ps("o", [4, 5])
                                off = g * 1536 + 512 + half * 512
                                for k in range(8):
                                    self.mm(pp[:, :], xb[:, k, blk * 128:(blk + 1) * 128], wq[:, k, off:off + 512], k == 0, k == 7, (b_wq, b_xb), (bpp,))
                                if intail:
                                    self.cp("act", stf[i2][:, half * 512:(half + 1) * 512], pp[:, :], (bpp,), (b_stf[i2],))
                                if half == 1:
                                    self.cp("dve", stv[i2][:], pp[:, :], (bpp,), (b_stv[i2],))
                            if intail:
                                r0 = (bi - tail0) * 128
                                self.dma("sp", io[f"dsw_p{g}"][r0:r0 + 128, :], stf[i2][:], (b_stf[i2],), (Buf(),))
                            self.dma("sp", v3[g, bi * 128:(bi + 1) * 128, :], stv[i2][:], (b_stv[i2],), (b_v3,))
            self.P.barrier()
            with contextlib.ExitStack() as st:
              if "dsp" not in SKIP:
                Qj = self.sb("d_Qj", [128, 3, NP], BF16, st)
                Kj = self.sb("d_Kj", [128, 3, NP], BF16, st)
                Va = [self.sb(f"d_Va{i}", [128, 32, 3, 128], BF16, st) for i in range(2)]
                b_Qj, b_Kj, b_Va = Buf(), Buf(), Buf()
                Bm = [self.sb(f"d_Bm{i}", [128, 24 * 128], F32, st) for i in range(2)]
                b_Bm = [Buf(), Buf()]
                NR = 3
                u_t = [self.sb(f"d_u{i}", [128, 512], F32, st) for i in range(NR)]
                p_t = [self.sb(f"d_p{i}", [128, 512], BF16, st) for i in range(NR)]
                b_u = [Buf() for _ in range(NR)]
                b_p = [Buf() for _ in range(NR)]
                acc = [self.sb(f"d_acc{i}", [128, NP], F32, st) for i in range(2)]
                b_acc = [Buf(), Buf()]
                rD = self.sb("d_rD", [128, 512], F32, st)
                b_rD = Buf()
                sel = [self.sb(f"d_sel{i}", [128, 128], F32, st) for i in range(2)]
                b_sel = Buf()
                self.memset("pool", Va[0][:, :, :, 64:128], 0.0, (b_Va,))
                self.memset("pool", Va[0][:, :, :, 64:65], 1.0, (b_Va,))
                self.memset("pool", Va[1][:, :, :, 0:64], 0.0, (b_Va,))
                self.memset("pool", Va[1][:, :, :, 0:1], 1.0, (b_Va,))
                self.memset("pool", sel[0][:], 0.0, (b_sel,))
                self.memset("pool", sel[0][64:65, 0:64], 1.0, (b_sel,))
                self.memset("pool", sel[1][:], 0.0, (b_sel,))
                self.memset("pool", sel[1][0:1, 64:128], 1.0, (b_sel,))
                q3 = qT3.rearrange("(g m) n -> m g n", g=3)
                k3 = kT3.rearrange("(g m) n -> m g n", g=3)
                NJ_G = (2, 5, 17)
                E0_G = (0, 2, 7)
                cn = 0
                for jj in range(4):
                    self.dma("sp", Qj[:], q3[jj * 128:(jj + 1) * 128, :, :], (b_qT3,), (b_Qj,))
                    self.dma("sp", Kj[:], k3[jj * 128:(jj + 1) * 128, :, :], (b_kT3,), (b_Kj,))
                    for g in range(3):
                        vv = v3[g].rearrange("(b p) c -> p b c", p=128)
                        self.dma("sp", Va[0][:, :, g, 0:64], vv[:, :, jj * 128:jj * 128 + 64], (b_v3,), (b_Va,))
                        self.dma("sp", Va[1][:, :, g, 64:128], vv[:, :, jj * 128 + 64:(jj + 1) * 128], (b_v3,), (b_Va,))
                    for hh in range(2):
                        h = jj * 2 + hh
                        self.dma("sp", Bm[hh][:], io["dswB"][h], (), (b_Bm[hh],))
                    for hh in range(2):
                        p0 = 64 * hh
                        self.memset("pool", acc[hh][:], 0.0, (b_acc[hh],))
                        chunks = []
                        for kb in range(32):
                            for g in range(3):
                                js = [j for j in range(NJ_G[g]) if kb + j <= 31]
                                for a in range(0, len(js), 4):
                                    ch = js[a:a + 4]
                                    chunks.append((kb, g, len(ch) * 128, (kb + ch[0]) * 128, E0_G[g] + ch[0], cn))
                                    cn += 1

                        def stA(c):
                            kb, g, nco, q0, e0, ci = c
                            x = ci % NR
                            S, bS = self.banks[ci % 3], self.bbuf[ci % 3]
                            self.mm(S[:, :nco], Kj[p0:p0 + 64, g, kb * 128:(kb + 1) * 128], Qj[p0:p0 + 64, g, q0:q0 + nco],
                                    True, True, (b_Kj, b_Qj), (bS,))
                            self.stt(u_t[x][:, :nco], S[:, :nco], 0.125, Bm[hh][:, e0 * 128:e0 * 128 + nco], ALU.mult, ALU.add,
                                     (bS, b_Bm[hh]), (b_u[x],))
                            self.act(p_t[x][:, :nco], u_t[x][:, :nco], AF.Exp, (b_u[x],), (b_p[x],))

                        def stB(c):
                            kb, g, nco, q0, e0, ci = c
                            x = ci % NR
                            PS, bPS = self.banks[3 + ci % 3], self.bbuf[3 + ci % 3]
                            self.mm(PS[:, :nco], Va[hh][:, kb, g, :], p_t[x][:, :nco], True, True, (b_Va, b_p[x]), (bPS,))
                            self.tt("dve", acc[hh][:, q0:q0 + nco], PS[:, :nco], acc[hh][:, q0:q0 + nco], ALU.add,
                                    (bPS, b_acc[hh]), (b_acc[hh],))

                        SK = 2
                        for i in range(len(chunks) + SK):
                            if i < len(chunks):
                                stA(chunks[i])
                            if i - SK >= 0:
                                stB(chunks[i - SK])
                        for t in range(8):
                            Dn, bDn = self.banks[6 + t % 2], self.bbuf[6 + t % 2]
                            self.mm(Dn[:, :], sel[hh][:, :], acc[hh][:, t * 512:(t + 1) * 512], True, True, (b_sel, b_acc[hh]), (bDn,))
                            self.P.op("dve", lambda hd, Dn=Dn, p0=p0: hd.reciprocal(rD[p0:p0 + 64, :], Dn[p0:p0 + 64, :]), (bDn,), (b_rD,))
                            self.tt("dve", oT[p0:p0 + 64, jj, t * 512:(t + 1) * 512], acc[hh][p0:p0 + 64, t * 512:(t + 1) * 512], rD[p0:p0 + 64, :], ALU.mult,
                                    (b_acc[hh], b_rD), (b_oT[jj],))
            self.P.barrier()
            with contextlib.ExitStack() as st:
              if "dss" not in SKIP:
                self.dsw_sample(st, io, oT, b_oT, qkvs, b_qkvs, WG, DG)
            self.P.barrier()
            with contextlib.ExitStack() as st:
                self.out_proj_ln(st, oT, b_oT, 4, w["dsw_out"], 512, src, b_src, dst, b_dst, 6)
        self.P.barrier()

    def dsw_sample(self, st, io, oT, b_oT, qkvs, b_qkvs, WG, DG):
        cst, cstb, pv = self.cst, self.cstb, self.pv
        qk = self.sb("ds_qkv", [NS, 4608], F32, st)
        qkb = self.sb("ds_qkvb", [NS, 4608], BF16, st)
        b_qk = Buf()
        self.dma("sp", qk[:], qkvs[:, :], (b_qkvs,), (b_qk,))
        self.dma("pool", qkb[:], qkvs[:, :], (b_qkvs,), (b_qk,))
        ab = self.sb("ds_ab", [128, 24], F32, st)
        bd8 = self.sb("ds_bd8", [8, 512], F32, st)
        b_c = Buf()
        self.dma("sp", ab[:], io["dswAB"][:, :], (), (b_c,))
        self.dma("sp", bd8[:], io["bd8"][:, :], (), (b_c,))
        kv = [self.sb(f"ds_kv{i}", [128, D], BF16, st) for i in range(2)]
        b_kv = [Buf(), Buf()]
        kn = [self.sb(f"ds_kn{i}", [1, D], BF16, st) for i in range(2)]
        b_kn = [Buf(), Buf()]
        qb = [self.sb(f"ds_qb{i}", [128, 512], BF16, st) for i in range(2)]
        b_qb = [Buf(), Buf()]
        prod = [self.sb(f"ds_pr{i}", [128, 512], F32, st) for i in range(2)]
        b_prod = [Buf(), Buf()]
        z = self.sb("ds_z", [128, 8], F32, st)
        zn = self.sb("ds_zn", [1, 8], F32, st)
        pw = [self.sb(f"ds_pw{i}", [128, 8], BF16, st) for i in range(2)]
        pn = [self.sb(f"ds_pn{i}", [1, 8], BF16, st) for i in range(2)]
        b_z, b_zn = Buf(), Buf()
        b_pw, b_pn = [Buf(), Buf()], [Buf(), Buf()]
        rden = self.sb("ds_rden", [8, 1], F32, st)
        osb = self.sb("ds_osb", [8, 512], F32, st)
        b_rden, b_osb = Buf(), Buf()
        for g in range(3):
            self.dma("sp", io[f"dsw_s{g}"][:, WG[g] - 1, :], qk[:, g * 1536 + 512:g * 1536 + 1536], (b_qk,), (Buf(),))
        cnt = 0
        for s_ in range(NS):
            num, bnum = self.banks[4], self.bbuf[4]
            den, bden = self.banks[5], self.bbuf[5]
            for g in range(3):
                x = cnt % 2
                cnt += 1
                srcv = io[f"cache_dsw{g}"][s_].rearrange("(r d) c -> r d c", d=DG[g])[:, 0, :]
                self.dma("pool", kv[x][:], srcv, (), (b_kv[x],))
                self.dma("pool", kn[x][:], qkvs[s_:s_ + 1, g * 1536 + 512:g * 1536 + 1536], (b_qkvs,), (b_kn[x],))
                pp, bpp = self.ps("s", [2, 3])
                self.mm(pp[:, :], cstb[0:NS, C_SEL + s_ * 128:C_SEL + (s_ + 1) * 128], qkb[:, g * 1536:g * 1536 + 512], True, True,
                        (b_qk, self.b_cst), (bpp,))
                self.cp("act", qb[x][:], pp[:, :], (bpp,), (b_qb[x],))
                self.tt("dve", prod[x][:], kv[x][:, 0:512], qb[x][:], ALU.mult, (b_kv[x], b_qb[x]), (b_prod[x],))
                self.P.op("dve", lambda h, x=x: h.tensor_reduce(z[:, :], prod[x][:].rearrange("p (h d) -> p h d", d=64), mybir.AxisListType.X, ALU.add),
                          (b_prod[x],), (b_z,))
                self.stt(z[:, :], z[:, :], 0.125, ab[:, g * 8:(g + 1) * 8], ALU.mult, ALU.add, (b_z, b_c), (b_z,))
                self.act(pw[x][:], z[:, :], AF.Exp, (b_z,), (b_pw[x],))
                self.tt("dve", prod[x][0:1, :], kn[x][0:1, 0:512], qb[x][0:1, :], ALU.mult, (b_kn[x], b_qb[x], b_prod[x]), (b_prod[x],))
                self.P.op("dve", lambda h, x=x: h.tensor_reduce(zn[:, :], prod[x][0:1, :].rearrange("p (h d) -> p h d", d=64), mybir.AxisListType.X, ALU.add),
                          (b_prod[x],), (b_zn,))
                self.act(pn[x][:], zn[:, :], AF.Exp, (b_zn,), (b_pn[x],), scale=0.125)
                self.mm(num[0:8, :], pw[x][:], kv[x][:, 512:1024], g == 0, False, (b_pw[x], b_kv[x]), (bnum,))
                self.mm(num[0:8, :], pn[x][0:1, :], kn[x][0:1, 512:1024], False, g == 2, (b_pn[x], b_kn[x]), (bnum,))
                self.mm(den[0:8, 0:1], pw[x][:], cstb[:, C_NONES:C_NONES + 1], g == 0, False, (b_pw[x], self.b_cst), (bden,))
                self.mm(den[0:8, 0:1], pn[x][0:1, :], cstb[0:1, C_NONES:C_NONES + 1], False, g == 2, (b_pn[x], self.b_cst), (bden,))
            self.P.op("dve", lambda h: h.reciprocal(rden[:, :], den[0:8, 0:1]), (bden,), (b_rden,))
            self.ts("dve", osb[:, :], num[0:8, :], rden[:, 0:1], -1.0, ALU.mult, ALU.mult, (bnum, b_rden), (b_osb,))
            self.tt("dve", osb[:, :], osb[:, :], bd8[:, :], ALU.mult, (b_osb, b_c), (b_osb,))
            pc, bpc = self.ps("s", [2, 3])
            for jj in range(4):
                self.mm(pc[:, jj:jj + 1], osb[0:8, jj * 128:(jj + 1) * 128], cst[0:8, C_ONES:C_ONES + 1], True, True, (b_osb, self.b_cst), (bpc,))
            self.cp("dve", oT[:, :, NP + s_], pc[:, 0:4], (bpc,), tuple(b_oT))

    def ffn(self, layer, src, b_src, dst, b_dst, w, ffn_state, ffn_out):
        with contextlib.ExitStack() as st:
            wup = self.sb("f_wup", [128, 8, 2 * DFF], BF16, st)
            wdn = self.sb("f_wdn", [128, NJ, D], BF16, st)
            b_wup, b_wdn = Buf("wup"), Buf("wdn")
            self.load_w(wup, b_wup, w["ffn_up"][layer], D, 2 * DFF)
            self.load_w(wdn, b_wdn, w["ffn_down"][layer], DFF, D)
            xt = self.sb("f_xt", [128, 8, 512], F32, st)
            xb = self.sb("f_xb", [128, 8, 512], BF16, st)
            b_xt, b_xb = Buf("xt"), Buf("xb")
            hb = self.sb("f_h", [128, NJ, 512], BF16, st)
            b_hb = Buf("h")
            gs = [self.sb(f"f_g{i}", [128, 516], F32, st) for i in range(2)]
            b_gs = [Buf(), Buf()]
            zs = [self.sb(f"f_z{i}", [128, 512], F32, st) for i in range(2)]
            b_zs = [Buf(), Buf()]
            halo = self.sb("f_halo", [128, NJ, 2], F32, st)
            b_halo = Buf("halo")
            gsm = self.sb("f_gsm", [128, NJ, 3, NS], F32, st)
            b_gsm = Buf("gsm")
            fo = self.sb("f_o", [128, NJ, 2 + 2 * NS], F32, st)
            b_fo = Buf()
            tmp = self.ln_tmp(st)
            ident = self.cst[:, C_IDENT:C_IDENT + 128]
            self.load_halo(st, ffn_state[layer], NJ, gsm, b_gsm)
            self.memset("dve", halo[:], 0.0, (b_halo,))
            wc0 = PV_FFW + layer * NJ * 3

            def up_part(N, sample, j0, j1):
                for j in range(j0, j1):
                    pg, bpg = self.ps("a", [0, 1, 2, 3])
                    pu, bpu = self.ps("a", [0, 1, 2, 3])
                    for (pp, bpp, off) in ((pg, bpg, 0), (pu, bpu, DFF)):
                        for k in range(8):
                            self.mm(pp[:, :N], wup[:, k, off + j * 128: off + (j + 1) * 128], xb[:, k, :N], k == 0, k == 7,
                                    (b_wup, b_xb), (bpp,))
                    i = j % 2
                    if sample:
                        self.cp("act", gsm[:, j, 2, :], pg[:, :N], (bpg,), (b_gsm,))
                        taps = [gsm[:, j, 0, :], gsm[:, j, 1, :], gsm[:, j, 2, :]]
                        rd = (b_gsm, self.b_pv)
                    else:
                        g = gs[i]
                        self.cp("pool", g[:, 0:2], halo[:, j, :], (b_halo,), (b_gs[i],))
                        self.cp("act", g[:, 2:2 + N], pg[:, :N], (bpg,), (b_gs[i],))
                        self.cp("pool", halo[:, j, :], g[:, N:N + 2], (b_gs[i],), (b_halo,))
                        taps = [g[:, 0:N], g[:, 1:1 + N], g[:, 2:2 + N]]
                        rd = (b_gs[i], self.b_pv)
                    self.conv3(zs[i][:, :N], taps, lambda t, j=j: self.pv[:, wc0 + j * 3 + t: wc0 + j * 3 + t + 1], rd, (b_zs[i],))
                    self.act(zs[i][:, :N], zs[i][:, :N], AF.Silu, (b_zs[i],), (b_zs[i],))
                    self.tt("dve", hb[:, j, :N], zs[i][:, :N], pu[:, :N], ALU.mult, (b_zs[i], bpu), (b_hb,))

            def tile_src(ti):
                c0, N = TILES[ti]
                return src[:, c0:c0 + N].rearrange("(k p) n -> p k n", p=128), c0, N

            sv0, _, N0 = tile_src(0)
            self.dma("pool", xb[:, :, :N0], sv0, (b_src,), (b_xb,))
            prev = None
            for ti in range(len(TILES)):
                sv, c0, N = tile_src(ti)
                sample = N == NS
                up_part(N, sample, 0, 8)
                if prev is not None:
                    self.post_ln(st, xt, b_xt, prev[1], layer * 2 + 1, dst, b_dst, prev[0], tmp)
                self.dma("sp", xt[:, :, :N], sv, (b_src,), (b_xt,))
                up_part(N, sample, 8, NJ)
                if ti + 1 < len(TILES):
                    svn, _, Nn = tile_src(ti + 1)
                    self.dma("pool", xb[:, :, :Nn], svn, (b_src,), (b_xb,))
                for m in range(8):
                    po, bpo = self.ps("o", [4, 5])
                    for k in range(NJ):
                        self.mm(po[:, :N], wdn[:, k, m * 128:(m + 1) * 128], hb[:, k, :N], k == 0, k == NJ - 1, (b_wdn, b_hb), (bpo,))
                    self.stt(xt[:, m, :N], xt[:, m, :N], DN_ALPHA, po[:, :N], ALU.mult, ALU.add, (b_xt, bpo), (b_xt,))
                prev = (c0, N)
            self.post_ln(st, xt, b_xt, prev[1], layer * 2 + 1, dst, b_dst, prev[0], tmp)
            self.cp("pool", fo[:, :, 0:2], halo[:], (b_halo,), (b_fo,))
            self.cp("pool", fo[:, :, 2:2 + 2 * NS].rearrange("p k (r s) -> p k r s", r=2), gsm[:, :, 1:3, :], (b_gsm,), (b_fo,))
            self.dma("sp", ffn_out[layer].rearrange("(k p) c -> p k c", p=128), fo[:], (b_fo,), (Buf(),))


PV_LNG = 0
PV_LNB = 64
PV_SCW = 128
PV_FFW = 152
PV_EPS = 416
PV_BSB = 420
PV_BG = 436
PV_GN = 440
NPV = 444

C_IDENT = 0
C_MASK_S = 128
C_ONES = 256
C_PCOL = 257
NCF = 260
C_ONES_S = 0
C_NTRI = 128
C_NONES = 256
C_ZERO = 384
C_SEL = 512
NCB = 1024


def make_pv(inputs):
    pv = np.zeros((128, NPV), np.float32)
    pv[:, PV_LNG:PV_LNG + 64] = inputs["ln_g"].reshape(8, 8, 128).transpose(2, 0, 1).reshape(128, 64)
    pv[:, PV_LNB:PV_LNB + 64] = inputs["ln_b"].reshape(8, 8, 128).transpose(2, 0, 1).reshape(128, 64)
    pv[:, PV_SCW:PV_SCW + 24] = inputs["w_sc_conv"].reshape(3, 8, 128).transpose(2, 1, 0).reshape(128, 24)
    pv[:, PV_FFW:PV_FFW + 264] = inputs["w_ffn_conv"].reshape(4, 3, NJ, 128).transpose(3, 0, 2, 1).reshape(128, 264)
    pv[:, PV_EPS] = LN_EPS
    pv[:, PV_BSB:PV_BSB + 16] = inputs["b_sb"][None, :]
    pv[:, PV_BG:PV_BG + 4] = inputs["b_gla_gate"].reshape(4, 128).T
    pv[:, PV_GN:PV_GN + 2] = inputs["g_gla_norm"].reshape(2, 128).T
    return pv


def make_cst():
    p = np.arange(128)
    cf = np.zeros((128, NCF), np.float32)
    cf[:, C_IDENT:C_IDENT + 128] = np.eye(128, dtype=np.float32)
    cf[:, C_MASK_S:C_MASK_S + 128] = (p[None, :] > p[:, None])
    cf[:, C_ONES] = 1.0
    cf[:, C_PCOL] = p
    cb = np.zeros((128, NCB), np.float32)
    cb[:, C_ONES_S:C_ONES_S + 128] = 1.0 / 1024
    cb[:, C_NTRI:C_NTRI + 128] = -1.0 * (p[:, None] >= p[None, :])
    cb[:, C_NONES:C_NONES + 128] = -1.0
    for q in range(NS):
        cb[q, C_SEL + q * 128:C_SEL + (q + 1) * 128] = 1.0
    bd = np.zeros((16, 1024), np.float32)
    for h in range(16):
        bd[h, h * 64:(h + 1) * 64] = 1.0
    return cf, cb, bd


def make_dsw_tables():
    WG = (128, 512, 2048)
    DG = (1, 4, 16)
    ENT = [(0, 0), (0, 1)] + [(1, j) for j in range(5)] + [(2, j) for j in range(17)]
    slopes = (2.0 ** (-8.0 * np.arange(1, 25, dtype=np.float32) / 24)).astype(np.float32).reshape(3, 8)
    p = np.arange(128)[:, None]
    c = np.arange(128)[None, :]
    B = np.full((8, 128, 24 * 128), -30000.0, np.float32)
    for ei, (g, j) in enumerate(ENT):
        dist = 128 * j + c - p
        valid = (dist >= 0) & (dist <= WG[g]) & (dist % DG[g] == 0)
        for h in range(8):
            blk = np.where(valid, -slopes[g, h] * dist.astype(np.float32), np.float32(-30000.0)).astype(np.float32)
            B[h, :, ei * 128:(ei + 1) * 128] = blk
    ab = np.zeros((128, 24), np.float32)
    for g in range(3):
        for h in range(8):
            ab[:, g * 8 + h] = -slopes[g, h] * DG[g] * (128 - np.arange(128)).astype(np.float32)
    bd8 = np.zeros((8, 512), np.float32)
    for h in range(8):
        bd8[h, h * 64:(h + 1) * 64] = 1.0
    return B, ab, bd8


_CACHE = {}


def get_nc(layers=(0, 1, 2, 3), dbg=False, n_pool=2560):
    key = (tuple(layers), dbg, n_pool)
    if key not in _CACHE:
        k = K(layers=layers, dbg=dbg, n_pool=n_pool)
        _CACHE[key] = (k.build(), k)
    return _CACHE[key]


def make_in_maps(inputs, layers):
    pv = make_pv(inputs)
    cf, cb, bd = make_cst()
    bsbr = np.ascontiguousarray(np.broadcast_to(np.tile(inputs["b_sb"], 64)[None, :], (128, 1024))).astype(np.float32)
    maps = []
    for c in range(N_CORES):
        s0 = c * NS
        if c in PROMPT_CORES:
            xp = inputs["x_prompt"][PROMPT_CORES.index(c)].T
        else:
            xp = np.zeros((D, NP), np.float32)
        xT0 = np.ascontiguousarray(np.concatenate([xp, inputs["x_sample"][s0:s0 + NS, 0].T], axis=1))
        m = {
            "xT0": xT0, "pv": pv, "cstf": cf, "cstb": cb, "bd16": bd, "bsbr": bsbr,
            "w_sc_in": inputs["w_sc_in"], "w_sc_out": inputs["w_sc_out"],
            "w_ffn_up": inputs["w_ffn_up"], "w_ffn_down": inputs["w_ffn_down"],
            "sc_cache": np.ascontiguousarray(inputs["cache_sc_conv"][s0:s0 + NS].reshape(NS * 2, D)),
            "ffn_state": np.ascontiguousarray(inputs["state_ffn_conv"][:, s0:s0 + NS].reshape(4, NS * 2, DFF)),
        }
        if 1 in layers:
            m["w_sb_qkv"] = inputs["w_sb_qkv"]
            m["w_sb_out"] = inputs["w_sb_out"]
            m["cache_sb_k"] = inputs["cache_sb_k"]
            m["cache_sb_v"] = inputs["cache_sb_v"]
            m["page_table"] = np.ascontiguousarray(inputs["page_table"][s0:s0 + NS])
        if 3 in layers:
            if "dswB" not in _CACHE:
                _CACHE["dswB"] = make_dsw_tables()
            B, ab, bd8 = _CACHE["dswB"]
            m["dswB"], m["dswAB"], m["bd8"] = B, ab, bd8
            m["w_dsw_qkv"] = inputs["w_dsw_qkv"]
            m["w_dsw_out"] = inputs["w_dsw_out"]
            for g in range(3):
                cg = inputs[f"cache_dsw_kv{g}"][s0:s0 + NS]
                m[f"cache_dsw{g}"] = np.ascontiguousarray(cg.reshape(NS, cg.shape[1], D))
        if 2 in layers:
            m["w_gla_in"] = inputs["w_gla_in"]
            m["w_gla_gate_up"] = inputs["w_gla_gate_up"]
            m["w_gla_out"] = inputs["w_gla_out"]
            m["state_gla"] = np.ascontiguousarray(inputs["state_gla"][s0:s0 + NS])
        maps.append(m)
    return maps


def run(inputs, layers=(0, 1, 2, 3), dbg=False, n_pool=2560, override=None):
    nc, k = get_nc(layers, dbg, n_pool)
    maps = make_in_maps(inputs, layers)
    if override:
        for m in maps:
            m.update(override)
    maps = [{n: m[n] for n in k.din} for m in maps]
    res = run_bass_kernel_spmd(nc, maps, core_ids=list(range(N_CORES)))
    return res.results


def kernel(**inputs):
    inputs = {k: np.asarray(v) for k, v in inputs.items()}
    res = run(inputs)
    B = 4
    f32 = np.float32
    y_p = np.stack([res[b]["yT"][:, :NP].T for b in PROMPT_CORES]).astype(f32)
    y_s = np.concatenate([res[c]["yT"][:, NP:].T for c in range(N_CORES)])[:, None, :].astype(f32)
    sc_p = np.stack([res[b]["sc_out"][:, 0:2].T for b in PROMPT_CORES]).astype(f32)
    sc_s = np.concatenate([res[c]["sc_out"][:, 2:].reshape(D, 2, NS).transpose(2, 1, 0) for c in range(N_CORES)]).astype(f32)
    sbk_p = np.stack([res[b]["sb_k_p"] for b in PROMPT_CORES]).reshape(B, NP, 16, 64)
    sbv_p = np.stack([res[b]["sb_v_p"] for b in PROMPT_CORES]).reshape(B, NP, 16, 64)
    sbk_s = np.concatenate([res[c]["sb_k_s"] for c in range(N_CORES)]).reshape(32, 1, 16, 64)
    sbv_s = np.concatenate([res[c]["sb_v_s"] for c in range(N_CORES)]).reshape(32, 1, 16, 64)
    gla_p = np.stack([res[b]["gla_p"] for b in PROMPT_CORES])
    gla_s = np.concatenate([res[c]["gla_s"] for c in range(N_CORES)])
    dsw_p = [np.stack([res[b][f"dsw_p{g}"] for b in PROMPT_CORES]).reshape(B, wg, 2, 8, 64) for g, wg in enumerate((128, 512, 2048))]
    dsw_s = [np.concatenate([res[c][f"dsw_s{g}"] for c in range(N_CORES)]).reshape(32, wg, 2, 8, 64) for g, wg in enumerate((128, 512, 2048))]
    ffn_p = np.stack([np.stack([res[b]["ffn_out"][l][:, 0:2].T for b in PROMPT_CORES]) for l in range(4)]).astype(f32)
    ffn_s = np.stack([np.concatenate([res[c]["ffn_out"][l][:, 2:].reshape(DFF, 2, NS).transpose(2, 1, 0) for c in range(N_CORES)])
                      for l in range(4)]).astype(f32)
    outs = (y_p, y_s, sc_p, sc_s, sbk_p, sbv_p, sbk_s, sbv_s, gla_p, gla_s,
            dsw_p[0], dsw_p[1], dsw_p[2], dsw_s[0], dsw_s[1], dsw_s[2], ffn_p, ffn_s)
    return tuple(np.ascontiguousarray(o, dtype=f32) for o in outs)
```

```python
import contextlib
import os as _os
import numpy as np
import concourse.bass as bass
import concourse.mybir as mybir
from concourse.bass_utils import run_bass_kernel_spmd

F32 = mybir.dt.float32
BF16 = mybir.dt.bfloat16
I32 = mybir.dt.int32
AF = mybir.ActivationFunctionType
ALU = mybir.AluOpType

D = 1024
NP = 4096
NS = 4
NTOK = NP + NS
DFF = 2816
NJ = DFF // 128
DN_ALPHA = (2.0 * 4) ** 0.25
LN_EPS = 1e-5
TILES = [(i * 512, 512) for i in range(8)] + [(NP, NS)]
N_CORES = 8
PROMPT_CORES = [0, 1, 4, 5]

SAME_ENGINE_SYNC = _os.environ.get("KSES", "1") == "1"
SKIP = _os.environ.get("KSKIP", "").split(",")
KSTOP = int(_os.environ.get("KSTOP", "1000000000"))


class Buf:
    __slots__ = ("name", "w", "r", "excl")

    def __init__(self, name="", excl=False):
        self.name = name
        self.excl = excl
        self.w = None
        self.r = {}


class _Eng:
    def __init__(self, name, handle, sem):
        self.name = name
        self.h = handle
        self.sem = sem
        self.count = 0
        self.ops = []
        self.known = {}
        self.slots = []
        self.slot_i = 0
        self.pending = []


class Prog:
    def __init__(self, nc, stack, n_dma_slots=12):
        self.nc = nc
        self.stack = stack
        self.E = {}
        for name, h in (("pe", nc.tensor), ("act", nc.scalar), ("dve", nc.vector), ("pool", nc.gpsimd), ("sp", nc.sync)):
            sem = stack.enter_context(nc.semaphore("s_" + name))
            self.E[name] = _Eng(name, h, sem)
        for name in ("sp", "pool", "act"):
            e = self.E[name]
            for i in range(n_dma_slots):
                e.slots.append([stack.enter_context(nc.semaphore(f"d_{name}{i}")), 0])
        self.nops = 0

    def _deps(self, reads, writes):
        waits = {}

        def add(ev):
            if ev is None:
                return
            s, v = ev
            if waits.get(s, (None, 0))[1] < v:
                waits[s] = (s, v)

        for b in reads:
            add(b.w)
            if b.excl:
                for s, v in b.r.items():
                    add((s, v))
        for b in writes:
            add(b.w)
            for s, v in b.r.items():
                add((s, v))
        return waits

    def _filter(self, e, waits, skip_own):
        out = []
        if e.pending:
            for (s, v) in e.pending:
                if s is e.sem and e.name == "pe":
                    continue
                if waits.get(s, (None, 0))[1] < v:
                    waits[s] = (s, v)
            e.pending = []
        for key, (s, v) in waits.items():
            if skip_own and s is e.sem:
                continue
            if e.known.get(id(s), 0) >= v:
                continue
            e.known[id(s)] = v
            out.append((s, v))
        return out

    def _mark(self, ev, reads, writes):
        s, v = ev
        for b in reads:
            if b.excl:
                b.w = ev
                b.r = {}
            elif b.r.get(s, 0) < v:
                b.r[s] = v
        for b in writes:
            b.w = ev
            b.r = {}

    def op(self, eng, fn, reads=(), writes=()):
        if self.nops >= KSTOP:
            return None
        e = self.E[eng]
        waits = self._deps(reads, writes)
        wl = self._filter(e, waits, skip_own=(eng == "pe") or not SAME_ENGINE_SYNC)
        e.count += 1
        ev = (e.sem, e.count)
        e.ops.append((wl, fn, (e.sem, 1)))
        self._mark(ev, reads, writes)
        self.nops += 1
        return ev

    def dma(self, eng, fn, reads=(), writes=()):
        if self.nops >= KSTOP:
            return None
        e = self.E[eng]
        slot = e.slots[e.slot_i % len(e.slots)]
        e.slot_i += 1
        waits = self._deps(reads, writes)
        if slot[1] > 0:
            s = slot[0]
            if waits.get(s, (None, 0))[1] < slot[1]:
                waits[s] = (s, slot[1])
        wl = self._filter(e, waits, skip_own=False)
        slot[1] += 16
        ev = (slot[0], slot[1])
        e.ops.append((wl, fn, (slot[0], 16)))
        self._mark(ev, reads, writes)
        self.nops += 1
        return ev

    def barrier(self):
        if _os.environ.get("KVERB"):
            print("barrier at op", self.nops)
        evs = []
        for en in self.E.values():
            for sm, cum in en.slots:
                if cum > 0:
                    evs.append((sm, cum))
            if en.count > 0:
                evs.append((en.sem, en.count))
        for en in self.E.values():
            en.pending = list(evs)

    def finish(self):
        e = self.E["sp"]
        wl = []
        for en in self.E.values():
            for s, cum in en.slots:
                if cum > 0:
                    wl.append((s, cum))
            if en.count > 0 and en is not e:
                wl.append((en.sem, en.count))
        e.ops.append((wl, None, None))

    def emit(self, block):
        def mk(e):
            def body(h):
                for wl, fn, inc in e.ops:
                    for s, v in wl:
                        h.wait_ge(s, v)
                    if fn is not None:
                        ins = fn(h)
                        ins.then_inc(inc[0], inc[1])
            return body

        block.tensor(mk(self.E["pe"]))
        block.scalar(mk(self.E["act"]))
        block.vector(mk(self.E["dve"]))
        block.gpsimd(mk(self.E["pool"]))
        block.sync(mk(self.E["sp"]))


class K:
    def __init__(self, layers=(0, 1, 2, 3), dbg=False, n_pool=2560):
        self.layers = layers
        self.dbg = dbg
        self.n_pool = n_pool
        self.nc = bass.Bass("TRN2", target_bir_lowering=False)
        self.stack = contextlib.ExitStack()
        self.P = Prog(self.nc, self.stack)
        self.din = {}
        self.dout = {}

    def inp(self, name, shape, dt=F32):
        t = self.nc.dram_tensor(name, list(shape), dt, kind="ExternalInput").ap()
        self.din[name] = t
        return t

    def outp(self, name, shape, dt=F32):
        t = self.nc.dram_tensor(name, list(shape), dt, kind="ExternalOutput").ap()
        self.dout[name] = t
        return t

    def scratch(self, name, shape, dt=F32):
        return self.nc.dram_tensor(name, list(shape), dt, kind="Internal").ap()

    def sb(self, name, shape, dt=F32, stack=None):
        self._uid = getattr(self, "_uid", 0) + 1
        return (stack or self.stack).enter_context(self.nc.sbuf_tensor(f"{name}_{self._uid}", list(shape), dt))

    def pst(self, name, shape, dt=F32, stack=None):
        return (stack or self.stack).enter_context(self.nc.psum_tensor(name, list(shape), dt))

    def mm(self, out, lhsT, rhs, start, stop, reads, writes, **kw):
        return self.P.op("pe", lambda h: h.matmul(out, lhsT, rhs, start=start, stop=stop, **kw), reads, writes)

    def act(self, out, in_, func, reads, writes, bias=None, scale=None, eng="act"):
        kw = {}
        if bias is not None:
            kw["bias"] = bias
        if scale is not None:
            kw["scale"] = scale
        return self.P.op("act", lambda h: h.activation(out, in_, func, **kw), reads, writes)

    def tt(self, eng, out, in0, in1, op, reads, writes):
        return self.P.op(eng, lambda h: h.tensor_tensor(out, in0, in1, op), reads, writes)

    def ts(self, eng, out, in0, s1, s2, op0, op1, reads, writes):
        if op1 is None:
            return self.P.op(eng, lambda h: h.tensor_scalar(out, in0, s1, None, op0), reads, writes)
        return self.P.op(eng, lambda h: h.tensor_scalar(out, in0, s1, s2, op0, op1), reads, writes)

    def stt(self, out, in0, scalar, in1, op0, op1, reads, writes):
        return self.P.op("dve", lambda h: h.scalar_tensor_tensor(out, in0, scalar, in1, op0, op1), reads, writes)

    def cp(self, eng, out, in_, reads, writes):
        if eng == "act":
            return self.P.op("act", lambda h: h.copy(out, in_), reads, writes)
        return self.P.op(eng, lambda h: h.tensor_copy(out, in_), reads, writes)

    def memset(self, eng, ap, val, writes):
        return self.P.op(eng, lambda h: h.memset(ap, val), (), writes)

    def dma(self, eng, out, in_, reads, writes, **kw):
        return self.P.dma(eng, lambda h: h.dma_start(out=out, in_=in_, **kw), reads, writes)

    def build(self):
        nc, P = self.nc, self.P
        L = self.layers
        xT0 = self.inp("xT0", [D, NTOK])
        pv = self.inp("pv", [128, NPV])
        cst = self.inp("cstf", [128, NCF])
        cstb_d = self.inp("cstb", [128, NCB])
        w = {}
        w["sc_in"] = self.inp("w_sc_in", [D, 3 * D])
        w["sc_out"] = self.inp("w_sc_out", [D, D])
        w["ffn_up"] = self.inp("w_ffn_up", [4, D, 2 * DFF])
        w["ffn_down"] = self.inp("w_ffn_down", [4, DFF, D])
        io = {}
        if 1 in L:
            w["sb_qkv"] = self.inp("w_sb_qkv", [D, 3 * D])
            w["sb_out"] = self.inp("w_sb_out", [D, D])
            io["cache_k"] = self.inp("cache_sb_k", [self.n_pool, 128, 16, 64])
            io["cache_v"] = self.inp("cache_sb_v", [self.n_pool, 128, 16, 64])
            io["page_table"] = self.inp("page_table", [NS, 64], I32)
            io["bd16"] = self.inp("bd16", [16, D])
            io["bsbr"] = self.inp("bsbr", [128, D])
            io["sb_k_p"] = self.outp("sb_k_p", [NP, D])
            io["sb_v_p"] = self.outp("sb_v_p", [NP, D])
            io["sb_k_s"] = self.outp("sb_k_s", [NS, D])
            io["sb_v_s"] = self.outp("sb_v_s", [NS, D])
        if 2 in L:
            w["gla_in"] = self.inp("w_gla_in", [D, 3088])
            w["gla_gate_up"] = self.inp("w_gla_gate_up", [16, 512])
            w["gla_out"] = self.inp("w_gla_out", [D, D])
            io["state_gla"] = self.inp("state_gla", [NS, 4, 128, 256])
            io["gla_p"] = self.outp("gla_p", [4, 128, 256])
            io["gla_s"] = self.outp("gla_s", [NS, 4, 128, 256])
        if 3 in L:
            w["dsw_qkv"] = self.inp("w_dsw_qkv", [D, 4608])
            w["dsw_out"] = self.inp("w_dsw_out", [512, D])
            io["dswB"] = self.inp("dswB", [8, 128, 24 * 128])
            io["dswAB"] = self.inp("dswAB", [128, 24])
            io["bd8"] = self.inp("bd8", [8, 512])
            for g, wg in enumerate((128, 512, 2048)):
                io[f"cache_dsw{g}"] = self.inp(f"cache_dsw{g}", [NS, wg, D])
                io[f"dsw_p{g}"] = self.outp(f"dsw_p{g}", [wg, D])
                io[f"dsw_s{g}"] = self.outp(f"dsw_s{g}", [NS, wg, D])
        sc_cache = self.inp("sc_cache", [NS * 2, D])
        ffn_state = self.inp("ffn_state", [4, NS * 2, DFF])
        yT = self.outp("yT", [D, NTOK])
        sc_out = self.outp("sc_out", [D, 2 + 2 * NS])
        ffn_out = self.outp("ffn_out", [4, DFF, 2 + 2 * NS])
        xs = [self.scratch("xs0", [D, NTOK]), self.scratch("xs1", [D, NTOK])]
        self.xs = xs
        self.pv = self.sb("pv_sb", [128, NPV])
        self.cst = self.sb("cst_sb", [128, NCF])
        self.cstb = self.sb("cstb_sb", [128, NCB], BF16)
        self.b_pv = Buf("pv")
        self.b_cst = Buf("cst")
        self.dma("sp", self.pv[:], pv[:, :], (), (self.b_pv,))
        self.dma("sp", self.cst[:], cst[:, :], (), (self.b_cst,))
        self.dma("pool", self.cstb[:], cstb_d[:, :], (), (self.b_cst,))
        self.banks = [self.pst(f"bank{i}", [128, 512]) for i in range(8)]
        self.bbuf = [Buf(f"bank{i}", excl=True) for i in range(8)]
        self.bank_rr = {}
        self.b_xs = [Buf("xs0"), Buf("xs1")]

        src = xT0
        b_src = Buf("xT0")
        for layer in L:
            if layer == 0:
                self.mixer_sconv(src, b_src, xs[0], self.b_xs[0], w, sc_cache, sc_out)
            elif layer == 1:
                self.mixer_sb(src, b_src, xs[0], self.b_xs[0], w, io)
            elif layer == 2:
                self.mixer_gla(src, b_src, xs[0], self.b_xs[0], w, io)
            elif layer == 3:
                self.mixer_dsw(src, b_src, xs[0], self.b_xs[0], w, io)
            else:
                raise NotImplementedError
            P.barrier()
            last = layer == L[-1]
            self.ffn(layer, xs[0], self.b_xs[0], yT if last else xs[1], Buf("yT") if last else self.b_xs[1], w, ffn_state, ffn_out)
            P.barrier()
            src = xs[1]
            b_src = self.b_xs[1]
        P.finish()
        with nc.Block() as block:
            P.emit(block)
        self.stack.close()
        return nc

    def ps(self, pool, idxs):
        i = self.bank_rr.get(pool, 0)
        self.bank_rr[pool] = i + 1
        k = idxs[i % len(idxs)]
        return self.banks[k], self.bbuf[k]

    def load_w(self, dst, b_dst, src2d, K, M, mchunk=1024):
        sv = src2d.rearrange("(k p) m -> p k m", p=128)
        for m0 in range(0, M, mchunk):
            m1 = min(M, m0 + mchunk)
            self.dma("pool", dst[:, :, m0:m1], sv[:, :, m0:m1], (), (b_dst,))

    def post_ln(self, st, xt, b_xt, N, ln_idx, dst, b_dst, c0, tmp):
        cst, cstb = self.cst, self.cstb
        onesb = cstb[:, C_ONES_S:C_ONES_S + 128]
        A, bA = self.ps("ln", [6, 7])
        B, bB = self.ps("ln", [6, 7])
        rb, b_rb, sq, b_sq = tmp["rb"], tmp["b_rb"], tmp["sq"], tmp["b_sq"]
        for k in range(8):
            i = k % 2
            self.cp("dve", rb[i][:, :N], xt[:, k, :N], (b_xt,), (b_rb[i],))
            self.act(sq[i][:, :N], xt[:, k, :N], AF.Square, (b_xt,), (b_sq[i],))
            self.mm(A[:, :N], onesb, rb[i][:, :N], k == 0, k == 7, (b_rb[i], self.b_cst), (bA,))
            self.mm(B[:, :N], onesb, sq[i][:, :N], k == 0, k == 7, (b_sq[i], self.b_cst), (bB,))
        mean, b_mean = tmp["mean"], tmp["b_mean"]
        t1, b_t1 = tmp["t1"], tmp["b_t1"]
        rstd, b_rstd = tmp["rstd"], tmp["b_rstd"]
        self.cp("act", mean[:, :N], A[:, :N], (bA,), (b_mean,))
        self.tt("dve", t1[:, :N], mean[:, :N], mean[:, :N], ALU.mult, (b_mean,), (b_t1,))
        self.tt("dve", t1[:, :N], B[:, :N], t1[:, :N], ALU.subtract, (bB, b_t1), (b_t1,))
        self.act(t1[:, :N], t1[:, :N], AF.Sqrt, (b_t1,), (b_t1,), bias=self.pv[:, PV_EPS:PV_EPS + 1])
        self.P.op("dve", lambda h: h.reciprocal(rstd[:, :N], t1[:, :N]), (b_t1,), (b_rstd,))
        self.stt(t1[:, :N], mean[:, :N], -1.0, rstd[:, :N], ALU.mult, ALU.mult, (b_mean, b_rstd), (b_t1,))
        g0 = PV_LNG + ln_idx * 8
        b0 = PV_LNB + ln_idx * 8
        for k in range(8):
            self.tt("dve", xt[:, k, :N], xt[:, k, :N], rstd[:, :N], ALU.mult, (b_xt, b_rstd), (b_xt,))
            self.tt("dve", xt[:, k, :N], xt[:, k, :N], t1[:, :N], ALU.add, (b_xt, b_t1), (b_xt,))
            self.act(xt[:, k, :N], xt[:, k, :N], AF.Identity, (b_xt, self.b_pv), (b_xt,),
                     bias=self.pv[:, b0 + k:b0 + k + 1], scale=self.pv[:, g0 + k:g0 + k + 1])
        self.dma("act", dst[:, c0:c0 + N].rearrange("(k p) n -> p k n", p=128), xt[:, :, :N], (b_xt,), (b_dst,))

    def ln_tmp(self, st):
        t = {}
        t["rb"] = [self.sb(f"ln_rb{i}", [128, 512], BF16, st) for i in range(2)]
        t["sq"] = [self.sb(f"ln_sq{i}", [128, 512], BF16, st) for i in range(2)]
        t["b_rb"] = [Buf(), Buf()]
        t["b_sq"] = [Buf(), Buf()]
        for nm in ("mean", "t1", "rstd"):
            t[nm] = self.sb("ln_" + nm, [128, 512], F32, st)
            t["b_" + nm] = Buf()
        return t

    def load_halo(self, st, src2d, nj, dst, b_dst):
        ident = self.cst[:, C_IDENT:C_IDENT + 128]
        rows = 2 * NS * nj
        sv = src2d.rearrange("q (j c) -> (q j) c", c=128)
        pt, bpt = self.ps("tr", [4, 5])
        r0 = 0
        while r0 < rows:
            n = min(128, rows - r0)
            t = self.sb("halo_stage", [128, 128], F32, st)
            bt = Buf()
            self.dma("sp", t[0:n, :], sv[r0:r0 + n, :], (), (bt,))
            self.P.op("pe", lambda h, t=t, n=n, r0=r0: h.transpose(pt[:, r0:r0 + n], t[0:n, :], ident[0:n, 0:n]),
                      (bt, self.b_cst), (bpt,))
            r0 += n
        self.cp("dve", dst[:, :, 0:2, :], pt[:, 0:rows].rearrange("p (s r j) -> p j r s", s=NS, r=2), (bpt,), (b_dst,))

    def conv3(self, out, taps, wcol, reads, writes, eng="dve"):
        self.ts("dve", out, taps[0], wcol(0), None, ALU.mult, None, reads, writes)
        self.stt(out, taps[1], wcol(1), out, ALU.mult, ALU.add, list(reads) + list(writes), writes)
        self.stt(out, taps[2], wcol(2), out, ALU.mult, ALU.add, list(reads) + list(writes), writes)

    def mixer_sconv(self, src, b_src, dst, b_dst, w, sc_cache, sc_out):
        with contextlib.ExitStack() as st:
            win = self.sb("sc_win", [128, 8, 3 * D], BF16, st)
            wout = self.sb("sc_wout", [128, 8, D], BF16, st)
            b_win, b_wout = Buf("win"), Buf("wout")
            self.load_w(win, b_win, w["sc_in"], D, 3 * D)
            self.load_w(wout, b_wout, w["sc_out"], D, D)
            xts = [self.sb(f"sc_xt{i}", [128, 8, 512], F32, st) for i in range(2)]
            b_xts = [Buf("xt0"), Buf("xt1")]
            xb = self.sb("sc_xb", [128, 8, 512], BF16, st)
            b_xb = Buf("xb")
            u = self.sb("sc_u", [128, 8, 516], F32, st)
            b_u = Buf("u")
            yb = self.sb("sc_yb", [128, 8, 512], BF16, st)
            b_yb = Buf("yb")
            cs = [self.sb(f"sc_c{i}", [128, 512], F32, st) for i in range(2)]
            b_cs = [Buf(), Buf()]
            zs = [self.sb(f"sc_z{i}", [128, 512], F32, st) for i in range(2)]
            b_zs = [Buf(), Buf()]
            us = self.sb("sc_us", [128, 8, 3, NS], F32, st)
            b_us = Buf("us")
            sco = self.sb("sc_o", [128, 8, 2 + 2 * NS], F32, st)
            b_sco = Buf()
            tmps = [self.ln_tmp(st), self.ln_tmp(st)]
            ident = self.cst[:, C_IDENT:C_IDENT + 128]
            self.load_halo(st, sc_cache, 8, us, b_us)
            self.memset("dve", u[:, :, 0:2], 0.0, (b_u,))
            sv0 = src[:, 0:512].rearrange("(k p) n -> p k n", p=128)
            self.dma("pool", xb[:, :, :512], sv0, (b_src,), (b_xb,))
            for ti, (c0, N) in enumerate(TILES):
                sample = N == NS
                xt, b_xt, tmp = xts[ti % 2], b_xts[ti % 2], tmps[ti % 2]
                sv = src[:, c0:c0 + N].rearrange("(k p) n -> p k n", p=128)
                self.dma("sp", xt[:, :, :N], sv, (b_src,), (b_xt,))
                if not sample and ti > 0:
                    self.cp("dve", u[:, :, 0:2], u[:, :, 512:514], (b_u,), (b_u,))
                for j in range(8):
                    pb, bpb = self.ps("a6", [0, 1, 2, 3, 4, 5])
                    pc, bpc = self.ps("a6", [0, 1, 2, 3, 4, 5])
                    pvv, bpv = self.ps("a6", [0, 1, 2, 3, 4, 5])
                    for (pp, bpp, off) in ((pb, bpb, 0), (pc, bpc, D), (pvv, bpv, 2 * D)):
                        for k in range(8):
                            self.mm(pp[:, :N], win[:, k, off + j * 128: off + (j + 1) * 128], xb[:, k, :N], k == 0, k == 7,
                                    (b_win, b_xb), (bpp,))
                    i = j % 2
                    self.cp("act", cs[i][:, :N], pc[:, :N], (bpc,), (b_cs[i],))
                    if sample:
                        ud = us[:, j, 2, :]
                        taps = [us[:, j, 0, :], us[:, j, 1, :], us[:, j, 2, :]]
                        bu = b_us
                    else:
                        ud = u[:, j, 2:2 + N]
                        taps = [u[:, j, 0:N], u[:, j, 1:1 + N], u[:, j, 2:2 + N]]
                        bu = b_u
                    self.tt("dve", ud, cs[i][:, :N], pvv[:, :N], ALU.mult, (b_cs[i], bpv), (bu,))
                    self.conv3(zs[i][:, :N], taps, lambda t, j=j: self.pv[:, PV_SCW + j * 3 + t: PV_SCW + j * 3 + t + 1],
                               (bu, self.b_pv), (b_zs[i],))
                    self.tt("dve", yb[:, j, :N], zs[i][:, :N], pb[:, :N], ALU.mult, (b_zs[i], bpb), (b_yb,))
                if ti + 1 < len(TILES):
                    c0n, Nn = TILES[ti + 1]
                    self.dma("pool", xb[:, :, :Nn], src[:, c0n:c0n + Nn].rearrange("(k p) n -> p k n", p=128), (b_src,), (b_xb,))
                for m in range(8):
                    po, bpo = self.ps("o", [6, 7])
                    for k in range(8):
                        self.mm(po[:, :N], wout[:, k, m * 128:(m + 1) * 128], yb[:, k, :N], k == 0, k == 7, (b_wout, b_yb), (bpo,))
                    self.stt(xt[:, m, :N], xt[:, m, :N], DN_ALPHA, po[:, :N], ALU.mult, ALU.add, (b_xt, bpo), (b_xt,))
                self.post_ln(st, xt, b_xt, N, 0, dst, b_dst, c0, tmp)
                if ti == 7:
                    self.cp("pool", sco[:, :, 0:2], u[:, :, 512:514], (b_u,), (b_sco,))
            self.cp("pool", sco[:, :, 2:2 + 2 * NS].rearrange("p k (r s) -> p k r s", r=2), us[:, :, 1:3, :], (b_us,), (b_sco,))
            self.dma("sp", sc_out.rearrange("(k p) c -> p k c", p=128), sco[:], (b_sco,), (Buf(),))

    def out_proj_ln(self, st, oT, b_oT, nk, wsrc, Kdim, res_src, b_res, dst, b_dst, ln_idx):
        wo = self.sb("wo", [128, nk, D], BF16, st)
        b_wo = Buf("wo")
        self.load_w(wo, b_wo, wsrc, Kdim, D)
        xts = [self.sb(f"op_xt{i}", [128, 8, 512], F32, st) for i in range(2)]
        b_xts = [Buf("xt0"), Buf("xt1")]
        tmps = [self.ln_tmp(st), self.ln_tmp(st)]
        for ti, (c0, N) in enumerate(TILES):
            xt, b_xt, tmp = xts[ti % 2], b_xts[ti % 2], tmps[ti % 2]
            sv = res_src[:, c0:c0 + N].rearrange("(k p) n -> p k n", p=128)
            self.dma("sp", xt[:, :, :N], sv, (b_res,), (b_xt,))
            for m in range(8):
                po, bpo = self.ps("o", [4, 5])
                for k in range(nk):
                    self.mm(po[:, :N], wo[:, k, m * 128:(m + 1) * 128], oT[:, k, c0:c0 + N], k == 0, k == nk - 1,
                            (b_wo,) + tuple(b_oT), (bpo,))
                self.stt(xt[:, m, :N], xt[:, m, :N], DN_ALPHA, po[:, :N], ALU.mult, ALU.add, (b_xt, bpo), (b_xt,))
            self.post_ln(st, xt, b_xt, N, ln_idx, dst, b_dst, c0, tmp)

    def mixer_sb(self, src, b_src, dst, b_dst, w, io):
        qT = self.scratch("sb_qT", [D, NP], BF16)
        kT = self.scratch("sb_kT", [D, NP], BF16)
        vtok = self.scratch("sb_vtok", [NP, D], BF16)
        b_qT, b_kT, b_vtok = Buf("qT"), Buf("kT"), Buf("vtok")
        cst, cstb, pv = self.cst, self.cstb, self.pv
        with contextlib.ExitStack() as sl:
            oT = self.sb("sb_oT", [128, 8, NTOK], BF16, sl)
            b_oT = [Buf(f"oT{j}") for j in range(8)]
            qs = self.sb("sb_qs", [NS, D], F32, sl)
            b_qs = Buf("qs")
            with contextlib.ExitStack() as st:
                wq = self.sb("sb_wq", [128, 8, 3 * D], BF16, st)
                b_wq = Buf("wq")
                self.load_w(wq, b_wq, w["sb_qkv"], D, 3 * D)
                xb = self.sb("sb_xb", [128, 8, 512], BF16, st)
                b_xb = Buf("xb")
                stg = [self.sb(f"sb_stg{i}", [128, 8, 512], BF16, st) for i in range(2)]
                b_stg = [Buf(), Buf()]
                stf = [self.sb(f"sb_stf{i}", [128, D], F32, st) for i in range(2)]
                b_stf = [Buf(), Buf()]
                stv = self.sb("sb_stv", [128, D], BF16, st)
                b_stv = Buf()
                for ti, (c0, N) in enumerate(TILES):
                    sample = N == NS
                    sv = src[:, c0:c0 + N].rearrange("(k p) n -> p k n", p=128)
                    self.dma("pool", xb[:, :, :N], sv, (b_src,), (b_xb,))
                    if not sample:
                        for qk in range(2):
                            for m in range(8):
                                pp, bpp = self.ps("a", [0, 1, 2, 3])
                                for k in range(8):
                                    self.mm(pp[:, :N], wq[:, k, qk * D + m * 128: qk * D + (m + 1) * 128], xb[:, k, :N], k == 0, k == 7,
                                            (b_wq, b_xb), (bpp,))
                                self.cp("act" if m % 2 else "dve", stg[qk][:, m, :N], pp[:, :N], (bpp,), (b_stg[qk],))
                            dd, bd = (qT, b_qT) if qk == 0 else (kT, b_kT)
                            self.dma("sp", dd[:, c0:c0 + N].rearrange("(k p) n -> p k n", p=128), stg[qk][:, :, :N], (b_stg[qk],), (bd,))
                        for blk in range(4):
                            r0 = c0 + blk * 128
                            for kv in range(2):
                                for half in range(2):
                                    pp, bpp = self.ps("o", [4, 5])
                                    off = (1 + kv) * D + half * 512
                                    for k in range(8):
                                        self.mm(pp[:, :], xb[:, k, blk * 128:(blk + 1) * 128], wq[:, k, off:off + 512], k == 0, k == 7,
                                                (b_wq, b_xb), (bpp,))
                                    self.cp("act", stf[kv][:, half * 512:(half + 1) * 512], pp[:, :], (bpp,), (b_stf[kv],))
                                    if kv == 1:
                                        self.cp("dve", stv[:, half * 512:(half + 1) * 512], pp[:, :], (bpp,), (b_stv,))
                                od = io["sb_k_p"] if kv == 0 else io["sb_v_p"]
                                self.dma("sp", od[r0:r0 + 128, :], stf[kv][:], (b_stf[kv],), (Buf(),))
                            self.dma("sp", vtok[r0:r0 + 128, :], stv[:], (b_stv,), (b_vtok,))
                    else:
                        for qkv in range(3):
                            for half in range(2):
                                pp, bpp = self.ps("o", [4, 5])
                                off = qkv * D + half * 512
                                for k in range(8):
                                    self.mm(pp[0:NS, :], xb[:, k, 0:NS], wq[:, k, off:off + 512], k == 0, k == 7, (b_wq, b_xb), (bpp,))
                                if qkv == 0:
                                    self.cp("act", qs[:, half * 512:(half + 1) * 512], pp[0:NS, :], (bpp,), (b_qs,))
                                else:
                                    self.cp("act", stf[qkv - 1][0:NS, half * 512:(half + 1) * 512], pp[0:NS, :], (bpp,), (b_stf[qkv - 1],))
                            if qkv > 0:
                                od = io["sb_k_s"] if qkv == 1 else io["sb_v_s"]
                                self.dma("sp", od[:, :], stf[qkv - 1][0:NS, :], (b_stf[qkv - 1],), (Buf(),))
            self.P.barrier()
            with contextlib.ExitStack() as st:
              if "sbp" not in SKIP:
                  kj = [[self.sb(f"sb_kj{i}{x}", [128, NP], BF16, st) for x in range(2)] for i in range(2)]
                  qj = [self.sb(f"sb_qj{i}", [128, NP], BF16, st) for i in range(2)]
                  vj = [[self.sb(f"sb_vj{i}{x}", [128, 32, 128], BF16, st) for x in range(2)] for i in range(2)]
                  b_kj, b_qj, b_vj = [Buf(), Buf()], [Buf(), Buf()], [Buf(), Buf()]
                  for i in range(2):
                      self.memset("pool", kj[i][0][64:128, :], 0.0, (b_kj[i],))
                      self.memset("pool", kj[i][1][0:64, :], 0.0, (b_kj[i],))
                      self.memset("pool", vj[i][0][:, :, 64:128], 0.0, (b_vj[i],))
                      self.memset("pool", vj[i][1][:, :, 0:64], 0.0, (b_vj[i],))
                  NR = 3
                  e_t = [self.sb(f"sb_e{i}", [128, 512], F32, st) for i in range(NR)]
                  sp_t = [self.sb(f"sb_sp{i}", [128, 512], BF16, st) for i in range(NR)]
                  eb_t = [self.sb(f"sb_eb{i}", [128, 512], F32, st) for i in range(NR)]
                  w_t = [self.sb(f"sb_w{i}", [128, 512], BF16, st) for i in range(NR)]
                  b_e, b_sp, b_eb, b_w = ([Buf() for _ in range(NR)] for _ in range(4))
                  R_t = [self.sb(f"sb_R{i}", [128, 512], F32, st) for i in range(2)]
                  Rb_t = [self.sb(f"sb_Rb{i}", [128, 512], BF16, st) for i in range(2)]
                  b_R, b_Rb = [Buf(), Buf()], [Buf(), Buf()]
                  mask = cst[:, C_MASK_S:C_MASK_S + 128]
                  ntri = cstb[:, C_NTRI:C_NTRI + 128]
                  nones = cstb[:, C_NONES:C_NONES + 128]
                  zer = cstb[:, C_ZERO:C_ZERO + 128]
                  vt3 = vtok.rearrange("(b p) c -> p b c", p=128)
                  for j in range(8):
                      jb = j % 2
                      self.dma("sp", kj[jb][0][0:64, :], kT[j * 128:j * 128 + 64, :], (b_kT,), (b_kj[jb],))
                      self.dma("sp", kj[jb][1][64:128, :], kT[j * 128 + 64:(j + 1) * 128, :], (b_kT,), (b_kj[jb],))
                      self.dma("sp", qj[jb][:], qT[j * 128:(j + 1) * 128, :], (b_qT,), (b_qj[jb],))
                      self.dma("sp", vj[jb][0][:, :, 0:64], vt3[:, :, j * 128:j * 128 + 64], (b_vtok,), (b_vj[jb],))
                      self.dma("sp", vj[jb][1][:, :, 64:128], vt3[:, :, j * 128 + 64:(j + 1) * 128], (b_vtok,), (b_vj[jb],))
                      blocks = []
                      seqn = 0
                      for t in range(8):
                          for hh in range(2):
                              for kb in range(4 * t + 3, -1, -1):
                                  blocks.append((t, hh, kb, kb == 4 * t + 3, kb == 0, seqn))
                              seqn += 1

                      def geom(b):
                          t, hh, kb, first, last, sq = b
                          r = kb - 4 * t
                          c_lo = 128 * max(r, 0)
                          return t, hh, kb, first, last, sq, r, c_lo

                      def stA(i):
                          t, hh, kb, first, last, sq, r, c_lo = geom(blocks[i])
                          S, bS = self.banks[i % 2], self.bbuf[i % 2]
                          self.mm(S[:, c_lo:512], kj[jb][hh][:, kb * 128:(kb + 1) * 128], qj[jb][:, t * 512 + c_lo:(t + 1) * 512],
                                  True, True, (b_kj[jb], b_qj[jb]), (bS,))

                      def stB(i):
                          t, hh, kb, first, last, sq, r, c_lo = geom(blocks[i])
                          S, bS = self.banks[i % 2], self.bbuf[i % 2]
                          x = i % NR
                          h = 2 * j + hh
                          self.act(e_t[x][:, c_lo:512], S[:, c_lo:512], AF.Exp, (bS, self.b_pv), (b_e[x],),
                                   bias=pv[:, PV_BSB + h:PV_BSB + h + 1], scale=0.125)
                          if r >= 0:
                              self.tt("dve", e_t[x][:, c_lo:c_lo + 128], e_t[x][:, c_lo:c_lo + 128], mask, ALU.mult, (b_e[x], self.b_cst), (b_e[x],))
                          self.act(sp_t[x][:, c_lo:512], e_t[x][:, c_lo:512], AF.Ln, (b_e[x],), (b_sp[x],), bias=1.0)

                      def stC(i):
                          t, hh, kb, first, last, sq, r, c_lo = geom(blocks[i])
                          BT, bBT = self.banks[2 + i % 2], self.bbuf[2 + i % 2]
                          x = i % NR
                          self.mm(BT[:, c_lo:512], ntri, sp_t[x][:, c_lo:512], True, first, (b_sp[x], self.b_cst), (bBT,))
                          if not first:
                              self.mm(BT[:, c_lo:512], nones, Rb_t[sq % 2][:, c_lo:512], False, True, (b_Rb[sq % 2], self.b_cst), (bBT,))

                      def stD(i):
                          t, hh, kb, first, last, sq, r, c_lo = geom(blocks[i])
                          BT, bBT = self.banks[2 + i % 2], self.bbuf[2 + i % 2]
                          x = i % NR
                          self.act(eb_t[x][:, c_lo:512], BT[:, c_lo:512], AF.Exp, (bBT,), (b_eb[x],))
                          self.tt("dve", w_t[x][:, c_lo:512], e_t[x][:, c_lo:512], eb_t[x][:, c_lo:512], ALU.mult, (b_e[x], b_eb[x]), (b_w[x],))
                          if not last:
                              y = sq % 2
                              if first:
                                  self.memset("pool", R_t[y][:], 0.0, (b_R[y],))
                                  self.memset("pool", Rb_t[y][:], 0.0, (b_Rb[y],))
                              self.tt("pool", R_t[y][:, c_lo:512], R_t[y][:, c_lo:512], sp_t[x][:, c_lo:512], ALU.add, (b_R[y], b_sp[x]), (b_R[y],))
                              self.cp("act" if i % 2 else "dve", Rb_t[y][:, c_lo:512], R_t[y][:, c_lo:512], (b_R[y],), (b_Rb[y],))

                      def stE(i):
                          t, hh, kb, first, last, sq, r, c_lo = geom(blocks[i])
                          O, bO = self.banks[6 + t % 2], self.bbuf[6 + t % 2]
                          x = i % NR
                          if first and hh == 0:
                              self.mm(O[:, :], zer, qj[jb][:, 0:512], True, False, (b_qj[jb], self.b_cst), (bO,))
                          self.mm(O[:, c_lo:512], vj[jb][hh][:, kb, :], w_t[x][:, c_lo:512], False, last and hh == 1, (b_vj[jb], b_w[x]), (bO,))
                          if last and hh == 1:
                              self.cp("dve", oT[:, j, t * 512:(t + 1) * 512], O[:, :], (bO,), (b_oT[j],))

                      nb = len(blocks)
                      for i in range(nb + 2):
                          if i < nb:
                              stA(i)
                              stB(i)
                          if 0 <= i - 1 < nb:
                              stC(i - 1)
                              stD(i - 1)
                          if 0 <= i - 2 < nb:
                              stE(i - 2)
            self.P.barrier()
            with contextlib.ExitStack() as st:
                if "sbs" not in SKIP:
                    self.sb_sample(st, io, oT, b_oT, qs, b_qs)
            self.P.barrier()
            with contextlib.ExitStack() as st:
                self.out_proj_ln(st, oT, b_oT, 8, w["sb_out"], D, src, b_src, dst, b_dst, 2)
        self.P.barrier()

    def sb_sample(self, st, io, oT, b_oT, qs, b_qs):
        cst, cstb, pv = self.cst, self.cstb, self.pv
        NPG = 64
        ck = io["cache_k"].rearrange("n p h d -> (n p) (h d)")
        cv = io["cache_v"].rearrange("n p h d -> (n p) (h d)")
        pti = self.sb("ss_pti", [128, NPG], I32, st)
        ptf = self.sb("ss_ptf", [128, NPG], F32, st)
        idx = self.sb("ss_idx", [128, NPG], I32, st)
        b_pti, b_ptf, b_idx = Buf(), Buf(), Buf()
        qb = self.sb("ss_qb", [128, D], BF16, st)
        b_qb = Buf()
        NKB = 4
        kpg = [self.sb(f"ss_k{i}", [128, D], BF16, st) for i in range(NKB)]
        b_kpg = [Buf() for _ in range(NKB)]
        vpg = [self.sb(f"ss_v{i}", [128, D], BF16, st) for i in range(NKB)]
        b_vpg = [Buf() for _ in range(NKB)]
        prod = [self.sb(f"ss_prod{i}", [128, D], F32, st) for i in range(2)]
        b_prod = [Buf(), Buf()]
        z = self.sb("ss_z", [128, NPG * 16], F32, st)
        b_z = Buf()
        e = self.sb("ss_e", [128, NPG * 16], F32, st)
        b_e = Buf()
        spb = self.sb("ss_spb", [128, NPG * 16], BF16, st)
        b_spb = Buf()
        ca = self.sb("ss_ca", [128, NPG * 16], F32, st)
        cb = self.sb("ss_cb", [128, NPG * 16], F32, st)
        b_ca, b_cb = Buf(), Buf()
        wt = self.sb("ss_w", [128, NPG * 16], BF16, st)
        b_wt = Buf()
        osb = self.sb("ss_o", [16, D], F32, st)
        b_osb = Buf()
        ocol = self.sb("ss_oc", [128, 8], F32, st)
        b_ocol = Buf()
        ntri = cstb[:, C_NTRI:C_NTRI + 128]
        nones = cstb[:, C_NONES:C_NONES + 128]
        qsb = self.sb("ss_qsb", [NS, D], BF16, st)
        b_qsb = Buf()
        bd = self.sb("ss_bd", [16, D], F32, st)
        bsbr = self.sb("ss_bsbr", [128, D], F32, st)
        b_bd = Buf()
        self.dma("sp", bd[:], io["bd16"][:, :], (), (b_bd,))
        self.dma("sp", bsbr[:], io["bsbr"][:, :], (), (b_bd,))
        self.cp("dve", qsb[:], qs[:], (b_qs,), (b_qsb,))
        for s_ in range(NS):
            self.dma("sp", pti[:], io["page_table"][s_:s_ + 1, :].broadcast_to([128, NPG]), (), (b_pti,))
            self.cp("dve", ptf[:], pti[:], (b_pti,), (b_ptf,))
            self.ts("dve", ptf[:], ptf[:], 128.0, cst[:, C_PCOL:C_PCOL + 1], ALU.mult, ALU.add, (b_ptf, self.b_cst), (b_ptf,))
            self.cp("dve", idx[:], ptf[:], (b_ptf,), (b_idx,))
            for half in range(2):
                pp, bpp = self.ps("o", [4, 5])
                self.mm(pp[:, :], cstb[0:NS, C_SEL + s_ * 128:C_SEL + (s_ + 1) * 128], qsb[:, half * 512:(half + 1) * 512], True, True,
                        (b_qsb, self.b_cst), (bpp,))
                self.cp("act", qb[:, half * 512:(half + 1) * 512], pp[:, :], (bpp,), (b_qb,))
            for pg in range(NPG):
                x = pg % NKB
                self.P.dma("pool", lambda h, x=x, pg=pg: h.indirect_dma_start(
                    kpg[x][:], None, ck, bass.IndirectOffsetOnAxis(ap=idx[:, pg:pg + 1], axis=0)), (b_idx,), (b_kpg[x],))
                y = pg % 2
                self.tt("dve", prod[y][:], kpg[x][:], qb[:], ALU.mult, (b_kpg[x], b_qb), (b_prod[y],))
                self.P.op("dve", lambda h, y=y, pg=pg: h.tensor_reduce(
                    z[:, pg * 16:(pg + 1) * 16], prod[y][:].rearrange("p (h d) -> p h d", d=64), mybir.AxisListType.X, ALU.add),
                    (b_prod[y],), (b_z,))
            self.stt(z[:], z[:], 0.125, bsbr[:], ALU.mult, ALU.add, (b_z, b_bd), (b_z,))
            self.act(e[:], z[:], AF.Exp, (b_z,), (b_e,))
            self.act(spb[:], e[:], AF.Ln, (b_e,), (b_spb,), bias=1.0)
            bts = []
            for half in range(2):
                BT, bBT = self.banks[half], self.bbuf[half]
                self.mm(BT[:, :], ntri, spb[:, half * 512:(half + 1) * 512], True, True, (b_spb, self.b_cst), (bBT,))
                bts.append((BT, bBT))
            for half in range(2):
                T, bT = self.banks[2 + half], self.bbuf[2 + half]
                self.mm(T[:, :], nones, spb[:, half * 512:(half + 1) * 512], True, True, (b_spb, self.b_cst), (bT,))
                lo = half * 512
                if half == 0:
                    self.cp("dve", ca[:, 0:496], T[:, 16:512], (bT,), (b_ca,))
                else:
                    self.cp("dve", ca[:, 496:512], T[:, 0:16], (bT,), (b_ca,))
                    self.cp("dve", ca[:, 512:1008], T[:, 16:512], (bT,), (b_ca,))
            self.memset("dve", ca[:, 1008:1024], 0.0, (b_ca,))
            a, ba, b2, bb2 = ca, b_ca, cb, b_cb
            for step in (1, 2, 4, 8, 16, 32):
                n = (NPG - step) * 16
                self.tt("dve", b2[:, 0:n], a[:, 0:n], a[:, step * 16:step * 16 + n], ALU.add, (ba,), (bb2,))
                self.cp("pool", b2[:, n:1024], a[:, n:1024], (ba,), (bb2,))
                a, ba, b2, bb2 = b2, bb2, a, ba
            for half in range(2):
                BT, bBT = bts[half]
                self.tt("dve", b2[:, half * 512:(half + 1) * 512], BT[:, :], a[:, half * 512:(half + 1) * 512], ALU.add, (bBT, ba), (bb2,))
            self.act(b2[:], b2[:], AF.Exp, (bb2,), (bb2,))
            self.tt("dve", wt[:], e[:], b2[:], ALU.mult, (b_e, bb2), (b_wt,))
            O0, bO0 = self.banks[6], self.bbuf[6]
            O1, bO1 = self.banks[7], self.bbuf[7]
            for pg in range(NPG):
                x = pg % NKB
                self.P.dma("pool", lambda h, x=x, pg=pg: h.indirect_dma_start(
                    vpg[x][:], None, cv, bass.IndirectOffsetOnAxis(ap=idx[:, pg:pg + 1], axis=0)), (b_idx,), (b_vpg[x],))
                self.mm(O0[0:16, :], wt[:, pg * 16:(pg + 1) * 16], vpg[x][:, 0:512], pg == 0, pg == NPG - 1, (b_wt, b_vpg[x]), (bO0,))
                self.mm(O1[0:16, :], wt[:, pg * 16:(pg + 1) * 16], vpg[x][:, 512:1024], pg == 0, pg == NPG - 1, (b_wt, b_vpg[x]), (bO1,))
            self.tt("dve", osb[:, 0:512], O0[0:16, :], bd[:, 0:512], ALU.mult, (bO0, b_bd), (b_osb,))
            self.tt("dve", osb[:, 512:1024], O1[0:16, :], bd[:, 512:1024], ALU.mult, (bO1, b_bd), (b_osb,))
            pc, bpc = self.ps("o", [4, 5])
            for jj in range(8):
                self.mm(pc[:, jj:jj + 1], osb[0:16, jj * 128:(jj + 1) * 128], cst[0:16, C_ONES:C_ONES + 1], True, True, (b_osb, self.b_cst), (bpc,))
            self.cp("dve", oT[:, :, NP + s_], pc[:, 0:8], (bpc,), tuple(b_oT))

    def mixer_gla(self, src, b_src, dst, b_dst, w, io):
        cst, cstb, pv = self.cst, self.cstb, self.pv
        QS = 128 ** -0.5
        with contextlib.ExitStack() as st:
            GIN = 3088
            win = self.sb("g_win", [128, 8, GIN], BF16, st)
            b_win = Buf("win")
            self.load_w(win, b_win, w["gla_in"], D, GIN, mchunk=1024)
            wo = self.sb("g_wo", [128, 8, D], BF16, st)
            b_wo = Buf("wo")
            self.load_w(wo, b_wo, w["gla_out"], D, D)
            wgu = self.sb("g_wgu", [16, 512], BF16, st)
            b_wgu = Buf()
            self.dma("pool", wgu[:], w["gla_gate_up"][:, :], (), (b_wgu,))
            negb = self.sb("g_negb", [128, 4], F32, st)
            b_loc = Buf("gloc")
            self.ts("dve", negb[:], pv[:, PV_BG:PV_BG + 4], -1.0, None, ALU.mult, None, (self.b_pv,), (b_loc,))
            minc = self.sb("g_minc", [128, 128], F32, st)
            self.tt("dve", minc[:], cst[:, C_MASK_S:C_MASK_S + 128], cst[:, C_IDENT:C_IDENT + 128], ALU.add, (self.b_cst,), (b_loc,))
            identb = self.sb("g_identb", [128, 128], BF16, st)
            self.cp("dve", identb[:], cst[:, C_IDENT:C_IDENT + 128], (self.b_cst,), (b_loc,))
            o256 = self.sb("g_o256", [128, 128], BF16, st)
            self.memset("dve", o256[:], 1.0 / 256, (b_loc,))
            rmask = self.sb("g_rmask", [128, 512], F32, st)
            self.memset("dve", rmask[:], 1.0, (b_loc,))
            self.memset("dve", rmask[:].rearrange("p (c i) -> p c i", i=128)[:, :, 0:1], 0.0, (b_loc,))
            self_f = self.sb("g_self", [NS, NS * 128], F32, st)
            self.cp("dve", self_f[:], cstb[0:NS, C_SEL:C_SEL + NS * 128], (self.b_cst,), (b_loc,))
            S = [self.sb(f"g_S{h}", [128, 256], F32, st) for h in range(4)]
            Sb = [self.sb(f"g_Sb{h}", [128, 256], BF16, st) for h in range(4)]
            b_S = [Buf() for _ in range(4)]
            b_Sb = [Buf() for _ in range(4)]
            for h in range(4):
                self.memset("pool", S[h][:], 0.0, (b_S[h],))
                self.memset("pool", Sb[h][:], 0.0, (b_Sb[h],))
            xb = self.sb("g_xb", [128, 8, 512], BF16, st)
            xts = [self.sb(f"g_xt{i}", [128, 8, 512], F32, st) for i in range(2)]
            b_xb = Buf("xb")
            b_xts = [Buf("xt0"), Buf("xt1")]
            vtk = self.sb("g_vtk", [128, 4, D], BF16, st)
            b_vtk = Buf("vtk")
            glT = self.sb("g_glT", [16, 512], BF16, st)
            b_glT = Buf()
            f32t = {}
            for nm in ("e1", "sp", "Bp", "eb", "enb", "ebl"):
                f32t[nm] = (self.sb("g_" + nm, [128, 512], F32, st), Buf(nm))
            nbl = self.sb("g_nbl", [128, 4], F32, st)
            elast = self.sb("g_elast", [128, 4], F32, st)
            b_nbl, b_elast = Buf(), Buf()
            qg = self.sb("g_qg", [128, 512], BF16, st)
            kg = self.sb("g_kg", [128, 512], BF16, st)
            kl = self.sb("g_kl", [128, 512], BF16, st)
            b_qg, b_kg, b_kl = Buf(), Buf(), Buf()
            klt = [self.sb(f"g_klt{i}", [128, 128], BF16, st) for i in range(2)]
            b_klt = [Buf(), Buf()]
            sc = [self.sb(f"g_sc{i}", [128, 128], BF16, st) for i in range(2)]
            b_sc = [Buf(), Buf()]
            oraw = self.sb("g_oraw", [128, 8, 512], F32, st)
            b_oraw = [Buf() for _ in range(8)]
            sq = [self.sb(f"g_sq{i}", [128, 512], BF16, st) for i in range(2)]
            b_sq = [Buf(), Buf()]
            rstd = self.sb("g_rstd", [128, 512], F32, st)
            b_rstd = Buf()
            sr = [self.sb(f"g_sr{i}", [128, 512], F32, st) for i in range(2)]
            b_sr = [Buf(), Buf()]
            oTt = self.sb("g_oT", [128, 8, 512], BF16, st)
            b_oTt = Buf("oTt")
            qs_ = self.sb("g_qs", [128, 4, NS], F32, st)
            ks_ = self.sb("g_ks", [128, 4, NS], F32, st)
            as_ = self.sb("g_as", [128, 4, NS], F32, st)
            qa_ = self.sb("g_qa", [128, 4, NS], BF16, st)
            qk_ = self.sb("g_qk", [128, 4, NS], BF16, st)
            vT_ = self.sb("g_vT", [128, 8, NS], F32, st)
            vrow = self.sb("g_vrow", [NS, D], F32, st)
            b_smp = Buf("smp")
            S0 = [self.sb(f"g_S0{i}", [128, 256], F32, st) for i in range(2)]
            S0b = [self.sb(f"g_S0b{i}", [128, 256], BF16, st) for i in range(2)]
            b_S0 = [Buf(), Buf()]
            b_S0b = [Buf(), Buf()]
            Sn = [self.sb(f"g_Sn{i}", [128, 256], F32, st) for i in range(2)]
            b_Sn = [Buf(), Buf()]
            tmps = [self.ln_tmp(st), self.ln_tmp(st)]
            self.dma("pool", xb[:, :, :512], src[:, 0:512].rearrange("(k p) n -> p k n", p=128), (b_src,), (b_xb,))

            def proj(mcol, N, M=128):
                pp, bpp = self.ps("p", [0, 1])
                for k in range(8):
                    self.mm(pp[0:M, :N], win[:, k, mcol:mcol + M], xb[:, k, :N], k == 0, k == 7, (b_win, b_xb), (bpp,))
                return pp, bpp

            for ti, (c0, N) in enumerate(TILES):
                sample = N == NS
                sv = src[:, c0:c0 + N].rearrange("(k p) n -> p k n", p=128)
                xt, b_xt, tmp = xts[ti % 2], b_xts[ti % 2], tmps[ti % 2]
                self.dma("sp", xt[:, :, :N], sv, (b_src,), (b_xt,))
                nblk = 1 if sample else 4
                for blk in range(nblk):
                    M = NS if sample else 128
                    for half in range(2):
                        pp, bpp = self.ps("p", [0, 1])
                        for k in range(8):
                            self.mm(pp[0:M, :], xb[:, k, blk * 128:blk * 128 + M], win[:, k, 1024 + half * 512:1024 + (half + 1) * 512],
                                    k == 0, k == 7, (b_win, b_xb), (bpp,))
                        if sample:
                            self.cp("act", vrow[:, half * 512:(half + 1) * 512], pp[0:NS, :], (bpp,), (b_smp,))
                        else:
                            self.cp("act" if half else "dve", vtk[:, blk, half * 512:(half + 1) * 512], pp[:, :], (bpp,), (b_vtk,))
                pp, bpp = proj(3072, N, 16)
                self.cp("act", glT[:, :N], pp[0:16, :N], (bpp,), (b_glT,))
                for h in range(4):
                    gt, bgt = self.ps("p", [0, 1])
                    self.mm(gt[:, :N], wgu[0:16, h * 128:(h + 1) * 128], glT[0:16, :N], True, True, (b_wgu, b_glT), (bgt,))
                    e1, b_e1 = f32t["e1"]
                    sp, b_sp = f32t["sp"]
                    self.act(e1[:, :N], gt[:, :N], AF.Exp, (bgt, b_loc), (b_e1,), bias=negb[:, h:h + 1], scale=-1.0)
                    self.act(sp[:, :N], e1[:, :N], AF.Ln, (b_e1,), (b_sp,), bias=1.0)
                    if sample:
                        self.act(as_[:, h, :], sp[:, :N], AF.Exp, (b_sp,), (b_smp,), scale=-1.0 / 16)
                        pq, bpq = proj(h * 128, N)
                        self.ts("dve", qs_[:, h, :], pq[:, :N], QS, None, ALU.mult, None, (bpq,), (b_smp,))
                        pk, bpk = proj(512 + h * 128, N)
                        self.cp("act", ks_[:, h, :], pk[:, :N], (bpk,), (b_smp,))
                        self.tt("dve", qa_[:, h, :], qs_[:, h, :], as_[:, h, :], ALU.mult, (b_smp,), (b_smp,))
                        self.tt("dve", qk_[:, h, :], qs_[:, h, :], ks_[:, h, :], ALU.mult, (b_smp,), (b_smp,))
                        for m in range(2):
                            pv_, bpv_ = proj(1024 + h * 256 + m * 128, N)
                            self.cp("act", vT_[:, h * 2 + m, :], pv_[:, :N], (bpv_,), (b_smp,))
                        pqk, bpqk = self.banks[7], self.bbuf[7]
                        self.mm(pqk[:, 0:NS], cstb[:, C_NONES:C_NONES + 128], qk_[:, h, :], True, True, (b_smp, self.b_cst), (bpqk,))
                        for m in range(2):
                            oacc, boacc = self.banks[4 + m], self.bbuf[4 + m]
                            for s_ in range(NS):
                                y = (h * NS + s_) % 2
                                if m == 0:
                                    self.dma("sp", S0[y][:], io["state_gla"][s_, h], (), (b_S0[y],))
                                    self.dma("pool", S0b[y][:], io["state_gla"][s_, h], (), (b_S0b[y],))
                                else:
                                    self.dma("pool", S0b[y][:], io["state_gla"][s_, h], (), (b_S0b[y],))
                                self.mm(oacc[:, s_:s_ + 1], S0b[y][:, m * 128:(m + 1) * 128], qa_[:, h, s_:s_ + 1], True, True,
                                        (b_S0b[y], b_smp), (boacc,))
                                if m == 0:
                                    pvb, bpvb = self.ps("s", [2, 3])
                                    self.mm(pvb[:, 0:256], self_f[0:NS, s_ * 128:(s_ + 1) * 128], vrow[0:NS, h * 256:(h + 1) * 256], True, True,
                                            (b_loc, b_smp), (bpvb,))
                                    self.ts("dve", Sn[y][:], pvb[:, 0:256], ks_[:, h, s_:s_ + 1], None, ALU.mult, None, (bpvb, b_smp), (b_Sn[y],))
                                    self.stt(Sn[y][:], S0[y][:], as_[:, h, s_:s_ + 1], Sn[y][:], ALU.mult, ALU.add, (b_S0[y], b_smp, b_Sn[y]), (b_Sn[y],))
                                    self.dma("sp", io["gla_s"][s_, h], Sn[y][:], (b_Sn[y],), (Buf(),))
                            self.tt("dve", oraw[:, h * 2 + m, :N], pqk[:, 0:NS], vT_[:, h * 2 + m, :], ALU.mult, (bpqk, b_smp), (b_oraw[h * 2 + m],))
                            self.tt("dve", oraw[:, h * 2 + m, :N], oacc[:, 0:NS], oraw[:, h * 2 + m, :N], ALU.subtract, (boacc, b_oraw[h * 2 + m]), (b_oraw[h * 2 + m],))
                            i2 = m % 2
                            self.act(sq[i2][:, :N], oraw[:, h * 2 + m, :N], AF.Square, (b_oraw[h * 2 + m],), (b_sq[i2],))
                            pms, bpms = self.banks[6], self.bbuf[6]
                            self.mm(pms[:, :N], o256[:], sq[i2][:, :N], m == 0, m == 1, (b_sq[i2], b_loc), (bpms,))
                    else:
                        Bp, b_Bp = f32t["Bp"]
                        eb, b_eb = f32t["eb"]
                        enb, b_enb = f32t["enb"]
                        ebl, b_ebl = f32t["ebl"]
                        self.P.op("dve", lambda hh, Bp=Bp, sp=sp: hh.tensor_tensor_scan(Bp[:, :], rmask[:, :], sp[:, :], 0.0, ALU.mult, ALU.add),
                                  (b_sp, b_loc), (b_Bp,))
                        self.act(eb[:, :], Bp[:, :], AF.Exp, (b_Bp,), (b_eb,), scale=-1.0 / 16)
                        self.act(enb[:, :], Bp[:, :], AF.Exp, (b_Bp,), (b_enb,), scale=1.0 / 16)
                        self.ts("dve", nbl[:, :], Bp[:, :].rearrange("p (c i) -> p c i", i=128)[:, :, 127], -1.0 / 16, None, ALU.mult, None, (b_Bp,), (b_nbl,))
                        self.act(elast[:, :], nbl[:, :], AF.Exp, (b_nbl,), (b_elast,))
                        for c in range(4):
                            self.act(ebl[:, c * 128:(c + 1) * 128], Bp[:, c * 128:(c + 1) * 128], AF.Exp, (b_Bp, b_nbl), (b_ebl,),
                                     bias=nbl[:, c:c + 1], scale=1.0 / 16)
                        pq, bpq = proj(h * 128, N)
                        self.stt(qg[:, :], pq[:, :], QS, eb[:, :], ALU.mult, ALU.mult, (bpq, b_eb), (b_qg,))
                        pk, bpk = proj(512 + h * 128, N)
                        self.tt("dve", kg[:, :], pk[:, :], enb[:, :], ALU.mult, (bpk, b_enb), (b_kg,))
                        self.tt("dve", kl[:, :], pk[:, :], ebl[:, :], ALU.mult, (bpk, b_ebl), (b_kl,))
                        oacc = [(self.banks[4], self.bbuf[4]), (self.banks[5], self.bbuf[5])]
                        for c in range(4):
                            cs_ = slice(c * 128, (c + 1) * 128)
                            i2 = c % 2
                            ptb, bptb = self.ps("s", [2, 3])
                            ptv = ptb[:, 0:64].bitcast(BF16)
                            self.P.op("pe", lambda hh, ptv=ptv, cs_=cs_: hh.transpose(ptv, kl[:, cs_], identb[:, :]), (b_kl, b_loc), (bptb,))
                            self.cp("act", klt[i2][:], ptv, (bptb,), (b_klt[i2],))
                            psc, bpsc = self.ps("s", [2, 3])
                            self.mm(psc[:, 0:128], kg[:, cs_], qg[:, cs_], True, True, (b_kg, b_qg), (bpsc,))
                            self.tt("dve", sc[i2][:], psc[:, 0:128], minc[:], ALU.mult, (bpsc, b_loc), (b_sc[i2],))
                            for m in range(2):
                                ob, bob = oacc[m]
                                self.mm(ob[:, cs_], vtk[:, c, h * 256 + m * 128:h * 256 + (m + 1) * 128], sc[i2][:], True, False, (b_vtk, b_sc[i2]), (bob,))
                                self.mm(ob[:, cs_], Sb[h][:, m * 128:(m + 1) * 128], qg[:, cs_], False, True, (b_Sb[h], b_qg), (bob,))
                            pu, bpu = self.ps("s", [2, 3])
                            self.mm(pu[:, 0:256], klt[i2][:], vtk[:, c, h * 256:(h + 1) * 256], True, True, (b_klt[i2], b_vtk), (bpu,))
                            self.stt(S[h][:], S[h][:], elast[:, c:c + 1], pu[:, 0:256], ALU.mult, ALU.add, (b_S[h], b_elast, bpu), (b_S[h],))
                            self.cp("pool", Sb[h][:], S[h][:], (b_S[h],), (b_Sb[h],))
                        for m in range(2):
                            ob, bob = oacc[m]
                            self.cp("act", oraw[:, h * 2 + m, :], ob[:, :], (bob,), (b_oraw[h * 2 + m],))
                            i2 = m % 2
                            self.act(sq[i2][:, :], oraw[:, h * 2 + m, :], AF.Square, (b_oraw[h * 2 + m],), (b_sq[i2],))
                            pms, bpms = self.banks[6], self.bbuf[6]
                            self.mm(pms[:, :N], o256[:], sq[i2][:, :N], m == 0, m == 1, (b_sq[i2], b_loc), (bpms,))
                    pms, bpms = self.banks[6], self.bbuf[6]
                    self.act(rstd[:, :N], pms[:, :N], AF.Sqrt, (bpms, self.b_pv), (b_rstd,), bias=pv[:, PV_EPS:PV_EPS + 1])
                    self.P.op("dve", lambda hh, N=N: hh.reciprocal(rstd[:, :N], rstd[:, :N]), (b_rstd,), (b_rstd,))
                    for m in range(2):
                        jx = h * 2 + m
                        pr, bpr = proj(2048 + jx * 128, N)
                        i2 = m % 2
                        self.act(sr[i2][:, :N], pr[:, :N], AF.Silu, (bpr,), (b_sr[i2],))
                        self.tt("dve", oraw[:, jx, :N], oraw[:, jx, :N], rstd[:, :N], ALU.mult, (b_oraw[jx], b_rstd), (b_oraw[jx],))
                        self.stt(oTt[:, jx, :N], oraw[:, jx, :N], pv[:, PV_GN + m:PV_GN + m + 1], sr[i2][:, :N], ALU.mult, ALU.mult,
                                 (b_oraw[jx], b_sr[i2], self.b_pv), (b_oTt,))
                if ti + 1 < len(TILES):
                    c0n, Nn = TILES[ti + 1]
                    self.dma("pool", xb[:, :, :Nn], src[:, c0n:c0n + Nn].rearrange("(k p) n -> p k n", p=128), (b_src,), (b_xb,))
                for m in range(8):
                    po, bpo = self.ps("o", [4, 5])
                    for k in range(8):
                        self.mm(po[:, :N], wo[:, k, m * 128:(m + 1) * 128], oTt[:, k, :N], k == 0, k == 7, (b_wo, b_oTt), (bpo,))
                    self.stt(xt[:, m, :N], xt[:, m, :N], DN_ALPHA, po[:, :N], ALU.mult, ALU.add, (b_xt, bpo), (b_xt,))
                self.post_ln(st, xt, b_xt, N, 4, dst, b_dst, c0, tmp)
                if ti == 7:
                    for h in range(4):
                        self.dma("sp", io["gla_p"][h], S[h][:], (b_S[h],), (Buf(),))
        self.P.barrier()

    def mixer_dsw(self, src, b_src, dst, b_dst, w, io):
        cst, cstb, pv = self.cst, self.cstb, self.pv
        qT3 = self.scratch("dsw_qT", [1536, NP], BF16)
        kT3 = self.scratch("dsw_kT", [1536, NP], BF16)
        v3 = self.scratch("dsw_v", [3, NP, 512], BF16)
        qkvs = self.scratch("dsw_qkvs", [NS, 4608], F32)
        b_qT3, b_kT3, b_v3, b_qkvs = Buf(), Buf(), Buf(), Buf()
        WG = (128, 512, 2048)
        DG = (1, 4, 16)
        ENT = [(0, 0), (0, 1)] + [(1, j) for j in range(5)] + [(2, j) for j in range(17)]
        for g in range(3):
            for s_ in range(NS):
                self.dma("sp", io[f"dsw_s{g}"][s_, 0:WG[g] - 1, :], io[f"cache_dsw{g}"][s_, 1:WG[g], :], (), (Buf(),))
        with contextlib.ExitStack() as sl:
            oT = self.sb("d_oT", [128, 4, NTOK], BF16, sl)
            b_oT = [Buf(f"oT{j}") for j in range(4)]
            with contextlib.ExitStack() as st:
                wq = self.sb("d_wq", [128, 8, 4608], BF16, st)
                b_wq = Buf("wq")
                self.load_w(wq, b_wq, w["dsw_qkv"], D, 4608, mchunk=1536)
                xb = self.sb("d_xb", [128, 8, 512], BF16, st)
                b_xb = Buf("xb")
                stg = [self.sb(f"d_stg{i}", [128, 12, 512], BF16, st) for i in range(2)]
                b_stg = [Buf(), Buf()]
                stf = [self.sb(f"d_stf{i}", [128, D], F32, st) for i in range(2)]
                b_stf = [Buf(), Buf()]
                stv = [self.sb(f"d_stv{i}", [128, 512], BF16, st) for i in range(2)]
                b_stv = [Buf(), Buf()]
                sts = self.sb("d_sts", [NS, 4608], F32, st)
                b_sts = Buf()
                cnt = 0
                for ti, (c0, N) in enumerate(TILES):
                    sample = N == NS
                    sv = src[:, c0:c0 + N].rearrange("(k p) n -> p k n", p=128)
                    self.dma("pool", xb[:, :, :N], sv, (b_src,), (b_xb,))
                    if sample:
                        for c9 in range(9):
                            pp, bpp = self.ps("o", [4, 5])
                            for k in range(8):
                                self.mm(pp[0:NS, :], xb[:, k, 0:NS], wq[:, k, c9 * 512:(c9 + 1) * 512], k == 0, k == 7, (b_wq, b_xb), (bpp,))
                            self.cp("act", sts[:, c9 * 512:(c9 + 1) * 512], pp[0:NS, :], (bpp,), (b_sts,))
                        self.dma("sp", qkvs[:, :], sts[:], (b_sts,), (b_qkvs,))
                        continue
                    for qk in range(2):
                        for g in range(3):
                            for jj in range(4):
                                pp, bpp = self.ps("a", [0, 1, 2, 3])
                                col = g * 1536 + qk * 512 + jj * 128
                                for k in range(8):
                                    self.mm(pp[:, :N], wq[:, k, col:col + 128], xb[:, k, :N], k == 0, k == 7, (b_wq, b_xb), (bpp,))
                                self.cp("act" if jj % 2 else "dve", stg[qk][:, g * 4 + jj, :N], pp[:, :N], (bpp,), (b_stg[qk],))
                        dd, bd = (qT3, b_qT3) if qk == 0 else (kT3, b_kT3)
                        self.dma("sp", dd[:, c0:c0 + N].rearrange("(m p) n -> p m n", p=128), stg[qk][:, :, :N], (b_stg[qk],), (bd,))
                    for blk in range(4):
                        bi = ti * 4 + blk
                        for g in range(3):
                            tail0 = 32 - WG[g] // 128
                            intail = bi >= tail0
                            i2 = cnt % 2
                            cnt += 1
                            for half in ((0, 1) if intail else (1,)):
                                pp, bpp = self.ps("o", [4, 5])
                                off = g * 1536 + 512 + half * 512
                                for k in range(8):
                                    self.mm(pp[:, :], xb[:, k, blk * 128:(blk + 1) * 128], wq[:, k, off:off + 512], k == 0, k == 7, (b_wq, b_xb), (bpp,))
                                if intail:
                                    self.cp("act", stf[i2][:, half * 512:(half + 1) * 512], pp[:, :], (bpp,), (b_stf[i2],))
                                if half == 1:
                                    self.cp("dve", stv[i2][:], pp[:, :], (bpp,), (b_stv[i2],))
                            if intail:
                                r0 = (bi - tail0) * 128
                                self.dma("sp", io[f"dsw_p{g}"][r0:r0 + 128, :], stf[i2][:], (b_stf[i2],), (Buf(),))
                            self.dma("sp", v3[g, bi * 128:(bi + 1) * 128, :], stv[i2][:], (b_stv[i2],), (b_v3,))
            self.P.barrier()
            with contextlib.ExitStack() as st:
              if "dsp" not in SKIP:
                Qj = self.sb("d_Qj", [128, 3, NP], BF16, st)
                Kj = self.sb("d_Kj", [128, 3, NP], BF16, st)
                Va = [self.sb(f"d_Va{i}", [128, 32, 3, 128], BF16, st) for i in range(2)]
                b_Qj, b_Kj, b_Va = Buf(), Buf(), Buf()
                Bm = [self.sb(f"d_Bm{i}", [128, 24 * 128], F32, st) for i in range(2)]
                b_Bm = [Buf(), Buf()]
                NR = 3
                u_t = [self.sb(f"d_u{i}", [128, 512], F32, st) for i in range(NR)]
                p_t = [self.sb(f"d_p{i}", [128, 512], BF16, st) for i in range(NR)]
                b_u = [Buf() for _ in range(NR)]
                b_p = [Buf() for _ in range(NR)]
                acc = [self.sb(f"d_acc{i}", [128, NP], F32, st) for i in range(2)]
                b_acc = [Buf(), Buf()]
                rD = self.sb("d_rD", [128, 512], F32, st)
                b_rD = Buf()
                sel = [self.sb(f"d_sel{i}", [128, 128], F32, st) for i in range(2)]
                b_sel = Buf()
                self.memset("pool", Va[0][:, :, :, 64:128], 0.0, (b_Va,))
                self.memset("pool", Va[0][:, :, :, 64:65], 1.0, (b_Va,))
                self.memset("pool", Va[1][:, :, :, 0:64], 0.0, (b_Va,))
                self.memset("pool", Va[1][:, :, :, 0:1], 1.0, (b_Va,))
                self.memset("pool", sel[0][:], 0.0, (b_sel,))
                self.memset("pool", sel[0][64:65, 0:64], 1.0, (b_sel,))
                self.memset("pool", sel[1][:], 0.0, (b_sel,))
                self.memset("pool", sel[1][0:1, 64:128], 1.0, (b_sel,))
                q3 = qT3.rearrange("(g m) n -> m g n", g=3)
                k3 = kT3.rearrange("(g m) n -> m g n", g=3)
                NJ_G = (2, 5, 17)
                E0_G = (0, 2, 7)
                cn = 0
                for jj in range(4):
                    self.dma("sp", Qj[:], q3[jj * 128:(jj + 1) * 128, :, :], (b_qT3,), (b_Qj,))
                    self.dma("sp", Kj[:], k3[jj * 128:(jj + 1) * 128, :, :], (b_kT3,), (b_Kj,))
                    for g in range(3):
                        vv = v3[g].rearrange("(b p) c -> p b c", p=128)
                        self.dma("sp", Va[0][:, :, g, 0:64], vv[:, :, jj * 128:jj * 128 + 64], (b_v3,), (b_Va,))
                        self.dma("sp", Va[1][:, :, g, 64:128], vv[:, :, jj * 128 + 64:(jj + 1) * 128], (b_v3,), (b_Va,))
                    for hh in range(2):
                        h = jj * 2 + hh
                        self.dma("sp", Bm[hh][:], io["dswB"][h], (), (b_Bm[hh],))
                    for hh in range(2):
                        p0 = 64 * hh
                        self.memset("pool", acc[hh][:], 0.0, (b_acc[hh],))
                        chunks = []
                        for kb in range(32):
                            for g in range(3):
                                js = [j for j in range(NJ_G[g]) if kb + j <= 31]
                                for a in range(0, len(js), 4):
                                    ch = js[a:a + 4]
                                    chunks.append((kb, g, len(ch) * 128, (kb + ch[0]) * 128, E0_G[g] + ch[0], cn))
                                    cn += 1

                        def stA(c):
                            kb, g, nco, q0, e0, ci = c
                            x = ci % NR
                            S, bS = self.banks[ci % 3], self.bbuf[ci % 3]
                            self.mm(S[:, :nco], Kj[p0:p0 + 64, g, kb * 128:(kb + 1) * 128], Qj[p0:p0 + 64, g, q0:q0 + nco],
                                    True, True, (b_Kj, b_Qj), (bS,))
                            self.stt(u_t[x][:, :nco], S[:, :nco], 0.125, Bm[hh][:, e0 * 128:e0 * 128 + nco], ALU.mult, ALU.add,
                                     (bS, b_Bm[hh]), (b_u[x],))
                            self.act(p_t[x][:, :nco], u_t[x][:, :nco], AF.Exp, (b_u[x],), (b_p[x],))

                        def stB(c):
                            kb, g, nco, q0, e0, ci = c
                            x = ci % NR
                            PS, bPS = self.banks[3 + ci % 3], self.bbuf[3 + ci % 3]
                            self.mm(PS[:, :nco], Va[hh][:, kb, g, :], p_t[x][:, :nco], True, True, (b_Va, b_p[x]), (bPS,))
                            self.tt("dve", acc[hh][:, q0:q0 + nco], PS[:, :nco], acc[hh][:, q0:q0 + nco], ALU.add,
                                    (bPS, b_acc[hh]), (b_acc[hh],))

                        SK = 2
                        for i in range(len(chunks) + SK):
                            if i < len(chunks):
                                stA(chunks[i])
                            if i - SK >= 0:
                                stB(chunks[i - SK])
                        for t in range(8):
                            Dn, bDn = self.banks[6 + t % 2], self.bbuf[6 + t % 2]
                            self.mm(Dn[:, :], sel[hh][:, :], acc[hh][:, t * 512:(t + 1) * 512], True, True, (b_sel, b_acc[hh]), (bDn,))
                            self.P.op("dve", lambda hd, Dn=Dn, p0=p0: hd.reciprocal(rD[p0:p0 + 64, :], Dn[p0:p0 + 64, :]), (bDn,), (b_rD,))
                            self.tt("dve", oT[p0:p0 + 64, jj, t * 512:(t + 1) * 512], acc[hh][p0:p0 + 64, t * 512:(t + 1) * 512], rD[p0:p0 + 64, :], ALU.mult,
                                    (b_acc[hh], b_rD), (b_oT[jj],))
            self.P.barrier()
            with contextlib.ExitStack() as st:
              if "dss" not in SKIP:
                self.dsw_sample(st, io, oT, b_oT, qkvs, b_qkvs, WG, DG)
            self.P.barrier()
            with contextlib.ExitStack() as st:
                self.out_proj_ln(st, oT, b_oT, 4, w["dsw_out"], 512, src, b_src, dst, b_dst, 6)
        self.P.barrier()

    def dsw_sample(self, st, io, oT, b_oT, qkvs, b_qkvs, WG, DG):
        cst, cstb, pv = self.cst, self.cstb, self.pv
        qk = self.sb("ds_qkv", [NS, 4608], F32, st)
        qkb = self.sb("ds_qkvb", [NS, 4608], BF16, st)
        b_qk = Buf()
        self.dma("sp", qk[:], qkvs[:, :], (b_qkvs,), (b_qk,))
        self.dma("pool", qkb[:], qkvs[:, :], (b_qkvs,), (b_qk,))
        ab = self.sb("ds_ab", [128, 24], F32, st)
        bd8 = self.sb("ds_bd8", [8, 512], F32, st)
        b_c = Buf()
        self.dma("sp", ab[:], io["dswAB"][:, :], (), (b_c,))
        self.dma("sp", bd8[:], io["bd8"][:, :], (), (b_c,))
        kv = [self.sb(f"ds_kv{i}", [128, D], BF16, st) for i in range(2)]
        b_kv = [Buf(), Buf()]
        kn = [self.sb(f"ds_kn{i}", [1, D], BF16, st) for i in range(2)]
        b_kn = [Buf(), Buf()]
        qb = [self.sb(f"ds_qb{i}", [128, 512], BF16, st) for i in range(2)]
        b_qb = [Buf(), Buf()]
        prod = [self.sb(f"ds_pr{i}", [128, 512], F32, st) for i in range(2)]
        b_prod = [Buf(), Buf()]
        z = self.sb("ds_z", [128, 8], F32, st)
        zn = self.sb("ds_zn", [1, 8], F32, st)
        pw = [self.sb(f"ds_pw{i}", [128, 8], BF16, st) for i in range(2)]
        pn = [self.sb(f"ds_pn{i}", [1, 8], BF16, st) for i in range(2)]
        b_z, b_zn = Buf(), Buf()
        b_pw, b_pn = [Buf(), Buf()], [Buf(), Buf()]
        rden = self.sb("ds_rden", [8, 1], F32, st)
        osb = self.sb("ds_osb", [8, 512], F32, st)
        b_rden, b_osb = Buf(), Buf()
        for g in range(3):
            self.dma("sp", io[f"dsw_s{g}"][:, WG[g] - 1, :], qk[:, g * 1536 + 512:g * 1536 + 1536], (b_qk,), (Buf(),))
        cnt = 0
        for s_ in range(NS):
            num, bnum = self.banks[4], self.bbuf[4]
            den, bden = self.banks[5], self.bbuf[5]
            for g in range(3):
                x = cnt % 2
                cnt += 1
                srcv = io[f"cache_dsw{g}"][s_].rearrange("(r d) c -> r d c", d=DG[g])[:, 0, :]
                self.dma("pool", kv[x][:], srcv, (), (b_kv[x],))
                self.dma("pool", kn[x][:], qkvs[s_:s_ + 1, g * 1536 + 512:g * 1536 + 1536], (b_qkvs,), (b_kn[x],))
                pp, bpp = self.ps("s", [2, 3])
                self.mm(pp[:, :], cstb[0:NS, C_SEL + s_ * 128:C_SEL + (s_ + 1) * 128], qkb[:, g * 1536:g * 1536 + 512], True, True,
                        (b_qk, self.b_cst), (bpp,))
                self.cp("act", qb[x][:], pp[:, :], (bpp,), (b_qb[x],))
                self.tt("dve", prod[x][:], kv[x][:, 0:512], qb[x][:], ALU.mult, (b_kv[x], b_qb[x]), (b_prod[x],))
                self.P.op("dve", lambda h, x=x: h.tensor_reduce(z[:, :], prod[x][:].rearrange("p (h d) -> p h d", d=64), mybir.AxisListType.X, ALU.add),
                          (b_prod[x],), (b_z,))
                self.stt(z[:, :], z[:, :], 0.125, ab[:, g * 8:(g + 1) * 8], ALU.mult, ALU.add, (b_z, b_c), (b_z,))
                self.act(pw[x][:], z[:, :], AF.Exp, (b_z,), (b_pw[x],))
                self.tt("dve", prod[x][0:1, :], kn[x][0:1, 0:512], qb[x][0:1, :], ALU.mult, (b_kn[x], b_qb[x], b_prod[x]), (b_prod[x],))
                self.P.op("dve", lambda h, x=x: h.tensor_reduce(zn[:, :], prod[x][0:1, :].rearrange("p (h d) -> p h d", d=64), mybir.AxisListType.X, ALU.add),
                          (b_prod[x],), (b_zn,))
                self.act(pn[x][:], zn[:, :], AF.Exp, (b_zn,), (b_pn[x],), scale=0.125)
                self.mm(num[0:8, :], pw[x][:], kv[x][:, 512:1024], g == 0, False, (b_pw[x], b_kv[x]), (bnum,))
                self.mm(num[0:8, :], pn[x][0:1, :], kn[x][0:1, 512:1024], False, g == 2, (b_pn[x], b_kn[x]), (bnum,))
                self.mm(den[0:8, 0:1], pw[x][:], cstb[:, C_NONES:C_NONES + 1], g == 0, False, (b_pw[x], self.b_cst), (bden,))
                self.mm(den[0:8, 0:1], pn[x][0:1, :], cstb[0:1, C_NONES:C_NONES + 1], False, g == 2, (b_pn[x], self.b_cst), (bden,))
            self.P.op("dve", lambda h: h.reciprocal(rden[:, :], den[0:8, 0:1]), (bden,), (b_rden,))
            self.ts("dve", osb[:, :], num[0:8, :], rden[:, 0:1], -1.0, ALU.mult, ALU.mult, (bnum, b_rden), (b_osb,))
            self.tt("dve", osb[:, :], osb[:, :], bd8[:, :], ALU.mult, (b_osb, b_c), (b_osb,))
            pc, bpc = self.ps("s", [2, 3])
            for jj in range(4):
                self.mm(pc[:, jj:jj + 1], osb[0:8, jj * 128:(jj + 1) * 128], cst[0:8, C_ONES:C_ONES + 1], True, True, (b_osb, self.b_cst), (bpc,))
            self.cp("dve", oT[:, :, NP + s_], pc[:, 0:4], (bpc,), tuple(b_oT))

    def ffn(self, layer, src, b_src, dst, b_dst, w, ffn_state, ffn_out):
        with contextlib.ExitStack() as st:
            wup = self.sb("f_wup", [128, 8, 2 * DFF], BF16, st)
            wdn = self.sb("f_wdn", [128, NJ, D], BF16, st)
            b_wup, b_wdn = Buf("wup"), Buf("wdn")
            self.load_w(wup, b_wup, w["ffn_up"][layer], D, 2 * DFF)
            self.load_w(wdn, b_wdn, w["ffn_down"][layer], DFF, D)
            xt = self.sb("f_xt", [128, 8, 512], F32, st)
            xb = self.sb("f_xb", [128, 8, 512], BF16, st)
            b_xt, b_xb = Buf("xt"), Buf("xb")
            hb = self.sb("f_h", [128, NJ, 512], BF16, st)
            b_hb = Buf("h")
            gs = [self.sb(f"f_g{i}", [128, 516], F32, st) for i in range(2)]
            b_gs = [Buf(), Buf()]
            zs = [self.sb(f"f_z{i}", [128, 512], F32, st) for i in range(2)]
            b_zs = [Buf(), Buf()]
            halo = self.sb("f_halo", [128, NJ, 2], F32, st)
            b_halo = Buf("halo")
            gsm = self.sb("f_gsm", [128, NJ, 3, NS], F32, st)
            b_gsm = Buf("gsm")
            fo = self.sb("f_o", [128, NJ, 2 + 2 * NS], F32, st)
            b_fo = Buf()
            tmp = self.ln_tmp(st)
            ident = self.cst[:, C_IDENT:C_IDENT + 128]
            self.load_halo(st, ffn_state[layer], NJ, gsm, b_gsm)
            self.memset("dve", halo[:], 0.0, (b_halo,))
            wc0 = PV_FFW + layer * NJ * 3

            def up_part(N, sample, j0, j1):
                for j in range(j0, j1):
                    pg, bpg = self.ps("a", [0, 1, 2, 3])
                    pu, bpu = self.ps("a", [0, 1, 2, 3])
                    for (pp, bpp, off) in ((pg, bpg, 0), (pu, bpu, DFF)):
                        for k in range(8):
                            self.mm(pp[:, :N], wup[:, k, off + j * 128: off + (j + 1) * 128], xb[:, k, :N], k == 0, k == 7,
                                    (b_wup, b_xb), (bpp,))
                    i = j % 2
                    if sample:
                        self.cp("act", gsm[:, j, 2, :], pg[:, :N], (bpg,), (b_gsm,))
                        taps = [gsm[:, j, 0, :], gsm[:, j, 1, :], gsm[:, j, 2, :]]
                        rd = (b_gsm, self.b_pv)
                    else:
                        g = gs[i]
                        self.cp("pool", g[:, 0:2], halo[:, j, :], (b_halo,), (b_gs[i],))
                        self.cp("act", g[:, 2:2 + N], pg[:, :N], (bpg,), (b_gs[i],))
                        self.cp("pool", halo[:, j, :], g[:, N:N + 2], (b_gs[i],), (b_halo,))
                        taps = [g[:, 0:N], g[:, 1:1 + N], g[:, 2:2 + N]]
                        rd = (b_gs[i], self.b_pv)
                    self.conv3(zs[i][:, :N], taps, lambda t, j=j: self.pv[:, wc0 + j * 3 + t: wc0 + j * 3 + t + 1], rd, (b_zs[i],))
                    self.act(zs[i][:, :N], zs[i][:, :N], AF.Silu, (b_zs[i],), (b_zs[i],))
                    self.tt("dve", hb[:, j, :N], zs[i][:, :N], pu[:, :N], ALU.mult, (b_zs[i], bpu), (b_hb,))

            def tile_src(ti):
                c0, N = TILES[ti]
                return src[:, c0:c0 + N].rearrange("(k p) n -> p k n", p=128), c0, N

            sv0, _, N0 = tile_src(0)
            self.dma("pool", xb[:, :, :N0], sv0, (b_src,), (b_xb,))
            prev = None
            for ti in range(len(TILES)):
                sv, c0, N = tile_src(ti)
                sample = N == NS
                up_part(N, sample, 0, 8)
                if prev is not None:
                    self.post_ln(st, xt, b_xt, prev[1], layer * 2 + 1, dst, b_dst, prev[0], tmp)
                self.dma("sp", xt[:, :, :N], sv, (b_src,), (b_xt,))
                up_part(N, sample, 8, NJ)
                if ti + 1 < len(TILES):
                    svn, _, Nn = tile_src(ti + 1)
                    self.dma("pool", xb[:, :, :Nn], svn, (b_src,), (b_xb,))
                for m in range(8):
                    po, bpo = self.ps("o", [4, 5])
                    for k in range(NJ):
                        self.mm(po[:, :N], wdn[:, k, m * 128:(m + 1) * 128], hb[:, k, :N], k == 0, k == NJ - 1, (b_wdn, b_hb), (bpo,))
                    self.stt(xt[:, m, :N], xt[:, m, :N], DN_ALPHA, po[:, :N], ALU.mult, ALU.add, (b_xt, bpo), (b_xt,))
                prev = (c0, N)
            self.post_ln(st, xt, b_xt, prev[1], layer * 2 + 1, dst, b_dst, prev[0], tmp)
            self.cp("pool", fo[:, :, 0:2], halo[:], (b_halo,), (b_fo,))
            self.cp("pool", fo[:, :, 2:2 + 2 * NS].rearrange("p k (r s) -> p k r s", r=2), gsm[:, :, 1:3, :], (b_gsm,), (b_fo,))
            self.dma("sp", ffn_out[layer].rearrange("(k p) c -> p k c", p=128), fo[:], (b_fo,), (Buf(),))


PV_LNG = 0
PV_LNB = 64
PV_SCW = 128
PV_FFW = 152
PV_EPS = 416
PV_BSB = 420
PV_BG = 436
PV_GN = 440
NPV = 444

C_IDENT = 0
C_MASK_S = 128
C_ONES = 256
C_PCOL = 257
NCF = 260
C_ONES_S = 0
C_NTRI = 128
C_NONES = 256
C_ZERO = 384
C_SEL = 512
NCB = 1024


def make_pv(inputs):
    pv = np.zeros((128, NPV), np.float32)
    pv[:, PV_LNG:PV_LNG + 64] = inputs["ln_g"].reshape(8, 8, 128).transpose(2, 0, 1).reshape(128, 64)
    pv[:, PV_LNB:PV_LNB + 64] = inputs["ln_b"].reshape(8, 8, 128).transpose(2, 0, 1).reshape(128, 64)
    pv[:, PV_SCW:PV_SCW + 24] = inputs["w_sc_conv"].reshape(3, 8, 128).transpose(2, 1, 0).reshape(128, 24)
    pv[:, PV_FFW:PV_FFW + 264] = inputs["w_ffn_conv"].reshape(4, 3, NJ, 128).transpose(3, 0, 2, 1).reshape(128, 264)
    pv[:, PV_EPS] = LN_EPS
    pv[:, PV_BSB:PV_BSB + 16] = inputs["b_sb"][None, :]
    pv[:, PV_BG:PV_BG + 4] = inputs["b_gla_gate"].reshape(4, 128).T
    pv[:, PV_GN:PV_GN + 2] = inputs["g_gla_norm"].reshape(2, 128).T
    return pv


def make_cst():
    p = np.arange(128)
    cf = np.zeros((128, NCF), np.float32)
    cf[:, C_IDENT:C_IDENT + 128] = np.eye(128, dtype=np.float32)
    cf[:, C_MASK_S:C_MASK_S + 128] = (p[None, :] > p[:, None])
    cf[:, C_ONES] = 1.0
    cf[:, C_PCOL] = p
    cb = np.zeros((128, NCB), np.float32)
    cb[:, C_ONES_S:C_ONES_S + 128] = 1.0 / 1024
    cb[:, C_NTRI:C_NTRI + 128] = -1.0 * (p[:, None] >= p[None, :])
    cb[:, C_NONES:C_NONES + 128] = -1.0
    for q in range(NS):
        cb[q, C_SEL + q * 128:C_SEL + (q + 1) * 128] = 1.0
    bd = np.zeros((16, 1024), np.float32)
    for h in range(16):
        bd[h, h * 64:(h + 1) * 64] = 1.0
    return cf, cb, bd


def make_dsw_tables():
    WG = (128, 512, 2048)
    DG = (1, 4, 16)
    ENT = [(0, 0), (0, 1)] + [(1, j) for j in range(5)] + [(2, j) for j in range(17)]
    slopes = (2.0 ** (-8.0 * np.arange(1, 25, dtype=np.float32) / 24)).astype(np.float32).reshape(3, 8)
    p = np.arange(128)[:, None]
    c = np.arange(128)[None, :]
    B = np.full((8, 128, 24 * 128), -30000.0, np.float32)
    for ei, (g, j) in enumerate(ENT):
        dist = 128 * j + c - p
        valid = (dist >= 0) & (dist <= WG[g]) & (dist % DG[g] == 0)
        for h in range(8):
            blk = np.where(valid, -slopes[g, h] * dist.astype(np.float32), np.float32(-30000.0)).astype(np.float32)
            B[h, :, ei * 128:(ei + 1) * 128] = blk
    ab = np.zeros((128, 24), np.float32)
    for g in range(3):
        for h in range(8):
            ab[:, g * 8 + h] = -slopes[g, h] * DG[g] * (128 - np.arange(128)).astype(np.float32)
    bd8 = np.zeros((8, 512), np.float32)
    for h in range(8):
        bd8[h, h * 64:(h + 1) * 64] = 1.0
    return B, ab, bd8


_CACHE = {}


def get_nc(layers=(0, 1, 2, 3), dbg=False, n_pool=2560):
    key = (tuple(layers), dbg, n_pool)
    if key not in _CACHE:
        k = K(layers=layers, dbg=dbg, n_pool=n_pool)
        _CACHE[key] = (k.build(), k)
    return _CACHE[key]


def make_in_maps(inputs, layers):
    pv = make_pv(inputs)
    cf, cb, bd = make_cst()
    bsbr = np.ascontiguousarray(np.broadcast_to(np.tile(inputs["b_sb"], 64)[None, :], (128, 1024))).astype(np.float32)
    maps = []
    for c in range(N_CORES):
        s0 = c * NS
        if c in PROMPT_CORES:
            xp = inputs["x_prompt"][PROMPT_CORES.index(c)].T
        else:
            xp = np.zeros((D, NP), np.float32)
        xT0 = np.ascontiguousarray(np.concatenate([xp, inputs["x_sample"][s0:s0 + NS, 0].T], axis=1))
        m = {
            "xT0": xT0, "pv": pv, "cstf": cf, "cstb": cb, "bd16": bd, "bsbr": bsbr,
            "w_sc_in": inputs["w_sc_in"], "w_sc_out": inputs["w_sc_out"],
            "w_ffn_up": inputs["w_ffn_up"], "w_ffn_down": inputs["w_ffn_down"],
            "sc_cache": np.ascontiguousarray(inputs["cache_sc_conv"][s0:s0 + NS].reshape(NS * 2, D)),
            "ffn_state": np.ascontiguousarray(inputs["state_ffn_conv"][:, s0:s0 + NS].reshape(4, NS * 2, DFF)),
        }
        if 1 in layers:
            m["w_sb_qkv"] = inputs["w_sb_qkv"]
            m["w_sb_out"] = inputs["w_sb_out"]
            m["cache_sb_k"] = inputs["cache_sb_k"]
            m["cache_sb_v"] = inputs["cache_sb_v"]
            m["page_table"] = np.ascontiguousarray(inputs["page_table"][s0:s0 + NS])
        if 3 in layers:
            if "dswB" not in _CACHE:
                _CACHE["dswB"] = make_dsw_tables()
            B, ab, bd8 = _CACHE["dswB"]
            m["dswB"], m["dswAB"], m["bd8"] = B, ab, bd8
            m["w_dsw_qkv"] = inputs["w_dsw_qkv"]
            m["w_dsw_out"] = inputs["w_dsw_out"]
            for g in range(3):
                cg = inputs[f"cache_dsw_kv{g}"][s0:s0 + NS]
                m[f"cache_dsw{g}"] = np.ascontiguousarray(cg.reshape(NS, cg.shape[1], D))
        if 2 in layers:
            m["w_gla_in"] = inputs["w_gla_in"]
            m["w_gla_gate_up"] = inputs["w_gla_gate_up"]
            m["w_gla_out"] = inputs["w_gla_out"]
            m["state_gla"] = np.ascontiguousarray(inputs["state_gla"][s0:s0 + NS])
        maps.append(m)
    return maps


def run(inputs, layers=(0, 1, 2, 3), dbg=False, n_pool=2560, override=None):
    nc, k = get_nc(layers, dbg, n_pool)
    maps = make_in_maps(inputs, layers)
    if override:
        for m in maps:
            m.update(override)
    maps = [{n: m[n] for n in k.din} for m in maps]
    res = run_bass_kernel_spmd(nc, maps, core_ids=list(range(N_CORES)))
    return res.results


def kernel(**inputs):
    inputs = {k: np.asarray(v) for k, v in inputs.items()}
    res = run(inputs)
    B = 4
    f32 = np.float32
    y_p = np.stack([res[b]["yT"][:, :NP].T for b in PROMPT_CORES]).astype(f32)
    y_s = np.concatenate([res[c]["yT"][:, NP:].T for c in range(N_CORES)])[:, None, :].astype(f32)
    sc_p = np.stack([res[b]["sc_out"][:, 0:2].T for b in PROMPT_CORES]).astype(f32)
    sc_s = np.concatenate([res[c]["sc_out"][:, 2:].reshape(D, 2, NS).transpose(2, 1, 0) for c in range(N_CORES)]).astype(f32)
    sbk_p = np.stack([res[b]["sb_k_p"] for b in PROMPT_CORES]).reshape(B, NP, 16, 64)
    sbv_p = np.stack([res[b]["sb_v_p"] for b in PROMPT_CORES]).reshape(B, NP, 16, 64)
    sbk_s = np.concatenate([res[c]["sb_k_s"] for c in range(N_CORES)]).reshape(32, 1, 16, 64)
    sbv_s = np.concatenate([res[c]["sb_v_s"] for c in range(N_CORES)]).reshape(32, 1, 16, 64)
    gla_p = np.stack([res[b]["gla_p"] for b in PROMPT_CORES])
    gla_s = np.concatenate([res[c]["gla_s"] for c in range(N_CORES)])
    dsw_p = [np.stack([res[b][f"dsw_p{g}"] for b in PROMPT_CORES]).reshape(B, wg, 2, 8, 64) for g, wg in enumerate((128, 512, 2048))]
    dsw_s = [np.concatenate([res[c][f"dsw_s{g}"] for c in range(N_CORES)]).reshape(32, wg, 2, 8, 64) for g, wg in enumerate((128, 512, 2048))]
    ffn_p = np.stack([np.stack([res[b]["ffn_out"][l][:, 0:2].T for b in PROMPT_CORES]) for l in range(4)]).astype(f32)
    ffn_s = np.stack([np.concatenate([res[c]["ffn_out"][l][:, 2:].reshape(DFF, 2, NS).transpose(2, 1, 0) for c in range(N_CORES)])
                      for l in range(4)]).astype(f32)
    outs = (y_p, y_s, sc_p, sc_s, sbk_p, sbv_p, sbk_s, sbv_s, gla_p, gla_s,
            dsw_p[0], dsw_p[1], dsw_p[2], dsw_s[0], dsw_s[1], dsw_s[2], ffn_p, ffn_s)
    return tuple(np.ascontiguousarray(o, dtype=f32) for o in outs)
```
